# Optimizing a Trainium2 kernel written in Bass

```python
import math
import jax, jax.numpy as jnp
from jax import lax
import numpy as np

D_MODEL = 2048
BATCH = 4
SEQ = 8192
DEPTH = 2

GRID_W = 64
CTX_LEN = 256

GDN_HEADS = 8
GDN_DK = 128
GDN_DV = 128
GDN_CONV = 5
GDN_CHUNK_LOG2 = 6
GDN_CHUNK = 2 ** GDN_CHUNK_LOG2
GDN_QK_W = GDN_HEADS * GDN_DK
GDN_V_W = GDN_HEADS * GDN_DV
GDN_QKV_W = 2 * GDN_QK_W + GDN_V_W

SWA_Q_HEADS = 16
SWA_KV_HEADS = 4
SWA_HEAD_DIM = 64
SWA_WINDOW = 128
SWA_BLOCK = 128
ROPE_THETA = 10000.0
SWA_Q_W = SWA_Q_HEADS * SWA_HEAD_DIM
SWA_KV_W = SWA_KV_HEADS * SWA_HEAD_DIM

SGU_GROUPS = 8
SGU_CHUNK = 128
SGU_W = D_MODEL // 2

D_FF = 4 * D_MODEL
N_MOD = 6
DEEPNORM_ALPHA = (2 * DEPTH) ** 0.25
DEEPNORM_BETA = (8 * DEPTH) ** -0.25
LN_EPS = 1e-5
RMS_EPS = 1e-6

IN_SPLITS = (GDN_QKV_W, GDN_V_W, GDN_HEADS, GDN_HEADS, GDN_HEADS, GDN_HEADS,
             SWA_Q_W, SWA_KV_W, SWA_KV_W, SGU_W, SGU_W, D_MODEL, D_MODEL, D_MODEL)
IN_SPLIT_IDX = tuple(int(i) for i in np.cumsum(IN_SPLITS)[:-1])
D_IN = sum(IN_SPLITS)

kernel_name = 'hybrid_gdn_swa_sgu_diffusion_block'


def layer_norm(x, g, b):
    xf = x.astype(jnp.float32)
    mu = jnp.mean(xf, axis=-1, keepdims=True)
    var = jnp.mean(jnp.square(xf - mu), axis=-1, keepdims=True)
    return ((xf - mu) * lax.rsqrt(var + LN_EPS)).astype(x.dtype) * g + b


def l2norm(x):
    return x * lax.rsqrt(jnp.sum(jnp.square(x), axis=-1, keepdims=True) + RMS_EPS)


def modulate(x, shift, scale):
    return x * (1.0 + scale) + shift


def depthwise_conv(x, w):
    k_width, n_ch = w.shape
    return lax.conv_general_dilated(x, w[:, None, :], window_strides=(1,),
                                    padding=((k_width // 2, k_width // 2),),
                                    dimension_numbers=('NWC', 'WIO', 'NWC'),
                                    feature_group_count=n_ch)


def axial_rope(t_len):
    rows = t_len // GRID_W
    row = jnp.broadcast_to(jnp.arange(rows)[:, None], (rows, GRID_W)).reshape(t_len).astype(jnp.float32)
    col = jnp.broadcast_to(jnp.arange(GRID_W)[None, :], (rows, GRID_W)).reshape(t_len).astype(jnp.float32)
    n_freq = SWA_HEAD_DIM // 4
    freq = jnp.power(ROPE_THETA, -jnp.arange(n_freq, dtype=jnp.float32) / n_freq)
    ang = jnp.concatenate([row[:, None] * freq, col[:, None] * freq], axis=-1)[:, None, :]
    return jnp.cos(ang), jnp.sin(ang)


def apply_rope(x, cos, sin):
    x1, x2 = jnp.split(x.astype(jnp.float32), 2, axis=-1)
    return jnp.concatenate([x1 * cos - x2 * sin, x2 * cos + x1 * sin], axis=-1).astype(x.dtype)


def gdn_features(qkv, beta_f, beta_b, a_f, a_b, conv_w, a_log, dt_bias):
    bsz, t_len, _ = qkv.shape
    qkv = jax.nn.silu(depthwise_conv(qkv, conv_w)).astype(jnp.float32)
    q, k, v = jnp.split(qkv, (GDN_QK_W, 2 * GDN_QK_W), axis=-1)
    q = l2norm(q.reshape(bsz, t_len, GDN_HEADS, GDN_DK)) * (GDN_DK ** -0.5)
    k = l2norm(k.reshape(bsz, t_len, GDN_HEADS, GDN_DK))
    v = v.reshape(bsz, t_len, GDN_HEADS, GDN_DV)
    dirs = []
    for d, (b_in, a_in) in enumerate(((beta_f, a_f), (beta_b, a_b))):
        g = -jnp.exp(a_log[d].astype(jnp.float32)) * jax.nn.softplus((a_in + dt_bias[d]).astype(jnp.float32))
        dirs.append((g, jax.nn.sigmoid(b_in.astype(jnp.float32))))
    return q, k, v, dirs


def gdn_chunked(q, k, v, g, beta, state):
    bsz, t_len, n_h, _ = q.shape
    c_len = GDN_CHUNK
    n_chunk = t_len // c_len

    def chunks(t):
        return jnp.moveaxis(t.reshape((bsz, n_chunk, c_len, n_h) + t.shape[3:]), 3, 1)

    q, k, v, g, beta = chunks(q), chunks(k), chunks(v), chunks(g), chunks(beta)
    cum_g = jnp.cumsum(g, axis=-1)
    incl = jnp.tril(jnp.ones((c_len, c_len), dtype=bool))
    strict = jnp.tril(jnp.ones((c_len, c_len), dtype=bool), -1)
    diff = cum_g[..., :, None] - cum_g[..., None, :]
    decay_mat = jnp.where(incl, jnp.exp(jnp.where(incl, diff, 0.0)), 0.0)
    k_beta = k * beta[..., None]
    neg_m = -jnp.where(strict, jnp.einsum('bhnid,bhnjd->bhnij', k_beta, k) * decay_mat, 0.0)
    inv = jnp.eye(c_len, dtype=q.dtype) + neg_m
    power = neg_m
    for _ in range(GDN_CHUNK_LOG2 - 1):
        power = power @ power
        inv = inv + inv @ power
    u = inv @ (v * beta[..., None])
    w = inv @ (k_beta * jnp.exp(cum_g)[..., None])
    a_qk = jnp.where(incl, jnp.einsum('bhnid,bhnjd->bhnij', q, k) * decay_mat, 0.0)
    q_dec = q * jnp.exp(cum_g)[..., None]
    g_last = cum_g[..., -1:]
    k_dec = k * jnp.exp(g_last - cum_g)[..., None]
    state_dec = jnp.exp(g_last[..., 0])

    def step(s, xs):
        u_n, w_n, a_n, q_n, k_n, d_n = xs
        v_new = u_n - jnp.einsum('bhck,bhkv->bhcv', w_n, s)
        o_n = jnp.einsum('bhck,bhkv->bhcv', q_n, s) + jnp.einsum('bhcs,bhsv->bhcv', a_n, v_new)
        s = s * d_n[..., None, None] + jnp.einsum('bhck,bhcv->bhkv', k_n, v_new)
        return s, o_n

    xs = tuple(jnp.moveaxis(t, 2, 0) for t in (u, w, a_qk, q_dec, k_dec, state_dec))
    state, o = lax.scan(step, state, xs)
    return jnp.transpose(o, (1, 0, 3, 2, 4)).reshape(bsz, t_len, n_h, GDN_DV), state


def gdn_bidirectional(ctx_feats, lat_feats):
    q_c, k_c, v_c, dirs_c = ctx_feats
    q_l, k_l, v_l, dirs_l = lat_feats
    bsz = q_c.shape[0]
    o_c = 0.0
    o_l = 0.0
    for d in range(2):
        rev = (lambda t: jnp.flip(t, axis=1)) if d == 1 else (lambda t: t)
        g_c, b_c = dirs_c[d]
        g_l, b_l = dirs_l[d]
        s0 = jnp.zeros((bsz, GDN_HEADS, GDN_DK, GDN_DV), jnp.float32)
        oc, s_ctx = gdn_chunked(rev(q_c), rev(k_c), rev(v_c), rev(g_c), rev(b_c), s0)
        ol, _ = gdn_chunked(rev(q_l), rev(k_l), rev(v_l), rev(g_l), rev(b_l), s_ctx)
        o_c = o_c + rev(oc)
        o_l = o_l + rev(ol)
    return o_l, o_c


def gated_rmsnorm(o, z, w):
    bsz, t_len = z.shape[:2]
    on = o * lax.rsqrt(jnp.mean(jnp.square(o), axis=-1, keepdims=True) + RMS_EPS) * w.astype(jnp.float32)
    zf = jax.nn.silu(z.reshape(bsz, t_len, GDN_HEADS, GDN_DV).astype(jnp.float32))
    return (on * zf).reshape(bsz, t_len, GDN_V_W).astype(z.dtype)


def swa_latent(q, k, v, k_ctx, v_ctx, sinks):
    bsz, t_len, n_q, hd = q.shape
    n_blk = t_len // SWA_BLOCK
    n_g = SWA_KV_HEADS
    n_r = n_q // n_g
    scale = hd ** -0.5
    qb = q.reshape(bsz, n_blk, SWA_BLOCK, n_g, n_r, hd)

    def band(t):
        tp = jnp.pad(t, ((0, 0), (SWA_BLOCK, SWA_BLOCK), (0, 0), (0, 0))).reshape(bsz, n_blk + 2, SWA_BLOCK, n_g, hd)
        return jnp.concatenate([tp[:, :-2], tp[:, 1:-1], tp[:, 2:]], axis=2)

    kb, vb = band(k), band(v)
    s_win = jnp.einsum('bnqgrd,bnkgd->bgrnqk', qb, kb, preferred_element_type=jnp.float32) * scale
    key_off = jnp.arange(3 * SWA_BLOCK) - SWA_BLOCK
    rel = key_off[None, :] - jnp.arange(SWA_BLOCK)[:, None]
    key_pos = jnp.arange(n_blk)[:, None, None] * SWA_BLOCK + key_off[None, None, :]
    mask = (jnp.abs(rel) <= SWA_WINDOW)[None] & (key_pos >= 0) & (key_pos < t_len)
    s_win = jnp.where(mask, s_win, -jnp.inf)
    s_ctx = jnp.einsum('bnqgrd,bcgd->bgrnqc', qb, k_ctx, preferred_element_type=jnp.float32) * scale
    sink = sinks.astype(jnp.float32).reshape(n_g, n_r)[None, :, :, None, None, None]
    m = jnp.maximum(jnp.maximum(jnp.max(s_win, axis=-1, keepdims=True), jnp.max(s_ctx, axis=-1, keepdims=True)), sink)
    p_win = jnp.exp(s_win - m)
    p_ctx = jnp.exp(s_ctx - m)
    inv = 1.0 / (jnp.sum(p_win, axis=-1, keepdims=True) + jnp.sum(p_ctx, axis=-1, keepdims=True) + jnp.exp(sink - m))
    o = (jnp.einsum('bgrnqk,bnkgd->bnqgrd', p_win.astype(v.dtype), vb)
         + jnp.einsum('bgrnqc,bcgd->bnqgrd', p_ctx.astype(v.dtype), v_ctx))
    o = o * jnp.transpose(inv, (0, 3, 4, 1, 2, 5))
    return o.reshape(bsz, t_len, n_q * hd).astype(v.dtype)


def swa_context(q, k, v, sinks):
    bsz, c_len, n_q, hd = q.shape
    n_g = SWA_KV_HEADS
    n_r = n_q // n_g
    qg = q.reshape(bsz, c_len, n_g, n_r, hd)
    s = jnp.einsum('bqgrd,bkgd->bgrqk', qg, k, preferred_element_type=jnp.float32) * (hd ** -0.5)
    sink = jnp.broadcast_to(sinks.astype(jnp.float32).reshape(n_g, n_r)[None, :, :, None, None], (bsz, n_g, n_r, c_len, 1))
    p = jax.nn.softmax(jnp.concatenate([s, sink], axis=-1), axis=-1)[..., :-1]
    o = jnp.einsum('bgrqk,bkgd->bqgrd', p.astype(v.dtype), v)
    return o.reshape(bsz, c_len, n_q * hd)


def spatial_gating(u, v, ln_g, ln_b, w_s, b_s):
    bsz, t_len, width = u.shape
    n_chunk = t_len // SGU_CHUNK
    u = jax.nn.gelu(u)
    v = layer_norm(jax.nn.gelu(v), ln_g, ln_b)
    vb = v.reshape(bsz, n_chunk, SGU_CHUNK, SGU_GROUPS, width // SGU_GROUPS)
    s = jnp.einsum('gpq,bnqgc->bnpgc', w_s, vb) + b_s.T[:, :, None]
    return u * s.reshape(bsz, t_len, width)


def merge_branches(gate_a, gate_b, gate_c, y_a, y_b, y_c, w_a, w_b, w_c, w_out):
    y = (jax.nn.sigmoid(gate_a) * (y_a @ w_a) + jax.nn.sigmoid(gate_b) * (y_b @ w_b)
         + jax.nn.sigmoid(gate_c) * (y_c @ w_c))
    return y @ w_out


def token_mixer(h, hc, cos, sin, w_in, b_in, conv_w, a_log, dt_bias, gdn_norm_w, sinks,
                sgu_ln_g, sgu_ln_b, sgu_w, sgu_b, w_a, w_b, w_c, w_out, need_ctx):
    p_l = jnp.split(h @ w_in + b_in, IN_SPLIT_IDX, axis=-1)
    p_c = jnp.split(hc @ w_in + b_in, IN_SPLIT_IDX, axis=-1)

    feats_l = gdn_features(p_l[0], p_l[2], p_l[3], p_l[4], p_l[5], conv_w, a_log, dt_bias)
    feats_c = gdn_features(p_c[0], p_c[2], p_c[3], p_c[4], p_c[5], conv_w, a_log, dt_bias)
    o_a_l, o_a_c = gdn_bidirectional(feats_c, feats_l)
    y_a_l = gated_rmsnorm(o_a_l, p_l[1], gdn_norm_w)

    def heads(t, n):
        return t.reshape(t.shape[0], t.shape[1], n, SWA_HEAD_DIM)

    q_b_l = apply_rope(heads(p_l[6], SWA_Q_HEADS), cos, sin)
    k_b_l = apply_rope(heads(p_l[7], SWA_KV_HEADS), cos, sin)
    v_b_l = heads(p_l[8], SWA_KV_HEADS)
    k_b_c = heads(p_c[7], SWA_KV_HEADS)
    v_b_c = heads(p_c[8], SWA_KV_HEADS)
    y_b_l = swa_latent(q_b_l, k_b_l, v_b_l, k_b_c, v_b_c, sinks)

    y_c_l = spatial_gating(p_l[9], p_l[10], sgu_ln_g, sgu_ln_b, sgu_w, sgu_b)

    y_lat = merge_branches(p_l[11], p_l[12], p_l[13], y_a_l, y_b_l, y_c_l, w_a, w_b, w_c, w_out)
    if not need_ctx:
        return y_lat, None
    y_a_c = gated_rmsnorm(o_a_c, p_c[1], gdn_norm_w)
    y_b_c = swa_context(heads(p_c[6], SWA_Q_HEADS), k_b_c, v_b_c, sinks)
    y_c_c = spatial_gating(p_c[9], p_c[10], sgu_ln_g, sgu_ln_b, sgu_w, sgu_b)
    y_ctx = merge_branches(p_c[11], p_c[12], p_c[13], y_a_c, y_b_c, y_c_c, w_a, w_b, w_c, w_out)
    return y_lat, y_ctx


def squared_relu_mlp(h, w1, w2):
    return jnp.square(jax.nn.relu(h @ w1)) @ w2


def setup_inputs(seed: int = 0) -> dict:
    key = jax.random.key(seed)
    keys = iter(jax.random.split(key, 40))

    def normal(shape, scale):
        return jax.random.normal(next(keys), shape, jnp.float32) * scale

    def gain(shape):
        return 1.0 + normal(shape, 0.02)

    nl = DEPTH
    dt = jnp.exp(jax.random.uniform(next(keys), (nl, 2, GDN_HEADS), jnp.float32, math.log(1e-3), math.log(1e-1)))
    a_log = jnp.log(jax.random.uniform(next(keys), (nl, 2, GDN_HEADS), jnp.float32, 1.0, 16.0))
    return {
        'x': normal((BATCH, SEQ, D_MODEL), 1.0),
        'c': normal((BATCH, D_MODEL), 1.0),
        'ctx': normal((BATCH, CTX_LEN, D_MODEL), 1.0),
        'c_ctx': normal((D_MODEL,), 1.0),
        'w_ada': normal((nl, D_MODEL, N_MOD * D_MODEL), 0.5 * D_MODEL ** -0.5),
        'b_ada': normal((nl, N_MOD * D_MODEL), 0.02),
        'w_in': normal((nl, D_MODEL, D_IN), D_MODEL ** -0.5),
        'b_in': normal((nl, D_IN), 0.02),
        'gdn_conv': normal((nl, GDN_CONV, GDN_QKV_W), GDN_CONV ** -0.5),
        'gdn_a_log': a_log,
        'gdn_dt_bias': dt + jnp.log(-jnp.expm1(-dt)),
        'gdn_norm': gain((nl, GDN_DV)),
        'swa_sinks': normal((nl, SWA_Q_HEADS), 0.5),
        'sgu_ln_g': gain((nl, SGU_W)),
        'sgu_ln_b': normal((nl, SGU_W), 0.02),
        'sgu_w': normal((nl, SGU_GROUPS, SGU_CHUNK, SGU_CHUNK), SGU_CHUNK ** -0.5),
        'sgu_b': gain((nl, SGU_GROUPS, SGU_CHUNK)),
        'w_branch_a': normal((nl, GDN_V_W, D_MODEL), DEEPNORM_BETA * GDN_V_W ** -0.5),
        'w_branch_b': normal((nl, SWA_Q_W, D_MODEL), DEEPNORM_BETA * SWA_Q_W ** -0.5),
        'w_branch_c': normal((nl, SGU_W, D_MODEL), DEEPNORM_BETA * SGU_W ** -0.5),
        'w_out': normal((nl, D_MODEL, D_MODEL), DEEPNORM_BETA * D_MODEL ** -0.5),
        'ln_mix_g': gain((nl, D_MODEL)),
        'ln_mix_b': normal((nl, D_MODEL), 0.02),
        'w_ff1': normal((nl, D_MODEL, D_FF), D_MODEL ** -0.5),
        'w_ff2': normal((nl, D_FF, D_MODEL), DEEPNORM_BETA * D_FF ** -0.5),
        'ln_ff_g': gain((nl, D_MODEL)),
        'ln_ff_b': normal((nl, D_MODEL), 0.02),
    }


def reference(x, c, ctx, c_ctx, w_ada, b_ada, w_in, b_in, gdn_conv, gdn_a_log, gdn_dt_bias, gdn_norm,
              swa_sinks, sgu_ln_g, sgu_ln_b, sgu_w, sgu_b, w_branch_a, w_branch_b, w_branch_c, w_out,
              ln_mix_g, ln_mix_b, w_ff1, w_ff2, ln_ff_g, ln_ff_b):
    t_len = x.shape[1]
    cos, sin = axial_rope(t_len)
    c_act = jax.nn.silu(c)
    cc_act = jax.nn.silu(c_ctx)
    xc = ctx
    for l in range(DEPTH):
        need_ctx = l < DEPTH - 1
        mod = jnp.split((c_act @ w_ada[l] + b_ada[l])[:, None, :], N_MOD, axis=-1)
        mod_c = jnp.split(cc_act @ w_ada[l] + b_ada[l], N_MOD, axis=-1)
        y, y_c = token_mixer(modulate(x, mod[0], mod[1]), modulate(xc, mod_c[0], mod_c[1]), cos, sin,
                             w_in[l], b_in[l], gdn_conv[l], gdn_a_log[l], gdn_dt_bias[l], gdn_norm[l],
                             swa_sinks[l], sgu_ln_g[l], sgu_ln_b[l], sgu_w[l], sgu_b[l],
                             w_branch_a[l], w_branch_b[l], w_branch_c[l], w_out[l], need_ctx)
        x = layer_norm(DEEPNORM_ALPHA * x + mod[2] * y, ln_mix_g[l], ln_mix_b[l])
        f = squared_relu_mlp(modulate(x, mod[3], mod[4]), w_ff1[l], w_ff2[l])
        x = layer_norm(DEEPNORM_ALPHA * x + mod[5] * f, ln_ff_g[l], ln_ff_b[l])
        if need_ctx:
            xc = layer_norm(DEEPNORM_ALPHA * xc + mod_c[2] * y_c, ln_mix_g[l], ln_mix_b[l])
            f_c = squared_relu_mlp(modulate(xc, mod_c[3], mod_c[4]), w_ff1[l], w_ff2[l])
            xc = layer_norm(DEEPNORM_ALPHA * xc + mod_c[5] * f_c, ln_ff_g[l], ln_ff_b[l])
    return x
```

```python
import contextlib
import math
import numpy as np
import concourse.bass as bass
import concourse.mybir as mybir
from concourse.bass_utils import run_bass_kernel_spmd

F32 = mybir.dt.float32
BF16 = mybir.dt.bfloat16
AF = mybir.ActivationFunctionType
ALU = mybir.AluOpType

ENGS = ("pe", "act", "dve", "pool", "sp")
ENGMAP = {"pe": "tensor", "act": "scalar", "dve": "vector", "pool": "gpsimd", "sp": "sync"}

D = 2048
KC = 16
DFF = 8192
NMOD = 6
DEPTH = 2
ALPHA = (2 * DEPTH) ** 0.25
LN_EPS = 1e-5
RMS_EPS = 1e-6
GW = 64
NFM = 100
NIN = NFM * 128 + 1536
C_QKV, C_Z, C_SQ, C_SK, C_U, C_G = 0, 24, 32, 40, 44, 52


class Res:
    __slots__ = ("w", "r", "dsem", "dcnt", "name", "scoped")
    registry = []

    def __init__(self, name=""):
        Res.registry.append(self)
        self.w = None
        self.r = {}
        self.dsem = None
        self.dcnt = 0
        self.name = name


class Prog:
    def __init__(self, nc):
        self.nc = nc
        self.stack = contextlib.ExitStack()
        self.scopes = []
        self.streams = {e: [] for e in ENGS}
        self.esem = {}
        self.cnt = {e: 0 for e in ENGS}
        self.seen = {e: {} for e in ENGS}
        self.nsem = 0
        self.ntile = 0
        self.dres = []
        self.pool = []
        self.marks = []
        Res.registry = []
        for e in ENGS:
            self.esem[e] = self.new_sem("e_" + e)

    def new_sem(self, name):
        self.nsem += 1
        return self.stack.enter_context(self.nc.semaphore(f"{name}_{self.nsem}"))

    def push(self):
        self.scopes.append(contextlib.ExitStack())
        self.marks.append(len(Res.registry))

    def pop(self):
        self.barrier()
        mark = self.marks.pop()
        dead = Res.registry[mark:]
        del Res.registry[mark:]
        deadset = set(id(r) for r in dead)
        for r in dead:
            if r.dsem is not None:
                self.pool.append((r.dsem, r.dcnt))
        self.dres = [r for r in self.dres if id(r) not in deadset]
        self.scopes.pop().close()

    def _st(self):
        return self.scopes[-1] if self.scopes else self.stack

    def sb(self, shape, dt, name="t"):
        self.ntile += 1
        return self._st().enter_context(self.nc.sbuf_tensor(f"{name}_{self.ntile}", list(shape), dt))

    def ps(self, shape, dt=F32, name="p"):
        self.ntile += 1
        return self._st().enter_context(self.nc.psum_tensor(f"{name}_{self.ntile}", list(shape), dt))

    def _waits(self, eng, reads, writes):
        need = {}
        for r in reads:
            if r.w is not None:
                s, v = r.w
                if need.get(s, 0) < v:
                    need[s] = v
        for w in writes:
            if w.w is not None:
                s, v = w.w
                if need.get(s, 0) < v:
                    need[s] = v
            for s, v in w.r.items():
                if need.get(s, 0) < v:
                    need[s] = v
        self._emit_waits(eng, need)

    def _emit_waits(self, eng, need):
        seen = self.seen[eng]
        st = self.streams[eng]
        own = self.esem[eng]
        for s, v in need.items():
            if s is own and v > self.cnt[eng]:
                continue
            if seen.get(s, 0) < v:
                seen[s] = v
                st.append(("w", s, v))

    def _record(self, ev, reads, writes):
        s, v = ev
        for r in reads:
            if r.r.get(s, 0) < v:
                r.r[s] = v
        for w in writes:
            w.w = ev
            w.r = {}

    def op(self, eng, fn, reads=(), writes=(), signal=True):
        self._waits(eng, reads, writes)
        if signal:
            self.cnt[eng] += 1
            ev = (self.esem[eng], self.cnt[eng])
            self.streams[eng].append(("i", fn, self.esem[eng], 1))
        else:
            ev = (self.esem[eng], self.cnt[eng] + 1)
            self.streams[eng].append(("i", fn, None, 0))
        self._record(ev, reads, writes)

    def dma(self, q, out, in_, semres, reads=(), writes=()):
        self._waits(q, reads, writes)
        if semres.dsem is None:
            if self.pool:
                semres.dsem, semres.dcnt = self.pool.pop()
            else:
                semres.dsem = self.new_sem("d")
            self.dres.append(semres)
        semres.dcnt += 16
        ev = (semres.dsem, semres.dcnt)
        self.streams[q].append(("i", (lambda e, o=out, i=in_: e.dma_start(out=o, in_=i)), semres.dsem, 16))
        self._record(ev, reads, writes)

    def barrier(self):
        need = {self.esem[e]: self.cnt[e] for e in ENGS if self.cnt[e] > 0}
        for r in self.dres:
            need[r.dsem] = r.dcnt
        for e in ENGS:
            self._emit_waits(e, dict(need))

    def finish(self):
        self.barrier()
        nc = self.nc
        with nc.Block() as block:
            for e in ENGS:
                items = self.streams[e]

                def body(eng, items=items):
                    for it in items:
                        if it[0] == "w":
                            eng.wait_ge(it[1], it[2])
                        else:
                            ins = it[1](eng)
                            if it[2] is not None:
                                ins.then_inc(it[2], it[3])
                getattr(block, ENGMAP[e])(body)
        while self.scopes:
            self.scopes.pop().close()
        self.stack.close()


class Ring:
    def __init__(self, P, n, shape, dt, name, psum=False):
        self.t = [(P.ps(shape, dt, name) if psum else P.sb(shape, dt, name)) for _ in range(n)]
        self.r = [Res(name) for _ in range(n)]
        self.i = 0
        self.n = n

    def next(self):
        k = self.i % self.n
        self.i += 1
        return self.t[k], self.r[k]


def perm_in_cols():
    p = []
    p += list(range(0, 3072))
    p += list(range(3072, 4096))
    p += list(range(4128, 5152))
    for g in range(4):
        p += list(range(5152 + g * 64, 5152 + (g + 1) * 64)) * 2
    p += list(range(5664, 6688))
    p += list(range(7712, 13856))
    assert len(p) == NFM * 128
    p += list(range(5408, 5664))
    p += list(range(4096, 4128))
    p += [-1] * 224
    p += list(range(6688, 7712))
    assert len(p) == NIN
    return np.array(p)


import os
def build(T, L, nlayers=DEPTH, debug=None):
    try:
        return _build(T, L, nlayers, debug)
    except _Stop as e:
        return e.nc


_LAST = {}


class _Stop(Exception):
    pass


def _build(T, L, nlayers=DEPTH, debug=None):
    debug = debug or set()
    Tt = T + L
    nc = bass.Bass("TRN2", target_bir_lowering=False)
    P = Prog(nc)
    _LAST["P"] = P

    def din(name, shape, dt=F32):
        return nc.dram_tensor(name, list(shape), dt, kind="ExternalInput").ap()

    def dscr(name, shape, dt=F32):
        kind = "ExternalOutput" if name in debug else "Internal"
        return nc.dram_tensor(name, list(shape), dt, kind=kind).ap()

    NL = nlayers
    xin = din("xin", [Tt, D])
    cT = din("cT", [128, KC, 2])
    w_ada = din("w_ada", [NL, D, NMOD * D])
    b_adaT = din("b_adaT", [NL, 128, 96])
    b_adabc = din("b_adabc", [NL, 128, NMOD * D])
    w_in = din("w_in", [NL, D, NIN])
    b_inT = din("b_inT", [NL, 128, NFM])
    b_intok = din("b_intok", [NL, 128, 1536])
    convT = din("convT", [NL, 128, 24, 5])
    alog = din("alog", [NL, 128, 16])
    dtb = din("dtb", [NL, 128, 16])
    gnormT = din("gnormT", [NL, 128, 1])
    sinks = din("sinks", [NL, 128, 16])
    sgu_g = din("sgu_g", [NL, 128, 1024])
    sgu_bb = din("sgu_bb", [NL, 128, 1024])
    sgu_wT = din("sgu_wT", [NL, 8, 128, 128])
    sgu_bias = din("sgu_bias", [NL, 128, 1024])
    w_a = din("w_a", [NL, 1024, D])
    w_b = din("w_b", [NL, 1024, D])
    w_c = din("w_c", [NL, 1024, D])
    w_o = din("w_o", [NL, D, D])
    lnm_g = din("lnm_g", [NL, 128, D])
    lnm_b = din("lnm_b", [NL, 128, D])
    w_f1 = din("w_f1", [NL, D, DFF])
    w_f2 = din("w_f2", [NL, DFF, D])
    lnf_g = din("lnf_g", [NL, 128, D])
    lnf_b = din("lnf_b", [NL, 128, D])
    cosT = din("cosT", [128, Tt])
    sinT = din("sinT", [128, Tt])
    consts = din("consts", [12, 128, 128])
    esel = din("esel", [8, 8, 128])
    swamask = din("swamask", [2, 128, 128])
    out = nc.dram_tensor("out", [T, D], F32, kind="ExternalOutput").ap()

    X = dscr("X", [Tt, D])
    wb_in = [dscr(f"wb_in{l}", [D, NIN], BF16) for l in range(NL)]
    wb_a = [dscr(f"wb_a{l}", [1024, D], BF16) for l in range(NL)]
    wb_b = [dscr(f"wb_b{l}", [1024, D], BF16) for l in range(NL)]
    wb_c = [dscr(f"wb_c{l}", [1024, D], BF16) for l in range(NL)]
    wb_o = [dscr(f"wb_o{l}", [D, D], BF16) for l in range(NL)]
    wb_f1 = [dscr(f"wb_f1{l}", [D, DFF], BF16) for l in range(NL)]
    wb_f2 = [dscr(f"wb_f2{l}", [DFF, D], BF16) for l in range(NL)]
    QKVT = dscr("QKVT", [24, 128, Tt], BF16)
    ZT = dscr("ZT", [8, 128, Tt], BF16)
    SQT = dscr("SQT", [8, 128, Tt], BF16)
    SKT = dscr("SKT", [4, 128, Tt], BF16)
    SV = dscr("SV", [Tt, 256], BF16)
    UT = dscr("UT", [8, 128, Tt], BF16)
    SGV = dscr("SGV", [Tt, 1024], BF16)
    GT = dscr("GT", [48, 128, Tt], BF16)
    GG = dscr("GG", [Tt, 16])
    BETA = dscr("BETA", [Tt, 16])
    QNT = dscr("QNT", [8, 128, Tt], BF16)
    KNT = dscr("KNT", [8, 128, Tt], BF16)
    QTOK = dscr("QTOK", [Tt, 8, 128], BF16)
    KTOK = dscr("KTOK", [Tt, 8, 128], BF16)
    VTOK = dscr("VTOK", [Tt, 8, 128], BF16)
    OACC = dscr("OACC", [2, Tt, 8, 128])
    YAT = dscr("YAT", [8, 128, Tt], BF16)
    YBT = dscr("YBT", [8, 128, Tt], BF16)
    YCT = dscr("YCT", [8, 128, Tt], BF16)

    R_X = Res("X")
    R_W = Res("W")
    R_P1 = Res("P1")
    R_F = Res("F")
    R_O = Res("O")
    R_Y = Res("Y")

    segs = [(0, L, True)] + [(L + i * 512, min(512, T - i * 512), False) for i in range((T + 511) // 512)]

    cst = P.sb([128, 12, 128], F32, "cst")
    cstb = P.sb([128, 12, 128], BF16, "cstb")
    R_c = Res("c")
    P.dma("sp", cst[:], consts.rearrange("c p m -> p c m"), R_c, writes=[R_c])
    P.op("dve", lambda e: e.tensor_copy(out=cstb[:], in_=cst[:]), reads=[R_c], writes=[R_c])
    ident = cst[:, 0, :]
    identb = cstb[:, 0, :]
    permb = cstb[:, 1, :]
    onesb = cstb[:, 2, :]
    esl = P.sb([8, 8, 128], F32, "esel")
    P.dma("sp", esl[:], esel.rearrange("h r m -> r h m"), R_c, writes=[R_c])
    smk = P.sb([128, 2, 128], F32, "smk")
    P.dma("sp", smk[:], swamask.rearrange("c p m -> p c m"), R_c, writes=[R_c])
    epsln = P.sb([128, 2], F32, "epsln")
    P.op("dve", lambda e: e.memset(epsln[:, 0:1], LN_EPS), writes=[R_c])
    P.op("dve", lambda e: e.memset(epsln[:, 1:2], RMS_EPS), writes=[R_c])
    modT = P.sb([128, 96, 2], F32, "modT")
    R_mod = Res("mod")
    psum_box = [None]

    P.push()
    xr = Ring(P, 2, [128, D], F32, "xc")
    for t0 in range(0, Tt, 128):
        xt, xres = xr.next()
        P.dma("sp", xt[:], xin[t0:t0 + 128, :], xres, writes=[xres])
        P.dma("sp", X[t0:t0 + 128, :], xt[:], xres, reads=[xres], writes=[R_X])
    P.pop()

    def cast_weight(src, dst, K, M):
        wr, wo = cast_rings
        n = 0
        for kc in range(K // 128):
            for m0 in range(0, M, 2048):
                mw = min(2048, M - m0)
                a, ar = wr.next()
                b, br = wo.next()
                P.dma("sp", a[:, :mw], src[kc * 128:(kc + 1) * 128, m0:m0 + mw], ar, writes=[ar])
                eng = ("dve", "act", "pool")[n % 3]
                if eng == "act":
                    P.op("act", lambda e, a=a, b=b, mw=mw: e.activation(out=b[:, :mw], in_=a[:, :mw], func=AF.Copy), reads=[ar], writes=[br])
                else:
                    P.op(eng, lambda e, a=a, b=b, mw=mw: e.tensor_copy(out=b[:, :mw], in_=a[:, :mw]), reads=[ar], writes=[br])
                P.dma("sp", dst[kc * 128:(kc + 1) * 128, m0:m0 + mw], b[:, :mw], br, reads=[br], writes=[R_W])
                n += 1

    P.push()
    cast_rings = (Ring(P, 2, [128, 2048], F32, "wc"), Ring(P, 2, [128, 2048], BF16, "wo"))
    for l in range(NL):
        cast_weight(w_in[l], wb_in[l], D, NIN)
        cast_weight(w_a[l], wb_a[l], 1024, D)
        cast_weight(w_b[l], wb_b[l], 1024, D)
        cast_weight(w_c[l], wb_c[l], 1024, D)
        cast_weight(w_o[l], wb_o[l], D, D)
        cast_weight(w_f1[l], wb_f1[l], D, DFF)
        cast_weight(w_f2[l], wb_f2[l], DFF, D)
    P.pop()

    def stop_if(tag):
        if os.environ.get("KSTOP") == tag:
            P.finish()
            ex = _Stop()
            ex.nc = nc
            raise ex

    stop_if("W")
    def gelu_tanh(src_ps, src_res, bias_ap, out_ap, out_res, W, tmp, tmpr):
        a = tmp[:, 0, :W]
        b = tmp[:, 1, :W]
        if bias_ap is not None:
            P.op("act", lambda e: e.activation(out=a, in_=src_ps, func=AF.Identity, bias=bias_ap), reads=[R_mod, src_res], writes=[tmpr])
        else:
            P.op("act", lambda e: e.activation(out=a, in_=src_ps, func=AF.Copy), reads=[src_res], writes=[tmpr])
        P.op("dve", lambda e: e.tensor_tensor(out=b, in0=a, in1=a, op=ALU.mult), reads=[tmpr], writes=[tmpr])
        P.op("dve", lambda e: e.tensor_scalar(out=b, in0=b, scalar1=0.044715, scalar2=1.0, op0=ALU.mult, op1=ALU.add), reads=[tmpr], writes=[tmpr])
        P.op("dve", lambda e: e.tensor_tensor(out=b, in0=b, in1=a, op=ALU.mult), reads=[tmpr], writes=[tmpr])
        P.op("act", lambda e: e.activation(out=b, in_=b, func=AF.Sigmoid, scale=1.5957691216057308), reads=[tmpr], writes=[tmpr])
        P.op("dve", lambda e: e.tensor_tensor(out=out_ap, in0=b, in1=a, op=ALU.mult), reads=[tmpr], writes=[out_res])

    def layer_norm_tile(r_ap, g_ap, b_ap, out_ap, rres, small, smallr, ncols=D):
        nchunk = ncols // 512
        st = small[:, 0:nchunk * 6]
        mv = small[:, 24:26]
        rs = small[:, 26:27]
        for c in range(nchunk):
            P.op("dve", lambda e, c=c: e.bn_stats(out=small[:, c * 6:(c + 1) * 6], in_=r_ap[:, c * 512:(c + 1) * 512]), reads=[rres], writes=[smallr])
        P.op("dve", lambda e: e.bn_aggr(out=mv, in_=st), reads=[smallr], writes=[smallr])
        P.op("act", lambda e: e.activation(out=rs, in_=small[:, 25:26], func=AF.Sqrt, bias=epsln[:, 0:1]), reads=[smallr, R_c], writes=[smallr])
        P.op("dve", lambda e: e.reciprocal(out=rs, in_=rs), reads=[smallr], writes=[smallr])
        P.op("dve", lambda e: e.tensor_scalar(out=out_ap, in0=r_ap, scalar1=small[:, 24:25], scalar2=rs, op0=ALU.subtract, op1=ALU.mult), reads=[smallr, rres], writes=[rres])
        P.op("pool", lambda e: e.tensor_tensor(out=out_ap, in0=out_ap, in1=g_ap, op=ALU.mult), reads=[rres, R_mod], writes=[rres])
        P.op("pool", lambda e: e.tensor_tensor(out=out_ap, in0=out_ap, in1=b_ap, op=ALU.add), reads=[rres, R_mod], writes=[rres])

    for l in range(NL):
        need_ctx = l < NL - 1
        P.push()
        psum = Ring(P, 8, [128, 512], F32, "ps", psum=True)
        psum_box[0] = psum
        csl = P.sb([128, KC, 2], F32, "csl")
        cbc = P.sb([128, KC, 2, 128], F32, "cbc")
        R_cs = Res("cs")
        P.dma("sp", csl[:], cT[:, :, :], R_cs, writes=[R_cs])
        P.op("act", lambda e: e.activation(out=csl[:], in_=csl[:], func=AF.Silu), reads=[R_cs], writes=[R_cs])
        for b in range(2):
            P.op("dve", lambda e, b=b: e.tensor_copy(out=cbc[:, :, b, :], in_=csl[:, :, b:b + 1].to_broadcast([128, KC, 128])), reads=[R_cs], writes=[R_cs])
        MODBC = dscr(f"MODBC{l}", [2, 2, 128, D])
        R_mbc = Res("mbc")
        wring = Ring(P, 2, [128, KC, 512], F32, "wada")
        bring = Ring(P, 2, [128, 512], F32, "bada")
        stg = Ring(P, 2, [128, 512], F32, "stg")
        for blk in range(24):
            wt, wres = wring.next()
            for k4 in range(4):
                P.dma("sp", wt[:, k4 * 4:(k4 + 1) * 4, :], w_ada[l][k4 * 512:(k4 + 1) * 512, blk * 512:(blk + 1) * 512].rearrange("(k p) m -> p k m", p=128), wres, writes=[wres])
            bt, btr = bring.next()
            P.dma("sp", bt[:], b_adabc[l][:, blk * 512:(blk + 1) * 512], btr, writes=[btr])
            mi = blk // 4
            for b in range(2):
                pt, ptr = psum.next()
                for kc in range(KC):
                    P.op("pe", lambda e, wt=wt, kc=kc, b=b, pt=pt: e.matmul(out=pt[:], lhsT=cbc[:, kc, b, :], rhs=wt[:, kc, :], start=(kc == 0), stop=(kc == KC - 1)),
                         reads=[wres, R_cs], writes=[ptr], signal=(kc == KC - 1))
                s, sr = stg.next()
                P.op("dve", lambda e, s=s, pt=pt, bt=bt: e.tensor_tensor(out=s[:], in0=pt[:], in1=bt[:], op=ALU.add), reads=[ptr, btr], writes=[sr])
                if mi in (2, 5):
                    which = 0 if mi == 2 else 1
                    c0 = (blk % 4) * 512
                    P.dma("sp", MODBC[which, b, :, c0:c0 + 512], s[:], sr, reads=[sr], writes=[R_mbc])
                else:
                    p2, p2r = psum.next()
                    for j in range(4):
                        P.op("pe", lambda e, p2=p2, s=s, j=j: e.transpose(out=p2[:, j * 128:(j + 1) * 128], in_=s[:, j * 128:(j + 1) * 128], identity=ident), reads=[sr, R_c], writes=[p2r], signal=(j == 3))
                    P.op("dve", lambda e, p2=p2, blk=blk, b=b: e.tensor_copy(out=modT[:, blk * 4:(blk + 1) * 4, b], in_=p2[:].rearrange("p (j c) -> p j c", c=128)[:, :, 0]), reads=[p2r], writes=[R_mod])
        for mi in (1, 4):
            P.op("dve", lambda e, mi=mi: e.tensor_scalar(out=modT[:, mi * 16:(mi + 1) * 16, :], in0=modT[:, mi * 16:(mi + 1) * 16, :], scalar1=1.0, scalar2=None, op0=ALU.add), reads=[R_mod], writes=[R_mod])
        P.pop()

        stop_if("M")

        def transpose_mod(src_tile, srcres, dstT, dstres, ti, shift_i, scale_i, b):
            for k4 in range(4):
                pt, ptr = psum_box[0].next()
                for j in range(4):
                    kc = k4 * 4 + j
                    P.op("pe", lambda e, pt=pt, j=j, kc=kc: e.transpose(out=pt[:, j * 128:(j + 1) * 128], in_=src_tile[:, kc * 128:(kc + 1) * 128], identity=ident),
                         reads=[srcres, R_c], writes=[ptr], signal=(j == 3))
                for j in range(4):
                    kc = k4 * 4 + j
                    eng = "act" if j % 2 == 0 else "dve"
                    sc = modT[:, scale_i * 16 + kc, b:b + 1]
                    sh = modT[:, shift_i * 16 + kc, b:b + 1]
                    o = dstT[:, kc, ti * 128:(ti + 1) * 128]
                    i = pt[:, j * 128:(j + 1) * 128]
                    if eng == "act":
                        P.op("act", lambda e, o=o, i=i, sc=sc, sh=sh: e.activation(out=o, in_=i, func=AF.Identity, scale=sc, bias=sh), reads=[ptr, R_mod], writes=[dstres])
                    else:
                        P.op("dve", lambda e, o=o, i=i, sc=sc, sh=sh: e.tensor_scalar(out=o, in0=i, scalar1=sc, scalar2=sh, op0=ALU.mult, op1=ALU.add), reads=[ptr, R_mod], writes=[dstres])

        def load_wblk(ring, src, k0, nk, m0, mw=512):
            wt, wres = ring.next()
            for k4 in range(0, nk, 4):
                n4 = min(4, nk - k4)
                P.dma("sp", wt[:, k4:k4 + n4, :mw], src[(k0 + k4) * 128:(k0 + k4 + n4) * 128, m0:m0 + mw].rearrange("(k p) m -> p k m", p=128), wres, reads=[R_W], writes=[wres])
            return wt, wres

        P.push()
        psum = Ring(P, 8, [128, 512], F32, "ps", psum=True)
        psum_box[0] = psum
        xring = Ring(P, 2, [128, D], F32, "x1")
        hT = P.sb([128, KC, 512], BF16, "hT")
        R_h = Res("h")
        wring = Ring(P, 2, [128, KC, 512], BF16, "w1")
        stg = Ring(P, 4, [128, 512], BF16, "stg")
        stg32 = Ring(P, 2, [128, 2, 512], F32, "stg32")
        qb_r = Ring(P, 2, [128, 512], BF16, "qb")
        cs_r = Ring(P, 2, [128, 2, 512], F32, "cs")
        binT = P.sb([128, NFM], F32, "binT")
        P.dma("sp", binT[:], b_inT[l], R_mod, writes=[R_mod])
        btok = P.sb([128, 1536], F32, "btok")
        P.dma("sp", btok[:], b_intok[l], R_mod, writes=[R_mod])
        tmb = Ring(P, 2, [128, 512], F32, "tmb")
        sg_g = P.sb([128, 1024], F32, "sg_g")
        sg_b = P.sb([128, 1024], F32, "sg_b")
        P.dma("sp", sg_g[:], sgu_g[l], R_mod, writes=[R_mod])
        P.dma("sp", sg_b[:], sgu_bb[l], R_mod, writes=[R_mod])
        al = P.sb([128, 16], F32, "al")
        db = P.sb([128, 16], F32, "db")
        P.dma("sp", al[:], alog[l], R_mod, writes=[R_mod])
        P.dma("sp", db[:], dtb[l], R_mod, writes=[R_mod])
        P.op("act", lambda e: e.activation(out=al[:], in_=al[:], func=AF.Exp), reads=[R_mod], writes=[R_mod])
        P.op("dve", lambda e: e.tensor_scalar(out=al[:], in0=al[:], scalar1=-1.0, scalar2=None, op0=ALU.mult), reads=[R_mod], writes=[R_mod])
        sgv = Ring(P, 4, [128, 1024], F32, "sgv")
        sgvb = Ring(P, 2, [128, 1024], BF16, "sgvb")
        small = Ring(P, 2, [128, 32], F32, "small")
        gbt = Ring(P, 2, [128, 64], F32, "gbt")

        for (s0, W, isctx) in segs:
            b = 1 if isctx else 0
            nt = W // 128
            for ti in range(nt):
                xt, xres = xring.next()
                P.dma("sp", xt[:], X[s0 + ti * 128:s0 + (ti + 1) * 128, :], xres, reads=[R_X], writes=[xres])
                transpose_mod(xt, xres, hT, R_h, ti, 0, 1, b)
            cs, csr = cs_r.next()
            P.dma("sp", cs[:, 0, :W], cosT[:, s0:s0 + W], csr, writes=[csr])
            P.dma("sp", cs[:, 1, :W], sinT[:, s0:s0 + W], csr, writes=[csr])
            for blk in range(NFM // 4):
                wt, wres = load_wblk(wring, wb_in[l], 0, KC, blk * 512)
                for j in range(4):
                    ch = blk * 4 + j
                    pt, ptr = psum.next()
                    for kc in range(KC):
                        P.op("pe", lambda e, pt=pt, wt=wt, kc=kc, j=j, W=W: e.matmul(out=pt[:, :W], lhsT=wt[:, kc, j * 128:(j + 1) * 128], rhs=hT[:, kc, :W], start=(kc == 0), stop=(kc == KC - 1)),
                             reads=[wres, R_h], writes=[ptr], signal=(kc == KC - 1))
                    bias = binT[:, ch:ch + 1]
                    s, sr = stg.next()
                    if ch < C_Z:
                        P.op("act", lambda e, s=s, pt=pt, W=W, bias=bias: e.activation(out=s[:, :W], in_=pt[:, :W], func=AF.Identity, bias=bias), reads=[ptr, R_mod], writes=[sr])
                        dst = QKVT[ch, :, s0:s0 + W]
                    elif ch < C_SQ:
                        P.op("act", lambda e, s=s, pt=pt, W=W, bias=bias: e.activation(out=s[:, :W], in_=pt[:, :W], func=AF.Silu, bias=bias), reads=[ptr, R_mod], writes=[sr])
                        dst = ZT[ch - C_Z, :, s0:s0 + W]
                    elif ch < C_U:
                        qb, qbr = qb_r.next()
                        P.op("act", lambda e, qb=qb, pt=pt, W=W, bias=bias: e.activation(out=qb[:, :W], in_=pt[:, :W], func=AF.Identity, bias=bias), reads=[ptr, R_mod], writes=[qbr])
                        p2, p2r = psum.next()
                        P.op("pe", lambda e, p2=p2, qb=qb, W=W: e.matmul(out=p2[:, :W], lhsT=permb, rhs=qb[:, :W], start=True, stop=True), reads=[qbr, R_c], writes=[p2r])
                        t32, t32r = stg32.next()
                        P.op("dve", lambda e, t32=t32, qb=qb, cs=cs, W=W: e.tensor_tensor(out=t32[:, 0, :W], in0=qb[:, :W], in1=cs[:, 0, :W], op=ALU.mult), reads=[qbr, csr], writes=[t32r])
                        P.op("dve", lambda e, t32=t32, p2=p2, cs=cs, W=W: e.tensor_tensor(out=t32[:, 1, :W], in0=p2[:, :W], in1=cs[:, 1, :W], op=ALU.mult), reads=[p2r, csr], writes=[t32r])
                        P.op("pool", lambda e, t32=t32, s=s, W=W: e.tensor_tensor(out=s[:, :W], in0=t32[:, 0, :W], in1=t32[:, 1, :W], op=ALU.add), reads=[t32r], writes=[sr])
                        dst = SQT[ch - C_SQ, :, s0:s0 + W] if ch < C_SK else SKT[ch - C_SK, :, s0:s0 + W]
                    elif ch < C_G:
                        t32, t32r = stg32.next()
                        gelu_tanh(pt[:, :W], ptr, bias, s[:, :W], sr, W, t32, t32r)
                        dst = UT[ch - C_U, :, s0:s0 + W]
                    else:
                        P.op("act", lambda e, s=s, pt=pt, W=W, bias=bias: e.activation(out=s[:, :W], in_=pt[:, :W], func=AF.Sigmoid, bias=bias), reads=[ptr, R_mod], writes=[sr])
                        dst = GT[ch - C_G, :, s0:s0 + W]
                    P.dma("sp", dst, s[:, :W], sr, reads=[sr], writes=[R_P1])
            sv_store = [sgv.next() for _ in range(nt)]
            for tb in range(3):
                wt, wres = load_wblk(wring, wb_in[l], 0, KC, NFM * 128 + tb * 512)
                for ti in range(nt):
                    r0 = s0 + ti * 128
                    pt, ptr = psum.next()
                    for kc in range(KC):
                        P.op("pe", lambda e, pt=pt, wt=wt, kc=kc, ti=ti: e.matmul(out=pt[:], lhsT=hT[:, kc, ti * 128:(ti + 1) * 128], rhs=wt[:, kc, :], start=(kc == 0), stop=(kc == KC - 1)),
                             reads=[wres, R_h], writes=[ptr], signal=(kc == KC - 1))
                    pp, ppr = pt, ptr
                    pt, ptr = tmb.next()
                    P.op("dve", lambda e, pt=pt, pp=pp, tb=tb: e.tensor_tensor(out=pt[:], in0=pp[:], in1=btok[:, tb * 512:(tb + 1) * 512], op=ALU.add), reads=[ppr, R_mod], writes=[ptr])
                    if tb == 0:
                        s, sr = stg.next()
                        P.op("act", lambda e, s=s, pt=pt: e.activation(out=s[:, 0:256], in_=pt[:, 0:256], func=AF.Copy), reads=[ptr], writes=[sr])
                        P.dma("sp", SV[r0:r0 + 128, :], s[:, 0:256], sr, reads=[sr], writes=[R_P1])
                        g, gr = gbt.next()
                        P.op("act", lambda e, g=g, pt=pt: e.activation(out=g[:, 0:16], in_=pt[:, 256:272], func=AF.Sigmoid), reads=[ptr], writes=[gr])
                        P.op("dve", lambda e, g=g, pt=pt: e.tensor_tensor(out=g[:, 32:48], in0=pt[:, 272:288], in1=db[:], op=ALU.add), reads=[ptr, R_mod], writes=[gr])
                        P.op("act", lambda e, g=g: e.activation(out=g[:, 32:48], in_=g[:, 32:48], func=AF.Exp), reads=[gr], writes=[gr])
                        P.op("act", lambda e, g=g: e.activation(out=g[:, 32:48], in_=g[:, 32:48], func=AF.Ln, bias=1.0), reads=[gr], writes=[gr])
                        P.op("dve", lambda e, g=g: e.tensor_tensor(out=g[:, 16:32], in0=g[:, 32:48], in1=al[:], op=ALU.mult), reads=[gr, R_mod], writes=[gr])
                        P.dma("sp", BETA[r0:r0 + 128, :], g[:, 0:16], gr, reads=[gr], writes=[R_P1])
                        P.dma("sp", GG[r0:r0 + 128, :], g[:, 16:32], gr, reads=[gr], writes=[R_P1])
                    else:
                        v32, v32r = sv_store[ti]
                        t32, t32r = stg32.next()
                        gelu_tanh(pt[:], ptr, None, v32[:, (tb - 1) * 512:tb * 512], v32r, 512, t32, t32r)
                        if tb == 2:
                            sm, smr = small.next()
                            vb, vbr = sgvb.next()
                            layer_norm_tile(v32[:], sg_g[:], sg_b[:], v32[:], v32r, sm, smr, ncols=1024)
                            P.op("act", lambda e, vb=vb, v32=v32: e.activation(out=vb[:], in_=v32[:], func=AF.Copy), reads=[v32r], writes=[vbr])
                            P.dma("sp", SGV[r0:r0 + 128, :], vb[:], vbr, reads=[vbr], writes=[R_P1])
        P.pop()

        build_mixers(P, nc, locals(), l)
        build_merge_ffn(P, nc, locals(), l)

    P.push()
    xr = Ring(P, 2, [128, D], F32, "xo")
    for t0 in range(0, T, 128):
        xt, xres = xr.next()
        P.dma("sp", xt[:], X[L + t0:L + t0 + 128, :], xres, reads=[R_X], writes=[xres])
        P.dma("sp", out[t0:t0 + 128, :], xt[:], xres, reads=[xres])
    P.pop()
    P.finish()
    return nc


def build_mixers(P, nc, env, l):
    g = env
    L, T, Tt = g["L"], g["T"], g["Tt"]
    need_ctx = g["need_ctx"]
    R_P1, R_Y, R_c, R_mod = g["R_P1"], g["R_Y"], g["R_c"], g["R_mod"]
    cstb, smk = g["cstb"], g["smk"]
    onesb = cstb[:, 2, :]
    P.push()
    psum = Ring(P, 8, [128, 512], F32, "ps", psum=True)
    wsT32 = P.sb([128, 8, 128], F32, "wsT32")
    wsT = P.sb([128, 8, 128], BF16, "wsT")
    P.dma("sp", wsT32[:], g["sgu_wT"][l].rearrange("g q p -> q g p"), R_mod, writes=[R_mod])
    P.op("dve", lambda e: e.tensor_copy(out=wsT[:], in_=wsT32[:]), reads=[R_mod], writes=[R_mod])
    sbias = P.sb([128, 1024], F32, "sbias")
    P.dma("sp", sbias[:], g["sgu_bias"][l], R_mod, writes=[R_mod])
    vr = Ring(P, 2, [128, 1024], BF16, "sv")
    ur = Ring(P, 2, [128, 8, 128], BF16, "su")
    tr = Ring(P, 2, [128, 1024], F32, "st")
    yr = Ring(P, 2, [128, 8, 128], BF16, "sy")
    for t0 in range(0, Tt, 128):
        if t0 < L and not need_ctx:
            continue
        v, vres = vr.next()
        u, ures = ur.next()
        P.dma("sp", v[:], g["SGV"][t0:t0 + 128, :], vres, reads=[R_P1], writes=[vres])
        P.dma("sp", u[:], g["UT"][:, :, t0:t0 + 128].rearrange("g c p -> c g p"), ures, reads=[R_P1], writes=[ures])
        tmp, tmpr = tr.next()
        y, yres = yr.next()
        for half in range(2):
            pt, ptr = psum.next()
            for gg in range(4):
                gi = half * 4 + gg
                P.op("pe", lambda e, pt=pt, gg=gg, gi=gi, v=v: e.matmul(out=pt[:, gg * 128:(gg + 1) * 128], lhsT=v[:, gi * 128:(gi + 1) * 128], rhs=wsT[:, gi, :], start=True, stop=True),
                     reads=[vres, R_mod], writes=[ptr], signal=(gg == 3))
            P.op("dve", lambda e, pt=pt, tmp=tmp, half=half: e.tensor_tensor(out=tmp[:, half * 512:(half + 1) * 512], in0=pt[:], in1=sbias[:, half * 512:(half + 1) * 512], op=ALU.add), reads=[ptr, R_mod], writes=[tmpr])
        P.op("pool", lambda e, y=y, tmp=tmp, u=u: e.tensor_tensor(out=y[:].rearrange("c g p -> c (g p)"), in0=tmp[:], in1=u[:].rearrange("c g p -> c (g p)"), op=ALU.mult), reads=[tmpr, ures], writes=[yres])
        P.dma("sp", g["YCT"][:, :, t0:t0 + 128].rearrange("g c p -> c g p"), y[:], yres, reads=[yres], writes=[R_Y])
    P.pop()
    g["stop_if"]("SGU")

    P.push()
    psum = Ring(P, 4, [128, 512], F32, "ps", psum=True)
    pacc = Ring(P, 4, [128, 512], F32, "pa", psum=True)
    esink = P.sb([128, 16], F32, "esink")
    P.dma("sp", esink[:], g["sinks"][l], R_mod, writes=[R_mod])
    P.op("act", lambda e: e.activation(out=esink[:], in_=esink[:], func=AF.Exp), reads=[R_mod], writes=[R_mod])
    kr = Ring(P, 2, [128, 4, 640], BF16, "kk")
    vvr = Ring(P, 2, [128, 5, 256], BF16, "vv")
    qr = Ring(P, 2, [128, 8, 128], BF16, "qq")
    qmr = Ring(P, 2, [128, 8, 2, 128], BF16, "qm")
    cst = g["cst"]
    er = Ring(P, 3, [128, 512], BF16, "ee")
    dr = Ring(P, 2, [64, 512], F32, "dd")
    orr = Ring(P, 2, [64, 512], BF16, "oo")
    nblk = T // 128
    qblocks = ([("c", i) for i in range(L // 128)] if need_ctx else []) + [("l", i) for i in range(nblk)]
    for kind, i in qblocks:
        if kind == "c":
            q0 = i * 128
            kbs = [(j * 128, None) for j in range(L // 128)]
        else:
            q0 = L + i * 128
            kbs = []
            if i > 0:
                kbs.append((L + (i - 1) * 128, 0))
            kbs.append((L + i * 128, None))
            if i < nblk - 1:
                kbs.append((L + (i + 1) * 128, 1))
            kbs += [(j * 128, None) for j in range(L // 128)]
        nk = len(kbs)
        kt, kres = kr.next()
        vt, vres = vvr.next()
        qt, qres = qr.next()
        for bi, (k0, _) in enumerate(kbs):
            P.dma("sp", kt[:, :, bi * 128:(bi + 1) * 128], g["SKT"][:, :, k0:k0 + 128].rearrange("g p t -> p g t"), kres, reads=[R_P1], writes=[kres])
            P.dma("sp", vt[:, bi, :], g["SV"][k0:k0 + 128, :], vres, reads=[R_P1], writes=[vres])
        P.dma("sp", qt[:], g["SQT"][:, :, q0:q0 + 128].rearrange("c p t -> p c t"), qres, reads=[R_P1], writes=[qres])
        qm, qmres = qmr.next()
        for hf in range(2):
            P.op("dve", lambda e, qm=qm, qt=qt, hf=hf: e.tensor_scalar(out=qm[:, :, hf, :], in0=qt[:], scalar1=cst[:, 6 + hf, 0:1], scalar2=None, op0=ALU.mult), reads=[qres, R_c], writes=[qmres])
        for gi in range(4):
            po, por = pacc.next()
            pd, pdr = pacc.next()
            for bi, (k0, mi) in enumerate(kbs):
                pt, ptr = psum.next()
                for hh in range(4):
                    h = gi * 4 + hh
                    pb = (h % 2) * 64
                    P.op("pe", lambda e, pt=pt, hh=hh, h=h, bi=bi, kt=kt, qm=qm, gi=gi: e.matmul(out=pt[:, hh * 128:(hh + 1) * 128], lhsT=kt[:, gi, bi * 128:(bi + 1) * 128], rhs=qm[:, h // 2, h % 2, :], start=True, stop=True),
                         reads=[kres, qmres], writes=[ptr], signal=(hh == 3))
                ee, eres = er.next()
                P.op("act", lambda e, ee=ee, pt=pt: e.activation(out=ee[:], in_=pt[:], func=AF.Exp, scale=0.125), reads=[ptr], writes=[eres])
                if mi is not None:
                    P.op("dve", lambda e, ee=ee, mi=mi: e.tensor_tensor(out=ee[:].rearrange("k (h q) -> k h q", h=4), in0=ee[:].rearrange("k (h q) -> k h q", h=4), in1=smk[:, mi, :].unsqueeze(1).to_broadcast([128, 4, 128]), op=ALU.mult), reads=[eres, R_c], writes=[eres])
                P.op("pe", lambda e, po=po, vt=vt, bi=bi, gi=gi, ee=ee: e.matmul(out=po[0:64, :], lhsT=vt[:, bi, gi * 64:(gi + 1) * 64], rhs=ee[:], start=(bi == 0), stop=(bi == nk - 1)), reads=[vres, eres], writes=[por], signal=(bi == nk - 1))
                P.op("pe", lambda e, pd=pd, ee=ee, bi=bi: e.matmul(out=pd[0:64, :], lhsT=onesb[:, 0:64], rhs=ee[:], start=(bi == 0), stop=(bi == nk - 1)), reads=[eres, R_c], writes=[pdr], signal=(bi == nk - 1))
            dd, dres = dr.next()
            for hh in range(4):
                h = gi * 4 + hh
                P.op("dve", lambda e, dd=dd, pd=pd, hh=hh, h=h: e.tensor_scalar(out=dd[:, hh * 128:(hh + 1) * 128], in0=pd[0:64, hh * 128:(hh + 1) * 128], scalar1=esink[0:64, h:h + 1], scalar2=None, op0=ALU.add), reads=[pdr, R_mod], writes=[dres])
            P.op("dve", lambda e, dd=dd: e.reciprocal(out=dd[:], in_=dd[:]), reads=[dres], writes=[dres])
            oo, ores = orr.next()
            P.op("dve", lambda e, oo=oo, po=po, dd=dd: e.tensor_tensor(out=oo[:], in0=po[0:64, :], in1=dd[:], op=ALU.mult), reads=[por, dres], writes=[ores])
            for hh in range(4):
                h = gi * 4 + hh
                P.dma("sp", g["YBT"][h // 2, (h % 2) * 64:(h % 2) * 64 + 64, q0:q0 + 128], oo[:, hh * 128:(hh + 1) * 128], ores, reads=[ores], writes=[R_Y])
    P.pop()
    g["stop_if"]("SWA")
    build_gdn(P, nc, env, l)


def build_gdn(P, nc, env, l):
    g = env
    L, T, Tt = g["L"], g["T"], g["Tt"]
    need_ctx = g["need_ctx"]
    R_P1, R_Y, R_c = g["R_P1"], g["R_Y"], g["R_c"]
    cst, cstb, epsln = g["cst"], g["cstb"], g["epsln"]
    QKVT, ZT, GG, BETA = g["QKVT"], g["ZT"], g["GG"], g["BETA"]
    QNT, KNT, QTOK, KTOK, VTOK, OACC, YAT = g["QNT"], g["KNT"], g["QTOK"], g["KTOK"], g["VTOK"], g["OACC"], g["YAT"]
    ident, identb, onesb = cst[:, 0, :], cstb[:, 0, :], cstb[:, 2, :]
    R_F = Res("F")
    R_O = Res("O")
    bc3 = lambda ap, n: ap.unsqueeze(2).to_broadcast([128, ap.shape[1], n])

    P.push()
    psum = Ring(P, 4, [128, 512], F32, "ps", psum=True)
    ptb = Ring(P, 2, [128, 512], BF16, "ptb", psum=True)
    cw = P.sb([128, 24, 5], F32, "cw")
    R_cw = Res("cw")
    P.dma("sp", cw[:], g["convT"][l], R_cw, writes=[R_cw])
    winr = Ring(P, 3, [128, 516], BF16, "win")
    accr = Ring(P, 2, [128, 512], F32, "acc")
    sr_ = Ring(P, 2, [128, 512], F32, "sil")
    sqr = Ring(P, 2, [128, 512], BF16, "sq")
    rnr = Ring(P, 2, [128, 512], F32, "rn")
    fbr = Ring(P, 3, [128, 512], BF16, "fb")
    tkr = Ring(P, 3, [128, 512], BF16, "tk")
    segs = [(0, L, 0, L)] + [(L + i * 512, min(512, T - i * 512), L, Tt) for i in range((T + 511) // 512)]
    for (s0, W, lo, hi) in segs:
        nt = W // 128
        for ch in range(24):
            win, wres = winr.next()
            a0 = max(s0 - 2, lo)
            a1 = min(s0 + W + 2, hi)
            if a0 > s0 - 2:
                P.op("pool", lambda e, win=win: e.memset(win[:, 0:2], 0.0), writes=[wres])
            if a1 < s0 + W + 2:
                P.op("pool", lambda e, win=win, W=W: e.memset(win[:, W + 2:W + 4], 0.0), writes=[wres])
            P.dma("sp", win[:, a0 - (s0 - 2):a1 - (s0 - 2)], QKVT[ch, :, a0:a1], wres, reads=[R_P1], writes=[wres])
            acc, ares = accr.next()
            P.op("dve", lambda e, acc=acc, win=win, ch=ch, W=W: e.tensor_scalar(out=acc[:, :W], in0=win[:, 0:W], scalar1=cw[:, ch, 0:1], scalar2=None, op0=ALU.mult), reads=[wres, R_cw], writes=[ares])
            for k in range(1, 5):
                P.op("dve", lambda e, acc=acc, win=win, ch=ch, W=W, k=k: e.scalar_tensor_tensor(out=acc[:, :W], in0=win[:, k:k + W], scalar=cw[:, ch, k:k + 1], in1=acc[:, :W], op0=ALU.mult, op1=ALU.add), reads=[wres, R_cw, ares], writes=[ares])
            sl, slres = sr_.next()
            P.op("act", lambda e, sl=sl, acc=acc, W=W: e.activation(out=sl[:, :W], in_=acc[:, :W], func=AF.Silu), reads=[ares], writes=[slres])
            fb, fres = fbr.next()
            if ch < 16:
                sq, sqres = sqr.next()
                P.op("pool", lambda e, sq=sq, sl=sl, W=W: e.tensor_tensor(out=sq[:, :W], in0=sl[:, :W], in1=sl[:, :W], op=ALU.mult), reads=[slres], writes=[sqres])
                pt, ptr = psum.next()
                P.op("pe", lambda e, pt=pt, sq=sq, W=W: e.matmul(out=pt[:, :W], lhsT=onesb, rhs=sq[:, :W], start=True, stop=True), reads=[sqres, R_c], writes=[ptr])
                rn, rnres = rnr.next()
                P.op("act", lambda e, rn=rn, pt=pt, W=W: e.activation(out=rn[:, :W], in_=pt[:, :W], func=AF.Sqrt, bias=epsln[:, 1:2]), reads=[ptr, R_c], writes=[rnres])
                P.op("dve", lambda e, rn=rn, W=W: e.reciprocal(out=rn[:, :W], in_=rn[:, :W]), reads=[rnres], writes=[rnres])
                qs = (128 ** -0.5) if ch < 8 else 1.0
                P.op("dve", lambda e, fb=fb, sl=sl, rn=rn, W=W, qs=qs: e.scalar_tensor_tensor(out=fb[:, :W], in0=sl[:, :W], scalar=qs, in1=rn[:, :W], op0=ALU.mult, op1=ALU.mult), reads=[slres, rnres], writes=[fres])
                dstT = (QNT if ch < 8 else KNT)[ch % 8, :, s0:s0 + W]
                P.dma("sp", dstT, fb[:, :W], fres, reads=[fres], writes=[R_F])
            else:
                P.op("act", lambda e, fb=fb, sl=sl, W=W: e.activation(out=fb[:, :W], in_=sl[:, :W], func=AF.Copy), reads=[slres], writes=[fres])
            pb, pbr = ptb.next()
            for ti in range(nt):
                P.op("pe", lambda e, pb=pb, fb=fb, ti=ti: e.transpose(out=pb[:, ti * 128:(ti + 1) * 128], in_=fb[:, ti * 128:(ti + 1) * 128], identity=identb), reads=[fres, R_c], writes=[pbr], signal=(ti == nt - 1))
            tk, tkres = tkr.next()
            P.op("act", lambda e, tk=tk, pb=pb, W=W: e.activation(out=tk[:, :W], in_=pb[:, :W], func=AF.Copy), reads=[pbr], writes=[tkres])
            dtok = (QTOK, KTOK, VTOK)[ch // 8]
            P.dma("sp", dtok[s0:s0 + W, ch % 8, :].rearrange("(n p) d -> p n d", p=128), tk[:, :W].rearrange("p (n d) -> p n d", d=128), tkres, reads=[tkres], writes=[R_F])
    P.pop()

    P.push()
    psum = Ring(P, 6, [128, 512], F32, "ps", psum=True)
    ops_ = [P.ps([128, 512], F32, "po") for _ in range(2)]
    R_ops = [Res("po0"), Res("po1")]
    Esel = P.sb([128, 8, 128], F32, "Esel")
    R_E = Res("E")
    for h in range(8):
        P.op("dve", lambda e, h=h: e.tensor_copy(out=Esel[:, h, :], in_=ident[:, h:h + 1].to_broadcast([128, 128])), reads=[R_c], writes=[R_E])
    S32 = P.sb([128, 8, 128], F32, "S32")
    Sb = P.sb([128, 8, 128], BF16, "Sb")
    R_S = Res("S")
    R_Sb = Res("Sb")
    ldr = {n: Ring(P, 2, [128, 8, 128], BF16, n) for n in ("kT", "qT", "ktok", "qtok", "vtok")}
    gpr = Ring(P, 2, [128, 128], F32, "gpad")
    for _ in range(2):
        t_, r_ = gpr.next()
        P.op("pool", lambda e, t_=t_: e.memset(t_[:], 0.0), writes=[r_])
    btr = Ring(P, 2, [128, 8], F32, "beta")
    smr = Ring(P, 2, [128, 12, 8], F32, "sm")
    ctr = Ring(P, 2, [128, 128], F32, "cumT")
    big = {n: Ring(P, 2, [128, 8, 128], BF16, n) for n in ("vb", "kbg", "kd0", "kd1", "dg0", "dg1", "qd0", "qd1", "wTb", "aqk", "v0", "v1", "vf")}
    um = {n: Ring(P, 2, [128, 8, 128], F32, n) for n in ("um0", "um1")}
    f4 = {n: Ring(P, 2, [128, 4, 128], F32, n) for n in ("X", "Ds", "Dt", "N", "A", "inv")}
    invb_r = Ring(P, 2, [128, 4, 128], BF16, "invb")
    ostg = Ring(P, 2, [128, 8, 128], F32, "ostg")
    m_ap = [cst[:, 6, 0:1], cst[:, 7, 0:1]]
    negm = P.sb([128, 2], F32, "negm")
    for c in range(2):
        P.op("dve", lambda e, c=c: e.tensor_scalar(out=negm[:, c:c + 1], in0=m_ap[c], scalar1=-1.0, scalar2=None, op0=ALU.mult), reads=[R_c], writes=[R_E])
    nct, nlt = L // 128, T // 128
    for d in range(2):
        P.op("dve", lambda e: e.memset(S32[:], 0.0), writes=[R_S])
        P.op("dve", lambda e: e.memset(Sb[:], 0.0), writes=[R_Sb])
        tiles = [i * 128 for i in range(nct)] + [L + i * 128 for i in range(nlt)]
        if d == 1:
            tiles = [i * 128 for i in reversed(range(nct))] + [L + i * 128 for i in reversed(range(nlt))]
        order = (0, 1) if d == 0 else (1, 0)
        triT = cst[:, 3 + d, :]
        mposS = cst[:, 8 + d, :]
        mposT = cst[:, 10 + d, :]
        for t0 in tiles:
            ld = {}
            for n, src in (("kT", KNT), ("qT", QNT)):
                t_, r_ = ldr[n].next()
                P.dma("sp", t_[:], src[:, :, t0:t0 + 128].rearrange("h p t -> p h t"), r_, reads=[R_F], writes=[r_])
                ld[n] = (t_, r_)
            for n, src in (("ktok", KTOK), ("qtok", QTOK), ("vtok", VTOK)):
                t_, r_ = ldr[n].next()
                P.dma("sp", t_[:], src[t0:t0 + 128, :, :], r_, reads=[R_F], writes=[r_])
                ld[n] = (t_, r_)
            gp, gpres = gpr.next()
            bt, btres = btr.next()
            P.dma("sp", gp[:, 0:8], GG[t0:t0 + 128, d * 8:(d + 1) * 8], gpres, reads=[R_P1], writes=[gpres])
            P.dma("sp", bt[:], BETA[t0:t0 + 128, d * 8:(d + 1) * 8], btres, reads=[R_P1], writes=[btres])
            pt, ptr = psum.next()
            for j, lm in enumerate((triT, cst[:, 5, :], cst[:, 6, :], cst[:, 7, :])):
                P.op("pe", lambda e, pt=pt, j=j, lm=lm, gp=gp: e.matmul(out=pt[:, j * 128:(j + 1) * 128], lhsT=lm, rhs=gp[:], start=True, stop=True), reads=[gpres, R_c], writes=[ptr], signal=(j == 3))
            sm, smres = smr.next()
            P.op("dve", lambda e, sm=sm, pt=pt: e.tensor_copy(out=sm[:, 0:4, :], in_=pt[:].rearrange("p (a n) -> p a n", n=128)[:, :, 0:8]), reads=[ptr], writes=[smres])
            p2, p2r = psum.next()
            P.op("pe", lambda e, p2=p2, gp=gp, triT=triT: e.matmul(out=p2[:, 0:128], lhsT=gp[:], rhs=triT, start=True, stop=True), reads=[gpres, R_c], writes=[p2r])
            cT_, cTres = ctr.next()
            P.op("act", lambda e, cT_=cT_, p2=p2: e.activation(out=cT_[:], in_=p2[:, 0:128], func=AF.Copy), reads=[p2r], writes=[cTres])
            P.op("act", lambda e, sm=sm: e.activation(out=sm[:, 4, :], in_=sm[:, 0, :], func=AF.Exp), reads=[smres], writes=[smres])
            P.op("dve", lambda e, sm=sm: e.tensor_tensor(out=sm[:, 5, :], in0=sm[:, 1, :], in1=sm[:, 0, :], op=ALU.subtract), reads=[smres], writes=[smres])
            P.op("act", lambda e, sm=sm: e.activation(out=sm[:, 5:8, :], in_=sm[:, (5, 2, 3)[0]:(5, 2, 3)[0] + 1, :], func=AF.Exp) if False else e.activation(out=sm[:, 5, :], in_=sm[:, 5, :], func=AF.Exp), reads=[smres], writes=[smres])
            P.op("act", lambda e, sm=sm: e.activation(out=sm[:, 6:8, :], in_=sm[:, 2:4, :], func=AF.Exp), reads=[smres], writes=[smres])
            P.op("dve", lambda e, sm=sm, bt=bt: e.tensor_scalar(out=sm[:, 8, :], in0=bt[:], scalar1=-1.0, scalar2=None, op0=ALU.mult), reads=[btres], writes=[smres])
            P.op("dve", lambda e, sm=sm, bt=bt: e.tensor_tensor(out=sm[:, 9, :], in0=bt[:], in1=sm[:, 4, :], op=ALU.mult), reads=[btres, smres], writes=[smres])
            kT, kTres = ld["kT"]
            qT, qTres = ld["qT"]
            ktok, ktres = ld["ktok"]
            qtok, qtres = ld["qtok"]
            vtok, vtres = ld["vtok"]
            B = {n: big[n].next() for n in big}
            U = {n: um[n].next() for n in um}
            P.op("dve", lambda e, o=B["vb"][0], vtok=vtok, bt=bt: e.tensor_tensor(out=o[:], in0=vtok[:], in1=bc3(bt[:], 128), op=ALU.mult), reads=[vtres, btres], writes=[B["vb"][1]])
            P.op("dve", lambda e, o=B["kbg"][0], ktok=ktok, sm=sm: e.tensor_tensor(out=o[:], in0=ktok[:], in1=bc3(sm[:, 9, :], 128), op=ALU.mult), reads=[ktres, smres], writes=[B["kbg"][1]])
            for c in range(2):
                P.op("dve", lambda e, sm=sm, c=c: e.tensor_scalar(out=sm[:, 10 + c, :], in0=sm[:, 5, :], scalar1=m_ap[c], scalar2=None, op0=ALU.mult), reads=[smres, R_c], writes=[smres])
                P.op("dve", lambda e, o=B[f"kd{c}"][0], ktok=ktok, sm=sm, c=c: e.tensor_tensor(out=o[:], in0=ktok[:], in1=bc3(sm[:, 10 + c, :], 128), op=ALU.mult), reads=[ktres, smres], writes=[B[f"kd{c}"][1]])
                P.op("dve", lambda e, sm=sm, c=c: e.tensor_scalar(out=sm[:, 10 + c, :], in0=sm[:, 4, :], scalar1=m_ap[c], scalar2=None, op0=ALU.mult), reads=[smres, R_c], writes=[smres])
                P.op("dve", lambda e, o=B[f"dg{c}"][0], sm=sm, c=c: e.tensor_tensor(out=o[:], in0=identb.unsqueeze(1).to_broadcast([128, 8, 128]), in1=bc3(sm[:, 10 + c, :], 128), op=ALU.mult), reads=[smres, R_c], writes=[B[f"dg{c}"][1]])
                for hg in range(2):
                    pq, pqr = psum.next()
                    for hh in range(4):
                        h = hg * 4 + hh
                        P.op("pe", lambda e, pq=pq, hh=hh, h=h, qtok=qtok, dg=B[f"dg{c}"][0]: e.matmul(out=pq[:, hh * 128:(hh + 1) * 128], lhsT=qtok[:, h, :], rhs=dg[:, h, :], start=True, stop=True), reads=[qtres, B[f"dg{c}"][1]], writes=[pqr], signal=(hh == 3))
                    P.op("act", lambda e, o=B[f"qd{c}"][0], pq=pq, hg=hg: e.activation(out=o[:, hg * 4:(hg + 1) * 4, :].rearrange("p a n -> p (a n)"), in_=pq[:], func=AF.Copy), reads=[pqr], writes=[B[f"qd{c}"][1]])
            for hg in range(2):
                hs = slice(hg * 4, (hg + 1) * 4)
                F = {n: f4[n].next() for n in f4}
                pR, pRr = psum.next()
                pK, pKr = psum.next()
                pQ, pQr = psum.next()
                for hh in range(4):
                    h = hg * 4 + hh
                    cs = slice(hh * 128, (hh + 1) * 128)
                    P.op("pe", lambda e, pR=pR, cs=cs, h=h, cT_=cT_: e.matmul(out=pR[:, cs], lhsT=Esel[:, h, :], rhs=cT_[:], start=True, stop=True), reads=[R_E, cTres], writes=[pRr], signal=(hh == 3))
                for hh in range(4):
                    h = hg * 4 + hh
                    cs = slice(hh * 128, (hh + 1) * 128)
                    P.op("pe", lambda e, pK=pK, cs=cs, h=h, kT=kT: e.matmul(out=pK[:, cs], lhsT=kT[:, h, :], rhs=kT[:, h, :], start=True, stop=True), reads=[kTres], writes=[pKr], signal=(hh == 3))
                for hh in range(4):
                    h = hg * 4 + hh
                    cs = slice(hh * 128, (hh + 1) * 128)
                    P.op("pe", lambda e, pQ=pQ, cs=cs, h=h, kT=kT, qT=qT: e.matmul(out=pQ[:, cs], lhsT=kT[:, h, :], rhs=qT[:, h, :], start=True, stop=True), reads=[kTres, qTres], writes=[pQr], signal=(hh == 3))
                X, Xr = F["X"]
                Ds, Dsr = F["Ds"]
                Dt, Dtr = F["Dt"]
                N_, Nr = F["N"]
                A_, Ar = F["A"]
                inv, invr = F["inv"]
                fl = lambda t: t[:].rearrange("p a n -> p (a n)")
                P.op("dve", lambda e, X=X, pR=pR, sm=sm, hs=hs: e.tensor_tensor(out=X[:], in0=pR[:].rearrange("p (a n) -> p a n", n=128), in1=bc3(sm[:, 0, hs], 128), op=ALU.subtract), reads=[pRr, smres], writes=[Xr])
                P.op("dve", lambda e, X=X, Ds=Ds, mposS=mposS: e.tensor_tensor(out=Ds[:], in0=X[:], in1=mposS.unsqueeze(1).to_broadcast([128, 4, 128]), op=ALU.add), reads=[Xr, R_c], writes=[Dsr])
                P.op("act", lambda e, Ds=Ds: e.activation(out=fl(Ds), in_=fl(Ds), func=AF.Exp, scale=-1.0), reads=[Dsr], writes=[Dsr])
                P.op("dve", lambda e, X=X, Dt=Dt, mposT=mposT: e.tensor_tensor(out=Dt[:], in0=X[:], in1=mposT.unsqueeze(1).to_broadcast([128, 4, 128]), op=ALU.subtract), reads=[Xr, R_c], writes=[Dtr])
                P.op("act", lambda e, Dt=Dt: e.activation(out=fl(Dt), in_=fl(Dt), func=AF.Exp), reads=[Dtr], writes=[Dtr])
                P.op("dve", lambda e, N_=N_, pK=pK, sm=sm, hs=hs: e.tensor_tensor(out=N_[:], in0=pK[:].rearrange("p (a n) -> p a n", n=128), in1=bc3(sm[:, 8, hs], 128), op=ALU.mult), reads=[pKr, smres], writes=[Nr])
                P.op("dve", lambda e, N_=N_, Ds=Ds: e.tensor_tensor(out=N_[:], in0=N_[:], in1=Ds[:], op=ALU.mult), reads=[Nr, Dsr], writes=[Nr])
                if "DBG" in g and t0 == tiles[0] and hg == 0:
                    P.dma("sp", g["DBG"][d, 0, :, 0:96], sm[:].rearrange("p a n -> p (a n)"), smres, reads=[smres])
                    P.dma("sp", g["DBG"][d, 1], Ds[:].rearrange("p a n -> p (a n)"), Dsr, reads=[Dsr])
                    P.dma("sp", g["DBG"][d, 2], N_[:].rearrange("p a n -> p (a n)"), Nr, reads=[Nr])
                aq, aqr = B["aqk"]
                P.op("dve", lambda e, aq=aq, pQ=pQ, Dt=Dt, hs=hs: e.tensor_tensor(out=aq[:, hs, :], in0=pQ[:].rearrange("p (a n) -> p a n", n=128), in1=Dt[:], op=ALU.mult), reads=[pQr, Dtr], writes=[aqr])
                pT, pTr = psum.next()
                for hh in range(4):
                    P.op("pe", lambda e, pT=pT, hh=hh, N_=N_: e.transpose(out=pT[:, hh * 128:(hh + 1) * 128], in_=N_[:, hh, :], identity=ident), reads=[Nr, R_c], writes=[pTr], signal=(hh == 3))
                P.op("act", lambda e, A_=A_, pT=pT: e.activation(out=fl(A_), in_=pT[:], func=AF.Copy), reads=[pTr], writes=[Ar])
                P.op("dve", lambda e, inv=inv, A_=A_: e.tensor_tensor(out=inv[:], in0=A_[:], in1=ident.unsqueeze(1).to_broadcast([128, 4, 128]), op=ALU.add), reads=[Ar, R_c], writes=[invr])
                for it in range(5):
                    pA, pAr = psum.next()
                    pB, pBr = psum.next()
                    for hh in range(4):
                        cs = slice(hh * 128, (hh + 1) * 128)
                        P.op("pe", lambda e, pA=pA, cs=cs, hh=hh, A_=A_, N_=N_: e.matmul(out=pA[:, cs], lhsT=N_[:, hh, :], rhs=A_[:, hh, :], start=True, stop=True), reads=[Ar, Nr], writes=[pAr], signal=(hh == 3))
                    for hh in range(4):
                        cs = slice(hh * 128, (hh + 1) * 128)
                        P.op("pe", lambda e, pB=pB, cs=cs, hh=hh, A_=A_, N_=N_: e.matmul(out=pB[:, cs], lhsT=A_[:, hh, :], rhs=N_[:, hh, :], start=True, stop=True), reads=[Ar, Nr], writes=[pBr], signal=(hh == 3))
                    P.op("act", lambda e, A_=A_, pA=pA: e.activation(out=fl(A_), in_=pA[:], func=AF.Copy), reads=[pAr], writes=[Ar])
                    P.op("dve", lambda e, N_=N_, pB=pB: e.tensor_copy(out=fl(N_), in_=pB[:]), reads=[pBr], writes=[Nr])
                    pU, pUr = psum.next()
                    for hh in range(4):
                        cs = slice(hh * 128, (hh + 1) * 128)
                        P.op("pe", lambda e, pU=pU, cs=cs, hh=hh, inv=inv, N_=N_: e.matmul(out=pU[:, cs], lhsT=N_[:, hh, :], rhs=inv[:, hh, :], start=True, stop=True), reads=[Nr, invr], writes=[pUr], signal=(hh == 3))
                    P.op("dve", lambda e, inv=inv, pU=pU: e.tensor_tensor(out=fl(inv), in0=fl(inv), in1=pU[:], op=ALU.add), reads=[pUr, invr], writes=[invr])
                if "DBG" in g and t0 == tiles[0] and hg == 0:
                    P.dma("sp", g["DBG"][d, 3], inv[:].rearrange("p a n -> p (a n)"), invr, reads=[invr])
                ib, ibr = invb_r.next()
                P.op("act", lambda e, ib=ib, inv=inv: e.activation(out=fl(ib), in_=fl(inv), func=AF.Copy), reads=[invr], writes=[ibr])
                pu, pur = psum.next()
                pw, pwr = psum.next()
                for hh in range(4):
                    h = hg * 4 + hh
                    cs = slice(hh * 128, (hh + 1) * 128)
                    P.op("pe", lambda e, pu=pu, cs=cs, hh=hh, h=h, ib=ib, vb=B["vb"][0]: e.matmul(out=pu[:, cs], lhsT=ib[:, hh, :], rhs=vb[:, h, :], start=True, stop=True), reads=[ibr, B["vb"][1]], writes=[pur], signal=(hh == 3))
                for hh in range(4):
                    h = hg * 4 + hh
                    cs = slice(hh * 128, (hh + 1) * 128)
                    P.op("pe", lambda e, pw=pw, cs=cs, hh=hh, h=h, ib=ib, kbg=B["kbg"][0]: e.matmul(out=pw[:, cs], lhsT=kbg[:, h, :], rhs=ib[:, hh, :], start=True, stop=True), reads=[ibr, B["kbg"][1]], writes=[pwr], signal=(hh == 3))
                for c in range(2):
                    P.op("dve", lambda e, o=U[f"um{c}"][0], pu=pu, hs=hs, c=c: e.tensor_scalar(out=o[:, hs, :].rearrange("p a n -> p (a n)"), in0=pu[:], scalar1=m_ap[c], scalar2=None, op0=ALU.mult), reads=[pur, R_c], writes=[U[f"um{c}"][1]])
                P.op("act", lambda e, o=B["wTb"][0], pw=pw, hs=hs: e.activation(out=o[:, hs, :].rearrange("p a n -> p (a n)"), in_=pw[:], func=AF.Copy), reads=[pwr], writes=[B["wTb"][1]])
            vnames = ("v0", "v1")
            for ci, c in enumerate(order):
                vc, vcr = B[vnames[ci]]
                for hg in range(2):
                    hs = slice(hg * 4, (hg + 1) * 4)
                    pS, pSr = psum.next()
                    for hh in range(4):
                        h = hg * 4 + hh
                        cs = slice(hh * 128, (hh + 1) * 128)
                        P.op("pe", lambda e, pS=pS, cs=cs, h=h, w=B["wTb"][0]: e.matmul(out=pS[:, cs], lhsT=w[:, h, :], rhs=Sb[:, h, :], start=True, stop=True), reads=[B["wTb"][1], R_Sb], writes=[pSr], signal=(hh == 3))
                    P.op("dve", lambda e, vc=vc, pS=pS, hs=hs, c=c, u=U[f"um{c}"][0]: e.scalar_tensor_tensor(out=vc[:, hs, :].rearrange("p a n -> p (a n)"), in0=pS[:], scalar=negm[:, c:c + 1], in1=u[:, hs, :].rearrange("p a n -> p (a n)"), op0=ALU.mult, op1=ALU.add), reads=[pSr, U[f"um{c}"][1], R_E], writes=[vcr])
                    for hh in range(4):
                        h = hg * 4 + hh
                        cs = slice(hh * 128, (hh + 1) * 128)
                        P.op("pe", lambda e, hg=hg, cs=cs, h=h, qd=B[f"qd{c}"][0], ci=ci: e.matmul(out=ops_[hg][:, cs], lhsT=qd[:, h, :], rhs=Sb[:, h, :], start=(ci == 0 and cs.start == 0), stop=False), reads=[B[f"qd{c}"][1], R_Sb], writes=[R_ops[hg]], signal=False)
                for hg in range(2):
                    hs = slice(hg * 4, (hg + 1) * 4)
                    pD, pDr = psum.next()
                    for hh in range(4):
                        h = hg * 4 + hh
                        cs = slice(hh * 128, (hh + 1) * 128)
                        P.op("pe", lambda e, pD=pD, cs=cs, h=h, kd=B[f"kd{c}"][0], vc=vc: e.matmul(out=pD[:, cs], lhsT=kd[:, h, :], rhs=vc[:, h, :], start=True, stop=True), reads=[B[f"kd{c}"][1], vcr], writes=[pDr], signal=(hh == 3))
                    P.op("dve", lambda e, hs=hs, sm=sm, c=c: e.tensor_tensor(out=S32[:, hs, :], in0=S32[:, hs, :], in1=bc3(sm[:, 6 + c, hs], 128), op=ALU.mult), reads=[R_S, smres], writes=[R_S])
                    P.op("dve", lambda e, hs=hs, pD=pD: e.tensor_tensor(out=S32[:, hs, :].rearrange("p a n -> p (a n)"), in0=S32[:, hs, :].rearrange("p a n -> p (a n)"), in1=pD[:], op=ALU.add), reads=[R_S, pDr], writes=[R_S])
                P.op("act", lambda e: e.activation(out=Sb[:].rearrange("p a n -> p (a n)"), in_=S32[:].rearrange("p a n -> p (a n)"), func=AF.Copy), reads=[R_S], writes=[R_Sb])
            if "DBGB" in g and t0 == tiles[0]:
                for k_, n_ in enumerate(("aqk", "wTb", "qd0", "qd1", "kd0", "kd1", "v0", "v1")):
                    P.dma("sp", g["DBGB"][d, k_], B[n_][0][:, 0:4, :].rearrange("p a n -> p (a n)"), B[n_][1], reads=[B[n_][1]])
                for k_, n_ in enumerate(("um0", "um1")):
                    P.dma("sp", g["DBGF"][d, k_], U[n_][0][:, 0:4, :].rearrange("p a n -> p (a n)"), U[n_][1], reads=[U[n_][1]])
                P.dma("sp", g["DBGF"][d, 2], S32[:, 0:4, :].rearrange("p a n -> p (a n)"), R_S, reads=[R_S])
            vf, vfr = B["vf"]
            P.op("pool", lambda e, vf=vf, a=B["v0"][0], b=B["v1"][0]: e.tensor_tensor(out=vf[:], in0=a[:], in1=b[:], op=ALU.add), reads=[B["v0"][1], B["v1"][1]], writes=[vfr])
            og, ogr = ostg.next()
            for hg in range(2):
                hs = slice(hg * 4, (hg + 1) * 4)
                for hh in range(4):
                    h = hg * 4 + hh
                    cs = slice(hh * 128, (hh + 1) * 128)
                    P.op("pe", lambda e, hg=hg, cs=cs, h=h, aq=B["aqk"][0], vf=vf: e.matmul(out=ops_[hg][:, cs], lhsT=aq[:, h, :], rhs=vf[:, h, :], start=False, stop=True), reads=[B["aqk"][1], vfr], writes=[R_ops[hg]], signal=(hh == 3))
                P.op("act", lambda e, og=og, hg=hg, hs=hs: e.activation(out=og[:, hs, :].rearrange("p a n -> p (a n)"), in_=ops_[hg][:], func=AF.Copy), reads=[R_ops[hg]], writes=[ogr])
            P.dma("sp", OACC[d, t0:t0 + 128, :, :], og[:], ogr, reads=[ogr], writes=[R_O])
    P.pop()

    P.push()
    ptb = Ring(P, 2, [128, 1024], BF16, "ptb", psum=True)
    gn = P.sb([128, 1], F32, "gn")
    R_gn = Res("gn")
    P.dma("sp", gn[:], g["gnormT"][l], R_gn, writes=[R_gn])
    oar = Ring(P, 2, [128, 8, 128], F32, "oa")
    obr = Ring(P, 2, [128, 8, 128], F32, "ob")
    sqr2 = Ring(P, 2, [128, 8, 128], F32, "sq2")
    ssr = Ring(P, 2, [128, 8], F32, "ss")
    onr = Ring(P, 2, [128, 8, 128], BF16, "on")
    zr = Ring(P, 2, [128, 8, 128], BF16, "zz")
    yr = Ring(P, 2, [128, 8, 128], BF16, "ya")
    for t0 in range(0, Tt, 128):
        if t0 < L and not need_ctx:
            continue
        oa, oares = oar.next()
        ob, obres = obr.next()
        P.dma("sp", oa[:], OACC[0, t0:t0 + 128, :, :], oares, reads=[R_O], writes=[oares])
        P.dma("sp", ob[:], OACC[1, t0:t0 + 128, :, :], obres, reads=[R_O], writes=[obres])
        z, zres = zr.next()
        P.dma("sp", z[:], ZT[:, :, t0:t0 + 128].rearrange("h p t -> p h t"), zres, reads=[R_P1], writes=[zres])
        P.op("dve", lambda e, oa=oa, ob=ob: e.tensor_tensor(out=oa[:], in0=oa[:], in1=ob[:], op=ALU.add), reads=[oares, obres], writes=[oares])
        sq, sqres = sqr2.next()
        P.op("pool", lambda e, sq=sq, oa=oa: e.tensor_tensor(out=sq[:], in0=oa[:], in1=oa[:], op=ALU.mult), reads=[oares], writes=[sqres])
        ss, ssres = ssr.next()
        P.op("dve", lambda e, ss=ss, sq=sq: e.tensor_reduce(out=ss[:], in_=sq[:], axis=mybir.AxisListType.X, op=ALU.add), reads=[sqres], writes=[ssres])
        P.op("act", lambda e, ss=ss: e.activation(out=ss[:], in_=ss[:], func=AF.Sqrt, scale=1.0 / 128.0, bias=epsln[:, 1:2]), reads=[ssres, R_c], writes=[ssres])
        P.op("dve", lambda e, ss=ss: e.reciprocal(out=ss[:], in_=ss[:]), reads=[ssres], writes=[ssres])
        on, onres = onr.next()
        P.op("dve", lambda e, on=on, oa=oa, ss=ss: e.tensor_tensor(out=on[:], in0=oa[:], in1=bc3(ss[:], 128), op=ALU.mult), reads=[oares, ssres], writes=[onres])
        pb, pbr = ptb.next()
        for h in range(8):
            P.op("pe", lambda e, pb=pb, h=h, on=on: e.transpose(out=pb[:, h * 128:(h + 1) * 128], in_=on[:, h, :], identity=identb), reads=[onres, R_c], writes=[pbr], signal=(h == 7))
        y, yres = yr.next()
        P.op("dve", lambda e, y=y, pb=pb, z=z: e.scalar_tensor_tensor(out=y[:].rearrange("p a n -> p (a n)"), in0=pb[:], scalar=gn[:, 0:1], in1=z[:].rearrange("p a n -> p (a n)"), op0=ALU.mult, op1=ALU.mult), reads=[pbr, R_gn, zres], writes=[yres])
        P.dma("sp", YAT[:, :, t0:t0 + 128].rearrange("h p t -> p h t"), y[:], yres, reads=[yres], writes=[R_Y])
    P.pop()


def build_gdn_test(T, L, need_ctx=True):
    Tt = T + L
    nc = bass.Bass("TRN2", target_bir_lowering=False)
    P = Prog(nc)
    din = lambda name, shape, dt=F32: nc.dram_tensor(name, list(shape), dt, kind="ExternalInput").ap()
    dout = lambda name, shape, dt=F32: nc.dram_tensor(name, list(shape), dt, kind="ExternalOutput").ap()
    dscr = lambda name, shape, dt=F32: nc.dram_tensor(name, list(shape), dt).ap()
    env = dict(L=L, T=T, Tt=Tt, need_ctx=need_ctx)
    env["QKVT"] = din("QKVT", [24, 128, Tt], BF16)
    env["ZT"] = din("ZT", [8, 128, Tt], BF16)
    env["GG"] = din("GG", [Tt, 16])
    env["BETA"] = din("BETA", [Tt, 16])
    env["convT"] = din("convT", [1, 128, 24, 5])
    env["gnormT"] = din("gnormT", [1, 128, 1])
    consts = din("consts", [12, 128, 128])
    env["QNT"] = dout("QNT", [8, 128, Tt], BF16)
    env["KNT"] = dout("KNT", [8, 128, Tt], BF16)
    env["QTOK"] = dscr("QTOK", [Tt, 8, 128], BF16)
    env["KTOK"] = dscr("KTOK", [Tt, 8, 128], BF16)
    env["VTOK"] = dout("VTOK", [Tt, 8, 128], BF16)
    env["OACC"] = dout("OACC", [2, Tt, 8, 128])
    env["YAT"] = dout("YAT", [8, 128, Tt], BF16)
    env["DBG"] = dout("DBG", [2, 4, 128, 512])
    env["DBGB"] = dout("DBGB", [2, 8, 128, 512], BF16)
    env["DBGF"] = dout("DBGF", [2, 3, 128, 512])
    R_c = Res("c")
    cst = P.sb([128, 12, 128], F32, "cst")
    cstb = P.sb([128, 12, 128], BF16, "cstb")
    P.dma("sp", cst[:], consts.rearrange("c p m -> p c m"), R_c, writes=[R_c])
    P.op("dve", lambda e: e.tensor_copy(out=cstb[:], in_=cst[:]), reads=[R_c], writes=[R_c])
    epsln = P.sb([128, 2], F32, "epsln")
    P.op("dve", lambda e: e.memset(epsln[:, 0:1], LN_EPS), writes=[R_c])
    P.op("dve", lambda e: e.memset(epsln[:, 1:2], RMS_EPS), writes=[R_c])
    env.update(cst=cst, cstb=cstb, epsln=epsln, R_c=R_c, R_P1=Res("p1"), R_Y=Res("y"))
    build_gdn(P, nc, env, 0)
    P.finish()
    return nc


def build_merge_ffn(P, nc, env, l):
    g = env
    L, T, Tt = g["L"], g["T"], g["Tt"]
    need_ctx = g["need_ctx"]
    R_Y, R_c, R_mod, R_X, R_W, R_P1 = g["R_Y"], g["R_c"], g["R_mod"], g["R_X"], g["R_W"], g["R_P1"]
    modT, ident = g["modT"], g["ident"]
    X = g["X"]
    layer_norm_tile = g["layer_norm_tile"]
    load_wblk = g["load_wblk"]
    P.push()
    psum = Ring(P, 8, [128, 512], F32, "ps", psum=True)
    g["psum_box"][0] = psum
    wring = Ring(P, 2, [128, KC, 512], BF16, "w5")
    YT = [P.sb([128, 8, 256], BF16, "yt") for _ in range(3)]
    R_yt = Res("yt")
    gring = Ring(P, 6, [128, 256], BF16, "g5")
    mT = P.sb([128, KC, 256], BF16, "mT")
    R_m = Res("m")
    h2T = P.sb([128, KC, 256], BF16, "h2T")
    R_h2 = Res("h2")
    r = P.sb([128, 2, D], F32, "r")
    R_r = [Res("r0"), Res("r1")]
    f1T = P.sb([128, 64, 256], BF16, "f1T")
    R_f1 = Res("f1")
    acc4 = P.sb([128, 4, 256], F32, "acc4")
    R_acc = [Res("a") for _ in range(4)]
    lnp = [P.sb([128, D], F32, "lnp") for _ in range(4)]
    for t, src in zip(lnp, (g["lnm_g"], g["lnm_b"], g["lnf_g"], g["lnf_b"])):
        P.dma("sp", t[:], src[l], R_mod, writes=[R_mod])
    mbc = [P.sb([128, D], F32, "mbc") for _ in range(2)]
    R_mb = Res("mb")
    tmpr_ = Ring(P, 2, [128, 512], F32, "t5")
    small = Ring(P, 2, [128, 32], F32, "sm5")
    cur_b = None
    segs = [(0, 256, True)] if L >= 256 else []
    segs = [(i * 256, 256, True) for i in range(L // 256)] + [(L + i * 256, 256, False) for i in range(T // 256)]
    for (s0, W, isctx) in segs:
        if isctx and not need_ctx:
            continue
        b = 1 if isctx else 0
        nt = W // 128
        if cur_b != b:
            for which in range(2):
                P.dma("sp", mbc[which][:], g["MODBC"][which, b], R_mb, reads=[g["R_mbc"]], writes=[R_mb])
            cur_b = b
        for bi, src in enumerate((g["YAT"], g["YBT"], g["YCT"])):
            P.dma("sp", YT[bi][:, :, :W], src[:, :, s0:s0 + W].rearrange("c p t -> p c t"), R_yt, reads=[R_Y], writes=[R_yt])
        for ti in range(nt):
            P.dma("sp", r[:, ti, :], X[s0 + ti * 128:s0 + (ti + 1) * 128, :], R_r[ti], reads=[R_X], writes=[R_r[ti]])
        wsrc = (g["wb_a"][l], g["wb_b"][l], g["wb_c"][l])
        for mb in range(4):
            for bi in range(3):
                wt, wres = load_wblk(wring, wsrc[bi], 0, 8, mb * 512)
                for j in range(4):
                    mc = mb * 4 + j
                    gt, gres = gring.next()
                    P.dma("sp", gt[:, :W], g["GT"][bi * 16 + mc, :, s0:s0 + W], gres, reads=[R_P1], writes=[gres])
                    pt, ptr = psum.next()
                    for kc in range(8):
                        P.op("pe", lambda e, pt=pt, wt=wt, kc=kc, j=j, bi=bi, W=W: e.matmul(out=pt[:, :W], lhsT=wt[:, kc, j * 128:(j + 1) * 128], rhs=YT[bi][:, kc, :W], start=(kc == 0), stop=(kc == 7)),
                             reads=[wres, R_yt], writes=[ptr], signal=(kc == 7))
                    if bi == 0:
                        P.op("dve", lambda e, pt=pt, gt=gt, j=j, W=W: e.tensor_tensor(out=acc4[:, j, :W], in0=pt[:, :W], in1=gt[:, :W], op=ALU.mult), reads=[ptr, gres], writes=[R_acc[j]])
                    else:
                        tmp, tres = tmpr_.next()
                        P.op("dve", lambda e, pt=pt, gt=gt, tmp=tmp, W=W: e.tensor_tensor(out=tmp[:, :W], in0=pt[:, :W], in1=gt[:, :W], op=ALU.mult), reads=[ptr, gres], writes=[tres])
                        if bi == 1:
                            P.op("pool", lambda e, tmp=tmp, j=j, W=W: e.tensor_tensor(out=acc4[:, j, :W], in0=acc4[:, j, :W], in1=tmp[:, :W], op=ALU.add), reads=[tres, R_acc[j]], writes=[R_acc[j]])
                        else:
                            P.op("pool", lambda e, tmp=tmp, j=j, mc=mc, W=W: e.tensor_tensor(out=mT[:, mc, :W], in0=acc4[:, j, :W], in1=tmp[:, :W], op=ALU.add), reads=[tres, R_acc[j]], writes=[R_m])
        for nb in range(4):
            wt, wres = load_wblk(wring, g["wb_o"][l], 0, KC, nb * 512)
            for ti in range(nt):
                pt, ptr = psum.next()
                for kc in range(KC):
                    P.op("pe", lambda e, pt=pt, wt=wt, kc=kc, ti=ti: e.matmul(out=pt[:], lhsT=mT[:, kc, ti * 128:(ti + 1) * 128], rhs=wt[:, kc, :], start=(kc == 0), stop=(kc == KC - 1)),
                         reads=[wres, R_m], writes=[ptr], signal=(kc == KC - 1))
                tmp, tres = tmpr_.next()
                P.op("dve", lambda e, pt=pt, tmp=tmp, nb=nb: e.tensor_tensor(out=tmp[:], in0=pt[:], in1=mbc[0][:, nb * 512:(nb + 1) * 512], op=ALU.mult), reads=[ptr, R_mb], writes=[tres])
                P.op("dve", lambda e, tmp=tmp, ti=ti, nb=nb: e.scalar_tensor_tensor(out=r[:, ti, nb * 512:(nb + 1) * 512], in0=r[:, ti, nb * 512:(nb + 1) * 512], scalar=ALPHA, in1=tmp[:], op0=ALU.mult, op1=ALU.add), reads=[tres, R_r[ti]], writes=[R_r[ti]])
        for ti in range(nt):
            sm, smr = small.next()
            layer_norm_tile(r[:, ti, :], lnp[0][:], lnp[1][:], r[:, ti, :], R_r[ti], sm, smr)
            g["transpose_mod"](r[:, ti, :], R_r[ti], h2T, R_h2, ti, 3, 4, b)
        for blk in range(DFF // 512):
            wt, wres = load_wblk(wring, g["wb_f1"][l], 0, KC, blk * 512)
            for j in range(4):
                fc = blk * 4 + j
                pt, ptr = psum.next()
                for kc in range(KC):
                    P.op("pe", lambda e, pt=pt, wt=wt, kc=kc, j=j, W=W: e.matmul(out=pt[:, :W], lhsT=wt[:, kc, j * 128:(j + 1) * 128], rhs=h2T[:, kc, :W], start=(kc == 0), stop=(kc == KC - 1)),
                         reads=[wres, R_h2], writes=[ptr], signal=(kc == KC - 1))
                tmp, tres = tmpr_.next()
                P.op("act", lambda e, pt=pt, tmp=tmp, W=W: e.activation(out=tmp[:, :W], in_=pt[:, :W], func=AF.Relu), reads=[ptr], writes=[tres])
                P.op("pool", lambda e, tmp=tmp, fc=fc, W=W: e.tensor_tensor(out=f1T[:, fc, :W], in0=tmp[:, :W], in1=tmp[:, :W], op=ALU.mult), reads=[tres], writes=[R_f1])
        for nb in range(4):
            pts = [psum.next() for _ in range(nt)]
            for kp in range(4):
                wt, wres = load_wblk(wring, g["wb_f2"][l], kp * 16, 16, nb * 512)
                for ti in range(nt):
                    pt, ptr = pts[ti]
                    for kc in range(16):
                        P.op("pe", lambda e, pt=pt, wt=wt, kc=kc, kp=kp, ti=ti: e.matmul(out=pt[:], lhsT=f1T[:, kp * 16 + kc, ti * 128:(ti + 1) * 128], rhs=wt[:, kc, :], start=(kp == 0 and kc == 0), stop=(kp == 3 and kc == 15)),
                             reads=[wres, R_f1], writes=[ptr], signal=(kc == 15))
            for ti in range(nt):
                pt, ptr = pts[ti]
                tmp, tres = tmpr_.next()
                P.op("dve", lambda e, pt=pt, tmp=tmp, nb=nb: e.tensor_tensor(out=tmp[:], in0=pt[:], in1=mbc[1][:, nb * 512:(nb + 1) * 512], op=ALU.mult), reads=[ptr, R_mb], writes=[tres])
                P.op("dve", lambda e, tmp=tmp, ti=ti, nb=nb: e.scalar_tensor_tensor(out=r[:, ti, nb * 512:(nb + 1) * 512], in0=r[:, ti, nb * 512:(nb + 1) * 512], scalar=ALPHA, in1=tmp[:], op0=ALU.mult, op1=ALU.add), reads=[tres, R_r[ti]], writes=[R_r[ti]])
        for ti in range(nt):
            sm, smr = small.next()
            layer_norm_tile(r[:, ti, :], lnp[2][:], lnp[3][:], r[:, ti, :], R_r[ti], sm, smr)
            P.dma("sp", X[s0 + ti * 128:s0 + (ti + 1) * 128, :], r[:, ti, :], R_r[ti], reads=[R_r[ti]], writes=[R_X])
    P.pop()


def _consts(T, L):
    c = np.zeros((12, 128, 128), np.float32)
    idx = np.arange(128)
    c[0] = np.eye(128)
    sw = np.where((idx % 64) < 32, idx + 32, idx - 32)
    c[1][sw, idx] = 1.0
    c[2] = 1.0
    same = (idx[:, None] // 64) == (idx[None, :] // 64)
    jj, ii = idx[:, None], idx[None, :]
    c[3] = (same & (jj <= ii))
    c[4] = (same & (jj >= ii))
    c[5] = same
    c[6] = (jj < 64) * np.ones((1, 128))
    c[7] = (jj >= 64) * np.ones((1, 128))
    BIG = 30000.0
    i2, j2 = idx[:, None], idx[None, :]
    c[8] = np.where(same & (j2 < i2), 0.0, BIG)
    c[9] = np.where(same & (j2 > i2), 0.0, BIG)
    c[10] = np.where(same & (jj <= ii), 0.0, BIG)
    c[11] = np.where(same & (jj >= ii), 0.0, BIG)
    es = np.zeros((8, 8, 128), np.float32)
    for h in range(8):
        es[h, h, :] = 1.0
    sm = np.zeros((2, 128, 128), np.float32)
    kk, qq = idx[:, None], idx[None, :]
    sm[0] = (kk >= qq)
    sm[1] = (kk <= qq)
    t = np.arange(T)
    row = (t // GW).astype(np.float32)
    col = (t % GW).astype(np.float32)
    freq = np.power(np.float32(10000.0), -np.arange(16, dtype=np.float32) / np.float32(16)).astype(np.float32)
    ang = np.concatenate([row[:, None] * freq, col[:, None] * freq], axis=-1).astype(np.float32)
    cosv, sinv = np.cos(ang).astype(np.float32), np.sin(ang).astype(np.float32)
    cosT = np.ones((128, L + T), np.float32)
    sinT = np.zeros((128, L + T), np.float32)
    for m in range(128):
        f = m % 32
        sgn = -1.0 if (m % 64) < 32 else 1.0
        cosT[m, L:] = cosv[:, f]
        sinT[m, L:] = sgn * sinv[:, f]
    return c, es, sm, cosT, sinT


def prep_shared(inp, T, L, NL):
    f = lambda a: np.ascontiguousarray(np.asarray(a, dtype=np.float32))
    bc = lambda a: np.ascontiguousarray(np.broadcast_to(np.asarray(a, np.float32)[:, None, :], (a.shape[0], 128, a.shape[-1])))
    perm = perm_in_cols()
    w_in = np.asarray(inp["w_in"], np.float32)[:NL]
    b_in = np.asarray(inp["b_in"], np.float32)[:NL]
    wp = np.zeros((NL, D, NIN), np.float32)
    bp = np.zeros((NL, NIN), np.float32)
    ok = perm >= 0
    wp[:, :, ok] = w_in[:, :, perm[ok]]
    bp[:, ok] = b_in[:, perm[ok]]
    c, es, sm, cosT, sinT = _consts(T, L)
    sh = {
        "w_ada": f(inp["w_ada"][:NL]),
        "b_adaT": f(np.asarray(inp["b_ada"])[:NL].reshape(NL, 96, 128).transpose(0, 2, 1)),
        "b_adabc": bc(np.asarray(inp["b_ada"])[:NL]),
        "w_in": wp,
        "b_inT": f(bp[:, :NFM * 128].reshape(NL, NFM, 128).transpose(0, 2, 1)),
        "b_intok": bc(bp[:, NFM * 128:]),
        "convT": f(np.asarray(inp["gdn_conv"])[:NL].reshape(NL, 5, 24, 128).transpose(0, 3, 2, 1)),
        "alog": bc(np.asarray(inp["gdn_a_log"])[:NL].reshape(NL, 16)),
        "dtb": bc(np.asarray(inp["gdn_dt_bias"])[:NL].reshape(NL, 16)),
        "gnormT": f(np.asarray(inp["gdn_norm"])[:NL].reshape(NL, 128, 1)),
        "sinks": bc(np.asarray(inp["swa_sinks"])[:NL]),
        "sgu_g": bc(np.asarray(inp["sgu_ln_g"])[:NL]),
        "sgu_bb": bc(np.asarray(inp["sgu_ln_b"])[:NL]),
        "sgu_wT": f(np.asarray(inp["sgu_w"])[:NL].transpose(0, 1, 3, 2)),
        "sgu_bias": bc(np.asarray(inp["sgu_b"])[:NL].reshape(NL, 1024)),
        "w_a": f(inp["w_branch_a"][:NL]), "w_b": f(inp["w_branch_b"][:NL]), "w_c": f(inp["w_branch_c"][:NL]),
        "w_o": f(inp["w_out"][:NL]),
        "lnm_g": bc(np.asarray(inp["ln_mix_g"])[:NL]), "lnm_b": bc(np.asarray(inp["ln_mix_b"])[:NL]),
        "w_f1": f(inp["w_ff1"][:NL]), "w_f2": f(inp["w_ff2"][:NL]),
        "lnf_g": bc(np.asarray(inp["ln_ff_g"])[:NL]), "lnf_b": bc(np.asarray(inp["ln_ff_b"])[:NL]),
        "cosT": cosT, "sinT": sinT, "consts": c, "esel": es, "swamask": sm,
    }
    return sh


def prep_core(inp, b):
    x = np.asarray(inp["x"], np.float32)[b]
    ctx = np.asarray(inp["ctx"], np.float32)[b]
    xin = np.ascontiguousarray(np.concatenate([ctx, x], axis=0))
    cT = np.stack([np.asarray(inp["c"], np.float32)[b].reshape(KC, 128).T,
                   np.asarray(inp["c_ctx"], np.float32).reshape(KC, 128).T], axis=-1)
    return {"xin": xin, "cT": np.ascontiguousarray(cT)}


def kernel(**inputs):
    B, T, _ = inputs["x"].shape
    L = inputs["ctx"].shape[1]
    nc = build(T, L, DEPTH)
    sh = prep_shared(inputs, T, L, DEPTH)
    in_maps = []
    for core in range(B):
        m = dict(sh)
        m.update(prep_core(inputs, core))
        in_maps.append(m)
    res = run_bass_kernel_spmd(nc, in_maps, core_ids=list(range(B)))
    return np.stack([res.results[b]["out"] for b in range(B)], axis=0).astype(np.float32)
```

```python
import contextlib
import math
import numpy as np
import concourse.bass as bass
import concourse.mybir as mybir
from concourse.bass_utils import run_bass_kernel_spmd

F32 = mybir.dt.float32
BF16 = mybir.dt.bfloat16
AF = mybir.ActivationFunctionType
ALU = mybir.AluOpType

ENGS = ("pe", "act", "dve", "pool", "sp")
ENGMAP = {"pe": "tensor", "act": "scalar", "dve": "vector", "pool": "gpsimd", "sp": "sync"}

D = 2048
KC = 16
DFF = 8192
NMOD = 6
DEPTH = 2
ALPHA = (2 * DEPTH) ** 0.25
LN_EPS = 1e-5
RMS_EPS = 1e-6
GW = 64
NFM = 100
NIN = NFM * 128 + 1536
C_QKV, C_Z, C_SQ, C_SK, C_U, C_G = 0, 24, 32, 40, 44, 52


class Res:
    __slots__ = ("w", "r", "dsem", "dcnt", "name", "scoped")
    registry = []

    def __init__(self, name=""):
        Res.registry.append(self)
        self.w = None
        self.r = {}
        self.dsem = None
        self.dcnt = 0
        self.name = name


class Prog:
    def __init__(self, nc):
        self.nc = nc
        self.stack = contextlib.ExitStack()
        self.scopes = []
        self.streams = {e: [] for e in ENGS}
        self.esem = {}
        self.cnt = {e: 0 for e in ENGS}
        self.seen = {e: {} for e in ENGS}
        self.nsem = 0
        self.ntile = 0
        self.dres = []
        self.pool = []
        self.marks = []
        Res.registry = []
        for e in ENGS:
            self.esem[e] = self.new_sem("e_" + e)

    def new_sem(self, name):
        self.nsem += 1
        return self.stack.enter_context(self.nc.semaphore(f"{name}_{self.nsem}"))

    def push(self):
        self.scopes.append(contextlib.ExitStack())
        self.marks.append(len(Res.registry))

    def pop(self):
        self.barrier()
        mark = self.marks.pop()
        dead = Res.registry[mark:]
        del Res.registry[mark:]
        deadset = set(id(r) for r in dead)
        for r in dead:
            if r.dsem is not None:
                self.pool.append((r.dsem, r.dcnt))
        self.dres = [r for r in self.dres if id(r) not in deadset]
        self.scopes.pop().close()

    def _st(self):
        return self.scopes[-1] if self.scopes else self.stack

    def sb(self, shape, dt, name="t"):
        self.ntile += 1
        return self._st().enter_context(self.nc.sbuf_tensor(f"{name}_{self.ntile}", list(shape), dt))

    def ps(self, shape, dt=F32, name="p"):
        self.ntile += 1
        return self._st().enter_context(self.nc.psum_tensor(f"{name}_{self.ntile}", list(shape), dt))

    def _waits(self, eng, reads, writes):
        need = {}
        for r in reads:
            if r.w is not None:
                s, v = r.w
                if need.get(s, 0) < v:
                    need[s] = v
        for w in writes:
            if w.w is not None:
                s, v = w.w
                if need.get(s, 0) < v:
                    need[s] = v
            for s, v in w.r.items():
                if need.get(s, 0) < v:
                    need[s] = v
        self._emit_waits(eng, need)

    def _emit_waits(self, eng, need):
        seen = self.seen[eng]
        st = self.streams[eng]
        own = self.esem[eng]
        for s, v in need.items():
            if s is own and v > self.cnt[eng]:
                continue
            if seen.get(s, 0) < v:
                seen[s] = v
                st.append(("w", s, v))

    def _record(self, ev, reads, writes):
        s, v = ev
        for r in reads:
            if r.r.get(s, 0) < v:
                r.r[s] = v
        for w in writes:
            w.w = ev
            w.r = {}

    def op(self, eng, fn, reads=(), writes=(), signal=True):
        self._waits(eng, reads, writes)
        if signal:
            self.cnt[eng] += 1
            ev = (self.esem[eng], self.cnt[eng])
            self.streams[eng].append(("i", fn, self.esem[eng], 1))
        else:
            ev = (self.esem[eng], self.cnt[eng] + 1)
            self.streams[eng].append(("i", fn, None, 0))
        self._record(ev, reads, writes)

    def dma(self, q, out, in_, semres, reads=(), writes=()):
        if q == "sp" and str(out.space) == "DRAM" and str(in_.space) != "DRAM":
            q = "act"
        self._waits(q, reads, writes)
        if semres.dsem is None:
            if self.pool:
                semres.dsem, semres.dcnt = self.pool.pop()
            else:
                semres.dsem = self.new_sem("d")
            self.dres.append(semres)
        semres.dcnt += 16
        ev = (semres.dsem, semres.dcnt)
        self.streams[q].append(("i", (lambda e, o=out, i=in_: e.dma_start(out=o, in_=i)), semres.dsem, 16))
        self._record(ev, reads, writes)

    def barrier(self):
        need = {self.esem[e]: self.cnt[e] for e in ENGS if self.cnt[e] > 0}
        for r in self.dres:
            need[r.dsem] = r.dcnt
        for e in ENGS:
            self._emit_waits(e, dict(need))

    def finish(self):
        self.barrier()
        nc = self.nc
        with nc.Block() as block:
            for e in ENGS:
                items = self.streams[e]

                def body(eng, items=items):
                    for it in items:
                        if it[0] == "w":
                            eng.wait_ge(it[1], it[2])
                        else:
                            ins = it[1](eng)
                            if it[2] is not None:
                                ins.then_inc(it[2], it[3])
                getattr(block, ENGMAP[e])(body)
        while self.scopes:
            self.scopes.pop().close()
        self.stack.close()


class Ring:
    def __init__(self, P, n, shape, dt, name, psum=False):
        self.t = [(P.ps(shape, dt, name) if psum else P.sb(shape, dt, name)) for _ in range(n)]
        self.r = [Res(name) for _ in range(n)]
        self.i = 0
        self.n = n

    def next(self):
        k = self.i % self.n
        self.i += 1
        return self.t[k], self.r[k]


def perm_in_cols():
    p = []
    p += list(range(0, 3072))
    p += list(range(3072, 4096))
    p += list(range(4128, 5152))
    for g in range(4):
        p += list(range(5152 + g * 64, 5152 + (g + 1) * 64)) * 2
    p += list(range(5664, 6688))
    p += list(range(7712, 13856))
    assert len(p) == NFM * 128
    p += list(range(5408, 5664))
    p += list(range(4096, 4128))
    p += [-1] * 224
    p += list(range(6688, 7712))
    assert len(p) == NIN
    return np.array(p)


import os
def build(T, L, nlayers=DEPTH, debug=None):
    try:
        return _build(T, L, nlayers, debug)
    except _Stop as e:
        return e.nc


_LAST = {}


class _Stop(Exception):
    pass


def _build(T, L, nlayers=DEPTH, debug=None):
    debug = debug or set()
    Tt = T + L
    nc = bass.Bass("TRN2", target_bir_lowering=False)
    P = Prog(nc)
    _LAST["P"] = P

    def din(name, shape, dt=F32):
        return nc.dram_tensor(name, list(shape), dt, kind="ExternalInput").ap()

    def dscr(name, shape, dt=F32):
        kind = "ExternalOutput" if name in debug else "Internal"
        return nc.dram_tensor(name, list(shape), dt, kind=kind).ap()

    NL = nlayers
    xin = din("xin", [Tt, D])
    cT = din("cT", [128, KC, 2])
    w_ada = din("w_ada", [NL, D, NMOD * D])
    b_adaT = din("b_adaT", [NL, 128, 96])
    b_adabc = din("b_adabc", [NL, 128, NMOD * D])
    w_in = din("w_in", [NL, D, NIN])
    b_inT = din("b_inT", [NL, 128, NFM])
    b_intok = din("b_intok", [NL, 128, 1536])
    convT = din("convT", [NL, 128, 24, 5])
    alog = din("alog", [NL, 128, 16])
    dtb = din("dtb", [NL, 128, 16])
    gnormT = din("gnormT", [NL, 128, 1])
    sinks = din("sinks", [NL, 128, 16])
    sgu_g = din("sgu_g", [NL, 128, 1024])
    sgu_bb = din("sgu_bb", [NL, 128, 1024])
    sgu_wT = din("sgu_wT", [NL, 8, 128, 128])
    sgu_bias = din("sgu_bias", [NL, 128, 1024])
    w_a = din("w_a", [NL, 1024, D])
    w_b = din("w_b", [NL, 1024, D])
    w_c = din("w_c", [NL, 1024, D])
    w_o = din("w_o", [NL, D, D])
    lnm_g = din("lnm_g", [NL, 128, D])
    lnm_b = din("lnm_b", [NL, 128, D])
    w_f1 = din("w_f1", [NL, D, DFF])
    w_f2 = din("w_f2", [NL, DFF, D])
    lnf_g = din("lnf_g", [NL, 128, D])
    lnf_b = din("lnf_b", [NL, 128, D])
    cosT = din("cosT", [128, Tt])
    sinT = din("sinT", [128, Tt])
    consts = din("consts", [12, 128, 128])
    esel = din("esel", [8, 8, 128])
    swamask = din("swamask", [2, 128, 128])
    out = nc.dram_tensor("out", [T, D], F32, kind="ExternalOutput").ap()

    X = dscr("X", [Tt, D])
    wb_in = [dscr(f"wb_in{l}", [D, NIN], BF16) for l in range(NL)]
    wb_a = [dscr(f"wb_a{l}", [1024, D], BF16) for l in range(NL)]
    wb_b = [dscr(f"wb_b{l}", [1024, D], BF16) for l in range(NL)]
    wb_c = [dscr(f"wb_c{l}", [1024, D], BF16) for l in range(NL)]
    wb_o = [dscr(f"wb_o{l}", [D, D], BF16) for l in range(NL)]
    wb_f1 = [dscr(f"wb_f1{l}", [D, DFF], BF16) for l in range(NL)]
    wb_f2 = [dscr(f"wb_f2{l}", [DFF, D], BF16) for l in range(NL)]
    QKVT = dscr("QKVT", [24, 128, Tt], BF16)
    ZT = dscr("ZT", [8, 128, Tt], BF16)
    SQT = dscr("SQT", [8, 128, Tt], BF16)
    SKT = dscr("SKT", [4, 128, Tt], BF16)
    SV = dscr("SV", [Tt, 256], BF16)
    UT = dscr("UT", [8, 128, Tt], BF16)
    SGV = dscr("SGV", [Tt, 1024], BF16)
    GT = dscr("GT", [48, 128, Tt], BF16)
    GG = dscr("GG", [Tt, 16])
    BETA = dscr("BETA", [Tt, 16])
    QNT = dscr("QNT", [8, 128, Tt], BF16)
    KNT = dscr("KNT", [8, 128, Tt], BF16)
    QTOK = dscr("QTOK", [Tt, 8, 128], BF16)
    KTOK = dscr("KTOK", [Tt, 8, 128], BF16)
    VTOK = dscr("VTOK", [Tt, 8, 128], BF16)
    OACC = dscr("OACC", [2, Tt, 8, 128])
    YAT = dscr("YAT", [8, 128, Tt], BF16)
    YBT = dscr("YBT", [8, 128, Tt], BF16)
    YCT = dscr("YCT", [8, 128, Tt], BF16)

    R_X = Res("X")
    R_W = Res("W")
    R_P1 = Res("P1")
    R_F = Res("F")
    R_O = Res("O")
    R_Y = Res("Y")

    segs = [(0, L, True)] + [(L + i * 512, min(512, T - i * 512), False) for i in range((T + 511) // 512)]

    cst = P.sb([128, 12, 128], F32, "cst")
    cstb = P.sb([128, 12, 128], BF16, "cstb")
    R_c = Res("c")
    P.dma("sp", cst[:], consts.rearrange("c p m -> p c m"), R_c, writes=[R_c])
    P.op("dve", lambda e: e.tensor_copy(out=cstb[:], in_=cst[:]), reads=[R_c], writes=[R_c])
    ident = cst[:, 0, :]
    identb = cstb[:, 0, :]
    permb = cstb[:, 1, :]
    onesb = cstb[:, 2, :]
    esl = P.sb([8, 8, 128], F32, "esel")
    P.dma("sp", esl[:], esel.rearrange("h r m -> r h m"), R_c, writes=[R_c])
    smk = P.sb([128, 2, 128], F32, "smk")
    P.dma("sp", smk[:], swamask.rearrange("c p m -> p c m"), R_c, writes=[R_c])
    epsln = P.sb([128, 2], F32, "epsln")
    P.op("dve", lambda e: e.memset(epsln[:, 0:1], LN_EPS), writes=[R_c])
    P.op("dve", lambda e: e.memset(epsln[:, 1:2], RMS_EPS), writes=[R_c])
    modT = P.sb([128, 96, 2], F32, "modT")
    R_mod = Res("mod")
    psum_box = [None]

    P.push()
    xr = Ring(P, 2, [128, D], F32, "xc")
    for t0 in range(0, Tt, 128):
        xt, xres = xr.next()
        P.dma("sp", xt[:], xin[t0:t0 + 128, :], xres, writes=[xres])
        P.dma("sp", X[t0:t0 + 128, :], xt[:], xres, reads=[xres], writes=[R_X])
    P.pop()

    def cast_weight(src, dst, K, M):
        wr, wo = cast_rings
        n = 0
        for kc in range(K // 128):
            for m0 in range(0, M, 2048):
                mw = min(2048, M - m0)
                a, ar = wr.next()
                b, br = wo.next()
                P.dma("sp", a[:, :mw], src[kc * 128:(kc + 1) * 128, m0:m0 + mw], ar, writes=[ar])
                eng = ("dve", "act", "pool")[n % 3]
                if eng == "act":
                    P.op("act", lambda e, a=a, b=b, mw=mw: e.activation(out=b[:, :mw], in_=a[:, :mw], func=AF.Copy), reads=[ar], writes=[br])
                else:
                    P.op(eng, lambda e, a=a, b=b, mw=mw: e.tensor_copy(out=b[:, :mw], in_=a[:, :mw]), reads=[ar], writes=[br])
                P.dma("sp", dst[kc * 128:(kc + 1) * 128, m0:m0 + mw], b[:, :mw], br, reads=[br], writes=[R_W])
                n += 1

    P.push()
    cast_rings = (Ring(P, 2, [128, 2048], F32, "wc"), Ring(P, 2, [128, 2048], BF16, "wo"))
    for l in range(NL):
        cast_weight(w_in[l], wb_in[l], D, NIN)
        cast_weight(w_a[l], wb_a[l], 1024, D)
        cast_weight(w_b[l], wb_b[l], 1024, D)
        cast_weight(w_c[l], wb_c[l], 1024, D)
        cast_weight(w_o[l], wb_o[l], D, D)
        cast_weight(w_f1[l], wb_f1[l], D, DFF)
        cast_weight(w_f2[l], wb_f2[l], DFF, D)
    P.pop()

    def stop_if(tag):
        if os.environ.get("KSTOP") == tag:
            P.finish()
            ex = _Stop()
            ex.nc = nc
            raise ex

    stop_if("W")
    def gelu_tanh(src_ps, src_res, bias_ap, out_ap, out_res, W, tmp, tmpr):
        a = tmp[:, 0, :W]
        b = tmp[:, 1, :W]
        if bias_ap is not None:
            P.op("act", lambda e: e.activation(out=a, in_=src_ps, func=AF.Identity, bias=bias_ap), reads=[R_mod, src_res], writes=[tmpr])
        else:
            P.op("act", lambda e: e.activation(out=a, in_=src_ps, func=AF.Copy), reads=[src_res], writes=[tmpr])
        P.op("dve", lambda e: e.tensor_tensor(out=b, in0=a, in1=a, op=ALU.mult), reads=[tmpr], writes=[tmpr])
        P.op("dve", lambda e: e.tensor_scalar(out=b, in0=b, scalar1=0.044715, scalar2=1.0, op0=ALU.mult, op1=ALU.add), reads=[tmpr], writes=[tmpr])
        P.op("dve", lambda e: e.tensor_tensor(out=b, in0=b, in1=a, op=ALU.mult), reads=[tmpr], writes=[tmpr])
        P.op("act", lambda e: e.activation(out=b, in_=b, func=AF.Sigmoid, scale=1.5957691216057308), reads=[tmpr], writes=[tmpr])
        P.op("dve", lambda e: e.tensor_tensor(out=out_ap, in0=b, in1=a, op=ALU.mult), reads=[tmpr], writes=[out_res])

    def layer_norm_tile(r_ap, g_ap, b_ap, out_ap, rres, small, smallr, ncols=D):
        nchunk = ncols // 512
        st = small[:, 0:nchunk * 6]
        mv = small[:, 24:26]
        rs = small[:, 26:27]
        for c in range(nchunk):
            P.op("dve", lambda e, c=c: e.bn_stats(out=small[:, c * 6:(c + 1) * 6], in_=r_ap[:, c * 512:(c + 1) * 512]), reads=[rres], writes=[smallr])
        P.op("dve", lambda e: e.bn_aggr(out=mv, in_=st), reads=[smallr], writes=[smallr])
        P.op("act", lambda e: e.activation(out=rs, in_=small[:, 25:26], func=AF.Sqrt, bias=epsln[:, 0:1]), reads=[smallr, R_c], writes=[smallr])
        P.op("dve", lambda e: e.reciprocal(out=rs, in_=rs), reads=[smallr], writes=[smallr])
        P.op("dve", lambda e: e.tensor_scalar(out=out_ap, in0=r_ap, scalar1=small[:, 24:25], scalar2=rs, op0=ALU.subtract, op1=ALU.mult), reads=[smallr, rres], writes=[rres])
        P.op("pool", lambda e: e.tensor_tensor(out=out_ap, in0=out_ap, in1=g_ap, op=ALU.mult), reads=[rres, R_mod], writes=[rres])
        P.op("pool", lambda e: e.tensor_tensor(out=out_ap, in0=out_ap, in1=b_ap, op=ALU.add), reads=[rres, R_mod], writes=[rres])

    for l in range(NL):
        need_ctx = l < NL - 1
        P.push()
        psum = Ring(P, 8, [128, 512], F32, "ps", psum=True)
        psum_box[0] = psum
        csl = P.sb([128, KC, 2], F32, "csl")
        cbc = P.sb([128, KC, 2, 128], F32, "cbc")
        R_cs = Res("cs")
        P.dma("sp", csl[:], cT[:, :, :], R_cs, writes=[R_cs])
        P.op("act", lambda e: e.activation(out=csl[:], in_=csl[:], func=AF.Silu), reads=[R_cs], writes=[R_cs])
        for b in range(2):
            P.op("dve", lambda e, b=b: e.tensor_copy(out=cbc[:, :, b, :], in_=csl[:, :, b:b + 1].to_broadcast([128, KC, 128])), reads=[R_cs], writes=[R_cs])
        MODBC = dscr(f"MODBC{l}", [2, 2, 128, D])
        R_mbc = Res("mbc")
        wring = Ring(P, 2, [128, KC, 512], F32, "wada")
        bring = Ring(P, 2, [128, 512], F32, "bada")
        stg = Ring(P, 2, [128, 512], F32, "stg")
        for blk in range(24):
            wt, wres = wring.next()
            for k4 in range(4):
                P.dma("sp", wt[:, k4 * 4:(k4 + 1) * 4, :], w_ada[l][k4 * 512:(k4 + 1) * 512, blk * 512:(blk + 1) * 512].rearrange("(k p) m -> p k m", p=128), wres, writes=[wres])
            bt, btr = bring.next()
            P.dma("sp", bt[:], b_adabc[l][:, blk * 512:(blk + 1) * 512], btr, writes=[btr])
            mi = blk // 4
            for b in range(2):
                pt, ptr = psum.next()
                for kc in range(KC):
                    P.op("pe", lambda e, wt=wt, kc=kc, b=b, pt=pt: e.matmul(out=pt[:], lhsT=cbc[:, kc, b, :], rhs=wt[:, kc, :], start=(kc == 0), stop=(kc == KC - 1)),
                         reads=[wres, R_cs], writes=[ptr], signal=(kc == KC - 1))
                s, sr = stg.next()
                P.op("dve", lambda e, s=s, pt=pt, bt=bt: e.tensor_tensor(out=s[:], in0=pt[:], in1=bt[:], op=ALU.add), reads=[ptr, btr], writes=[sr])
                if mi in (2, 5):
                    which = 0 if mi == 2 else 1
                    c0 = (blk % 4) * 512
                    P.dma("sp", MODBC[which, b, :, c0:c0 + 512], s[:], sr, reads=[sr], writes=[R_mbc])
                else:
                    p2, p2r = psum.next()
                    for j in range(4):
                        P.op("pe", lambda e, p2=p2, s=s, j=j: e.transpose(out=p2[:, j * 128:(j + 1) * 128], in_=s[:, j * 128:(j + 1) * 128], identity=ident), reads=[sr, R_c], writes=[p2r], signal=(j == 3))
                    P.op("dve", lambda e, p2=p2, blk=blk, b=b: e.tensor_copy(out=modT[:, blk * 4:(blk + 1) * 4, b], in_=p2[:].rearrange("p (j c) -> p j c", c=128)[:, :, 0]), reads=[p2r], writes=[R_mod])
        for mi in (1, 4):
            P.op("dve", lambda e, mi=mi: e.tensor_scalar(out=modT[:, mi * 16:(mi + 1) * 16, :], in0=modT[:, mi * 16:(mi + 1) * 16, :], scalar1=1.0, scalar2=None, op0=ALU.add), reads=[R_mod], writes=[R_mod])
        P.pop()

        stop_if("M")

        def transpose_mod(src_tile, srcres, dstT, dstres, ti, shift_i, scale_i, b):
            for k4 in range(4):
                pt, ptr = psum_box[0].next()
                for j in range(4):
                    kc = k4 * 4 + j
                    P.op("pe", lambda e, pt=pt, j=j, kc=kc: e.transpose(out=pt[:, j * 128:(j + 1) * 128], in_=src_tile[:, kc * 128:(kc + 1) * 128], identity=ident),
                         reads=[srcres, R_c], writes=[ptr], signal=(j == 3))
                for j in range(4):
                    kc = k4 * 4 + j
                    eng = "act" if j % 2 == 0 else "dve"
                    sc = modT[:, scale_i * 16 + kc, b:b + 1]
                    sh = modT[:, shift_i * 16 + kc, b:b + 1]
                    o = dstT[:, kc, ti * 128:(ti + 1) * 128]
                    i = pt[:, j * 128:(j + 1) * 128]
                    if eng == "act":
                        P.op("act", lambda e, o=o, i=i, sc=sc, sh=sh: e.activation(out=o, in_=i, func=AF.Identity, scale=sc, bias=sh), reads=[ptr, R_mod], writes=[dstres])
                    else:
                        P.op("dve", lambda e, o=o, i=i, sc=sc, sh=sh: e.tensor_scalar(out=o, in0=i, scalar1=sc, scalar2=sh, op0=ALU.mult, op1=ALU.add), reads=[ptr, R_mod], writes=[dstres])

        def load_wblk(ring, src, k0, nk, m0, mw=512):
            wt, wres = ring.next()
            for k4 in range(0, nk, 4):
                n4 = min(4, nk - k4)
                P.dma("sp", wt[:, k4:k4 + n4, :mw], src[(k0 + k4) * 128:(k0 + k4 + n4) * 128, m0:m0 + mw].rearrange("(k p) m -> p k m", p=128), wres, reads=[R_W], writes=[wres])
            return wt, wres

        P.push()
        psum = Ring(P, 8, [128, 512], F32, "ps", psum=True)
        psum_box[0] = psum
        xring = Ring(P, 2, [128, D], F32, "x1")
        hT = P.sb([128, KC, 512], BF16, "hT")
        R_h = Res("h")
        wring = Ring(P, 3, [128, KC, 512], BF16, "w1")
        stg = Ring(P, 4, [128, 512], BF16, "stg")
        stg32 = Ring(P, 2, [128, 2, 512], F32, "stg32")
        qb_r = Ring(P, 2, [128, 512], BF16, "qb")
        cs_r = Ring(P, 2, [128, 2, 512], F32, "cs")
        binT = P.sb([128, NFM], F32, "binT")
        P.dma("sp", binT[:], b_inT[l], R_mod, writes=[R_mod])
        btok = P.sb([128, 1536], F32, "btok")
        P.dma("sp", btok[:], b_intok[l], R_mod, writes=[R_mod])
        tmb = Ring(P, 2, [128, 512], F32, "tmb")
        sg_g = P.sb([128, 1024], F32, "sg_g")
        sg_b = P.sb([128, 1024], F32, "sg_b")
        P.dma("sp", sg_g[:], sgu_g[l], R_mod, writes=[R_mod])
        P.dma("sp", sg_b[:], sgu_bb[l], R_mod, writes=[R_mod])
        al = P.sb([128, 16], F32, "al")
        db = P.sb([128, 16], F32, "db")
        P.dma("sp", al[:], alog[l], R_mod, writes=[R_mod])
        P.dma("sp", db[:], dtb[l], R_mod, writes=[R_mod])
        P.op("act", lambda e: e.activation(out=al[:], in_=al[:], func=AF.Exp), reads=[R_mod], writes=[R_mod])
        P.op("dve", lambda e: e.tensor_scalar(out=al[:], in0=al[:], scalar1=-1.0, scalar2=None, op0=ALU.mult), reads=[R_mod], writes=[R_mod])
        sgv = Ring(P, 4, [128, 1024], F32, "sgv")
        sgvb = Ring(P, 2, [128, 1024], BF16, "sgvb")
        small = Ring(P, 2, [128, 32], F32, "small")
        gbt = Ring(P, 2, [128, 64], F32, "gbt")

        for (s0, W, isctx) in segs:
            b = 1 if isctx else 0
            nt = W // 128
            for ti in range(nt):
                xt, xres = xring.next()
                P.dma("sp", xt[:], X[s0 + ti * 128:s0 + (ti + 1) * 128, :], xres, reads=[R_X], writes=[xres])
                transpose_mod(xt, xres, hT, R_h, ti, 0, 1, b)
            cs, csr = cs_r.next()
            P.dma("sp", cs[:, 0, :W], cosT[:, s0:s0 + W], csr, writes=[csr])
            P.dma("sp", cs[:, 1, :W], sinT[:, s0:s0 + W], csr, writes=[csr])
            for blk in range(NFM // 4):
                wt, wres = load_wblk(wring, wb_in[l], 0, KC, blk * 512)
                for j in range(4):
                    ch = blk * 4 + j
                    pt, ptr = psum.next()
                    for kc in range(KC):
                        P.op("pe", lambda e, pt=pt, wt=wt, kc=kc, j=j, W=W: e.matmul(out=pt[:, :W], lhsT=wt[:, kc, j * 128:(j + 1) * 128], rhs=hT[:, kc, :W], start=(kc == 0), stop=(kc == KC - 1)),
                             reads=[wres, R_h], writes=[ptr], signal=(kc == KC - 1))
                    bias = binT[:, ch:ch + 1]
                    s, sr = stg.next()
                    if ch < C_Z:
                        P.op("act", lambda e, s=s, pt=pt, W=W, bias=bias: e.activation(out=s[:, :W], in_=pt[:, :W], func=AF.Identity, bias=bias), reads=[ptr, R_mod], writes=[sr])
                        dst = QKVT[ch, :, s0:s0 + W]
                    elif ch < C_SQ:
                        P.op("act", lambda e, s=s, pt=pt, W=W, bias=bias: e.activation(out=s[:, :W], in_=pt[:, :W], func=AF.Silu, bias=bias), reads=[ptr, R_mod], writes=[sr])
                        dst = ZT[ch - C_Z, :, s0:s0 + W]
                    elif ch < C_U:
                        qb, qbr = qb_r.next()
                        P.op("act", lambda e, qb=qb, pt=pt, W=W, bias=bias: e.activation(out=qb[:, :W], in_=pt[:, :W], func=AF.Identity, bias=bias), reads=[ptr, R_mod], writes=[qbr])
                        p2, p2r = psum.next()
                        P.op("pe", lambda e, p2=p2, qb=qb, W=W: e.matmul(out=p2[:, :W], lhsT=permb, rhs=qb[:, :W], start=True, stop=True), reads=[qbr, R_c], writes=[p2r])
                        t32, t32r = stg32.next()
                        P.op("dve", lambda e, t32=t32, qb=qb, cs=cs, W=W: e.tensor_tensor(out=t32[:, 0, :W], in0=qb[:, :W], in1=cs[:, 0, :W], op=ALU.mult), reads=[qbr, csr], writes=[t32r])
                        P.op("dve", lambda e, t32=t32, p2=p2, cs=cs, W=W: e.tensor_tensor(out=t32[:, 1, :W], in0=p2[:, :W], in1=cs[:, 1, :W], op=ALU.mult), reads=[p2r, csr], writes=[t32r])
                        P.op("pool", lambda e, t32=t32, s=s, W=W: e.tensor_tensor(out=s[:, :W], in0=t32[:, 0, :W], in1=t32[:, 1, :W], op=ALU.add), reads=[t32r], writes=[sr])
                        dst = SQT[ch - C_SQ, :, s0:s0 + W] if ch < C_SK else SKT[ch - C_SK, :, s0:s0 + W]
                    elif ch < C_G:
                        t32, t32r = stg32.next()
                        gelu_tanh(pt[:, :W], ptr, bias, s[:, :W], sr, W, t32, t32r)
                        dst = UT[ch - C_U, :, s0:s0 + W]
                    else:
                        P.op("act", lambda e, s=s, pt=pt, W=W, bias=bias: e.activation(out=s[:, :W], in_=pt[:, :W], func=AF.Sigmoid, bias=bias), reads=[ptr, R_mod], writes=[sr])
                        dst = GT[ch - C_G, :, s0:s0 + W]
                    P.dma("sp", dst, s[:, :W], sr, reads=[sr], writes=[R_P1])
            sv_store = [sgv.next() for _ in range(nt)]
            for tb in range(3):
                wt, wres = load_wblk(wring, wb_in[l], 0, KC, NFM * 128 + tb * 512)
                for ti in range(nt):
                    r0 = s0 + ti * 128
                    pt, ptr = psum.next()
                    for kc in range(KC):
                        P.op("pe", lambda e, pt=pt, wt=wt, kc=kc, ti=ti: e.matmul(out=pt[:], lhsT=hT[:, kc, ti * 128:(ti + 1) * 128], rhs=wt[:, kc, :], start=(kc == 0), stop=(kc == KC - 1)),
                             reads=[wres, R_h], writes=[ptr], signal=(kc == KC - 1))
                    pp, ppr = pt, ptr
                    pt, ptr = tmb.next()
                    P.op("dve", lambda e, pt=pt, pp=pp, tb=tb: e.tensor_tensor(out=pt[:], in0=pp[:], in1=btok[:, tb * 512:(tb + 1) * 512], op=ALU.add), reads=[ppr, R_mod], writes=[ptr])
                    if tb == 0:
                        s, sr = stg.next()
                        P.op("act", lambda e, s=s, pt=pt: e.activation(out=s[:, 0:256], in_=pt[:, 0:256], func=AF.Copy), reads=[ptr], writes=[sr])
                        P.dma("sp", SV[r0:r0 + 128, :], s[:, 0:256], sr, reads=[sr], writes=[R_P1])
                        g, gr = gbt.next()
                        P.op("act", lambda e, g=g, pt=pt: e.activation(out=g[:, 0:16], in_=pt[:, 256:272], func=AF.Sigmoid), reads=[ptr], writes=[gr])
                        P.op("dve", lambda e, g=g, pt=pt: e.tensor_tensor(out=g[:, 32:48], in0=pt[:, 272:288], in1=db[:], op=ALU.add), reads=[ptr, R_mod], writes=[gr])
                        P.op("act", lambda e, g=g: e.activation(out=g[:, 32:48], in_=g[:, 32:48], func=AF.Exp), reads=[gr], writes=[gr])
                        P.op("act", lambda e, g=g: e.activation(out=g[:, 32:48], in_=g[:, 32:48], func=AF.Ln, bias=1.0), reads=[gr], writes=[gr])
                        P.op("dve", lambda e, g=g: e.tensor_tensor(out=g[:, 16:32], in0=g[:, 32:48], in1=al[:], op=ALU.mult), reads=[gr, R_mod], writes=[gr])
                        P.dma("sp", BETA[r0:r0 + 128, :], g[:, 0:16], gr, reads=[gr], writes=[R_P1])
                        P.dma("sp", GG[r0:r0 + 128, :], g[:, 16:32], gr, reads=[gr], writes=[R_P1])
                    else:
                        v32, v32r = sv_store[ti]
                        t32, t32r = stg32.next()
                        gelu_tanh(pt[:], ptr, None, v32[:, (tb - 1) * 512:tb * 512], v32r, 512, t32, t32r)
                        if tb == 2:
                            sm, smr = small.next()
                            vb, vbr = sgvb.next()
                            layer_norm_tile(v32[:], sg_g[:], sg_b[:], v32[:], v32r, sm, smr, ncols=1024)
                            P.op("act", lambda e, vb=vb, v32=v32: e.activation(out=vb[:], in_=v32[:], func=AF.Copy), reads=[v32r], writes=[vbr])
                            P.dma("sp", SGV[r0:r0 + 128, :], vb[:], vbr, reads=[vbr], writes=[R_P1])
        P.pop()

        build_mixers(P, nc, locals(), l)
        build_merge_ffn(P, nc, locals(), l)

    P.push()
    xr = Ring(P, 2, [128, D], F32, "xo")
    for t0 in range(0, T, 128):
        xt, xres = xr.next()
        P.dma("sp", xt[:], X[L + t0:L + t0 + 128, :], xres, reads=[R_X], writes=[xres])
        P.dma("sp", out[t0:t0 + 128, :], xt[:], xres, reads=[xres])
    P.pop()
    P.finish()
    return nc


def build_mixers(P, nc, env, l):
    g = env
    L, T, Tt = g["L"], g["T"], g["Tt"]
    need_ctx = g["need_ctx"]
    R_P1, R_Y, R_c, R_mod = g["R_P1"], g["R_Y"], g["R_c"], g["R_mod"]
    cstb, smk = g["cstb"], g["smk"]
    onesb = cstb[:, 2, :]
    P.push()
    psum = Ring(P, 8, [128, 512], F32, "ps", psum=True)
    wsT32 = P.sb([128, 8, 128], F32, "wsT32")
    wsT = P.sb([128, 8, 128], BF16, "wsT")
    P.dma("sp", wsT32[:], g["sgu_wT"][l].rearrange("g q p -> q g p"), R_mod, writes=[R_mod])
    P.op("dve", lambda e: e.tensor_copy(out=wsT[:], in_=wsT32[:]), reads=[R_mod], writes=[R_mod])
    sbias = P.sb([128, 1024], F32, "sbias")
    P.dma("sp", sbias[:], g["sgu_bias"][l], R_mod, writes=[R_mod])
    vr = Ring(P, 2, [128, 1024], BF16, "sv")
    ur = Ring(P, 2, [128, 8, 128], BF16, "su")
    tr = Ring(P, 2, [128, 1024], F32, "st")
    yr = Ring(P, 2, [128, 8, 128], BF16, "sy")
    for t0 in range(0, Tt, 128):
        if t0 < L and not need_ctx:
            continue
        v, vres = vr.next()
        u, ures = ur.next()
        P.dma("sp", v[:], g["SGV"][t0:t0 + 128, :], vres, reads=[R_P1], writes=[vres])
        P.dma("sp", u[:], g["UT"][:, :, t0:t0 + 128].rearrange("g c p -> c g p"), ures, reads=[R_P1], writes=[ures])
        tmp, tmpr = tr.next()
        y, yres = yr.next()
        for half in range(2):
            pt, ptr = psum.next()
            for gg in range(4):
                gi = half * 4 + gg
                P.op("pe", lambda e, pt=pt, gg=gg, gi=gi, v=v: e.matmul(out=pt[:, gg * 128:(gg + 1) * 128], lhsT=v[:, gi * 128:(gi + 1) * 128], rhs=wsT[:, gi, :], start=True, stop=True),
                     reads=[vres, R_mod], writes=[ptr], signal=(gg == 3))
            P.op("dve", lambda e, pt=pt, tmp=tmp, half=half: e.tensor_tensor(out=tmp[:, half * 512:(half + 1) * 512], in0=pt[:], in1=sbias[:, half * 512:(half + 1) * 512], op=ALU.add), reads=[ptr, R_mod], writes=[tmpr])
        P.op("pool", lambda e, y=y, tmp=tmp, u=u: e.tensor_tensor(out=y[:].rearrange("c g p -> c (g p)"), in0=tmp[:], in1=u[:].rearrange("c g p -> c (g p)"), op=ALU.mult), reads=[tmpr, ures], writes=[yres])
        P.dma("sp", g["YCT"][:, :, t0:t0 + 128].rearrange("g c p -> c g p"), y[:], yres, reads=[yres], writes=[R_Y])
    P.pop()
    g["stop_if"]("SGU")

    P.push()
    psum = Ring(P, 4, [128, 512], F32, "ps", psum=True)
    pacc = Ring(P, 4, [128, 512], F32, "pa", psum=True)
    esink = P.sb([128, 16], F32, "esink")
    P.dma("sp", esink[:], g["sinks"][l], R_mod, writes=[R_mod])
    P.op("act", lambda e: e.activation(out=esink[:], in_=esink[:], func=AF.Exp), reads=[R_mod], writes=[R_mod])
    kr = Ring(P, 2, [128, 4, 640], BF16, "kk")
    vvr = Ring(P, 2, [128, 5, 256], BF16, "vv")
    qr = Ring(P, 2, [128, 8, 128], BF16, "qq")
    qmr = Ring(P, 2, [128, 8, 2, 128], BF16, "qm")
    cst = g["cst"]
    er = Ring(P, 3, [128, 512], BF16, "ee")
    dr = Ring(P, 2, [64, 512], F32, "dd")
    orr = Ring(P, 2, [64, 512], BF16, "oo")
    nblk = T // 128
    qblocks = ([("c", i) for i in range(L // 128)] if need_ctx else []) + [("l", i) for i in range(nblk)]
    for kind, i in qblocks:
        if kind == "c":
            q0 = i * 128
            kbs = [(j * 128, None) for j in range(L // 128)]
        else:
            q0 = L + i * 128
            kbs = []
            if i > 0:
                kbs.append((L + (i - 1) * 128, 0))
            kbs.append((L + i * 128, None))
            if i < nblk - 1:
                kbs.append((L + (i + 1) * 128, 1))
            kbs += [(j * 128, None) for j in range(L // 128)]
        nk = len(kbs)
        kt, kres = kr.next()
        vt, vres = vvr.next()
        qt, qres = qr.next()
        for bi, (k0, _) in enumerate(kbs):
            P.dma("sp", kt[:, :, bi * 128:(bi + 1) * 128], g["SKT"][:, :, k0:k0 + 128].rearrange("g p t -> p g t"), kres, reads=[R_P1], writes=[kres])
            P.dma("sp", vt[:, bi, :], g["SV"][k0:k0 + 128, :], vres, reads=[R_P1], writes=[vres])
        P.dma("sp", qt[:], g["SQT"][:, :, q0:q0 + 128].rearrange("c p t -> p c t"), qres, reads=[R_P1], writes=[qres])
        qm, qmres = qmr.next()
        for hf in range(2):
            P.op("dve", lambda e, qm=qm, qt=qt, hf=hf: e.tensor_scalar(out=qm[:, :, hf, :], in0=qt[:], scalar1=cst[:, 6 + hf, 0:1], scalar2=None, op0=ALU.mult), reads=[qres, R_c], writes=[qmres])
        for gi in range(4):
            po, por = pacc.next()
            pd, pdr = pacc.next()
            for bi, (k0, mi) in enumerate(kbs):
                pt, ptr = psum.next()
                for hh in range(4):
                    h = gi * 4 + hh
                    pb = (h % 2) * 64
                    P.op("pe", lambda e, pt=pt, hh=hh, h=h, bi=bi, kt=kt, qm=qm, gi=gi: e.matmul(out=pt[:, hh * 128:(hh + 1) * 128], lhsT=kt[:, gi, bi * 128:(bi + 1) * 128], rhs=qm[:, h // 2, h % 2, :], start=True, stop=True),
                         reads=[kres, qmres], writes=[ptr], signal=(hh == 3))
                ee, eres = er.next()
                P.op("act", lambda e, ee=ee, pt=pt: e.activation(out=ee[:], in_=pt[:], func=AF.Exp, scale=0.125), reads=[ptr], writes=[eres])
                if mi is not None:
                    P.op("dve", lambda e, ee=ee, mi=mi: e.tensor_tensor(out=ee[:].rearrange("k (h q) -> k h q", h=4), in0=ee[:].rearrange("k (h q) -> k h q", h=4), in1=smk[:, mi, :].unsqueeze(1).to_broadcast([128, 4, 128]), op=ALU.mult), reads=[eres, R_c], writes=[eres])
                P.op("pe", lambda e, po=po, vt=vt, bi=bi, gi=gi, ee=ee: e.matmul(out=po[0:64, :], lhsT=vt[:, bi, gi * 64:(gi + 1) * 64], rhs=ee[:], start=(bi == 0), stop=(bi == nk - 1)), reads=[vres, eres], writes=[por], signal=(bi == nk - 1))
                P.op("pe", lambda e, pd=pd, ee=ee, bi=bi: e.matmul(out=pd[0:64, :], lhsT=onesb[:, 0:64], rhs=ee[:], start=(bi == 0), stop=(bi == nk - 1)), reads=[eres, R_c], writes=[pdr], signal=(bi == nk - 1))
            dd, dres = dr.next()
            for hh in range(4):
                h = gi * 4 + hh
                P.op("dve", lambda e, dd=dd, pd=pd, hh=hh, h=h: e.tensor_scalar(out=dd[:, hh * 128:(hh + 1) * 128], in0=pd[0:64, hh * 128:(hh + 1) * 128], scalar1=esink[0:64, h:h + 1], scalar2=None, op0=ALU.add), reads=[pdr, R_mod], writes=[dres])
            P.op("dve", lambda e, dd=dd: e.reciprocal(out=dd[:], in_=dd[:]), reads=[dres], writes=[dres])
            oo, ores = orr.next()
            P.op("dve", lambda e, oo=oo, po=po, dd=dd: e.tensor_tensor(out=oo[:], in0=po[0:64, :], in1=dd[:], op=ALU.mult), reads=[por, dres], writes=[ores])
            for hh in range(4):
                h = gi * 4 + hh
                P.dma("sp", g["YBT"][h // 2, (h % 2) * 64:(h % 2) * 64 + 64, q0:q0 + 128], oo[:, hh * 128:(hh + 1) * 128], ores, reads=[ores], writes=[R_Y])
    P.pop()
    g["stop_if"]("SWA")
    build_gdn(P, nc, env, l)


def build_gdn(P, nc, env, l):
    g = env
    L, T, Tt = g["L"], g["T"], g["Tt"]
    need_ctx = g["need_ctx"]
    R_P1, R_Y, R_c = g["R_P1"], g["R_Y"], g["R_c"]
    cst, cstb, epsln = g["cst"], g["cstb"], g["epsln"]
    QKVT, ZT, GG, BETA = g["QKVT"], g["ZT"], g["GG"], g["BETA"]
    QNT, KNT, QTOK, KTOK, VTOK, OACC, YAT = g["QNT"], g["KNT"], g["QTOK"], g["KTOK"], g["VTOK"], g["OACC"], g["YAT"]
    ident, identb, onesb = cst[:, 0, :], cstb[:, 0, :], cstb[:, 2, :]
    R_F = Res("F")
    R_O = Res("O")
    bc3 = lambda ap, n: ap.unsqueeze(2).to_broadcast([128, ap.shape[1], n])

    P.push()
    psum = Ring(P, 4, [128, 512], F32, "ps", psum=True)
    ptb = Ring(P, 2, [128, 512], BF16, "ptb", psum=True)
    cw = P.sb([128, 24, 5], F32, "cw")
    R_cw = Res("cw")
    P.dma("sp", cw[:], g["convT"][l], R_cw, writes=[R_cw])
    winr = Ring(P, 3, [128, 516], BF16, "win")
    accr = Ring(P, 2, [128, 512], F32, "acc")
    sr_ = Ring(P, 2, [128, 512], F32, "sil")
    sqr = Ring(P, 2, [128, 512], BF16, "sq")
    rnr = Ring(P, 2, [128, 512], F32, "rn")
    fbr = Ring(P, 3, [128, 512], BF16, "fb")
    tkr = Ring(P, 3, [128, 512], BF16, "tk")
    segs = [(0, L, 0, L)] + [(L + i * 512, min(512, T - i * 512), L, Tt) for i in range((T + 511) // 512)]
    for (s0, W, lo, hi) in segs:
        nt = W // 128
        for ch in range(24):
            win, wres = winr.next()
            a0 = max(s0 - 2, lo)
            a1 = min(s0 + W + 2, hi)
            if a0 > s0 - 2:
                P.op("pool", lambda e, win=win: e.memset(win[:, 0:2], 0.0), writes=[wres])
            if a1 < s0 + W + 2:
                P.op("pool", lambda e, win=win, W=W: e.memset(win[:, W + 2:W + 4], 0.0), writes=[wres])
            P.dma("sp", win[:, a0 - (s0 - 2):a1 - (s0 - 2)], QKVT[ch, :, a0:a1], wres, reads=[R_P1], writes=[wres])
            acc, ares = accr.next()
            P.op("dve", lambda e, acc=acc, win=win, ch=ch, W=W: e.tensor_scalar(out=acc[:, :W], in0=win[:, 0:W], scalar1=cw[:, ch, 0:1], scalar2=None, op0=ALU.mult), reads=[wres, R_cw], writes=[ares])
            for k in range(1, 5):
                P.op("dve", lambda e, acc=acc, win=win, ch=ch, W=W, k=k: e.scalar_tensor_tensor(out=acc[:, :W], in0=win[:, k:k + W], scalar=cw[:, ch, k:k + 1], in1=acc[:, :W], op0=ALU.mult, op1=ALU.add), reads=[wres, R_cw, ares], writes=[ares])
            sl, slres = sr_.next()
            P.op("act", lambda e, sl=sl, acc=acc, W=W: e.activation(out=sl[:, :W], in_=acc[:, :W], func=AF.Silu), reads=[ares], writes=[slres])
            fb, fres = fbr.next()
            if ch < 16:
                sq, sqres = sqr.next()
                P.op("pool", lambda e, sq=sq, sl=sl, W=W: e.tensor_tensor(out=sq[:, :W], in0=sl[:, :W], in1=sl[:, :W], op=ALU.mult), reads=[slres], writes=[sqres])
                pt, ptr = psum.next()
                P.op("pe", lambda e, pt=pt, sq=sq, W=W: e.matmul(out=pt[:, :W], lhsT=onesb, rhs=sq[:, :W], start=True, stop=True), reads=[sqres, R_c], writes=[ptr])
                rn, rnres = rnr.next()
                P.op("act", lambda e, rn=rn, pt=pt, W=W: e.activation(out=rn[:, :W], in_=pt[:, :W], func=AF.Sqrt, bias=epsln[:, 1:2]), reads=[ptr, R_c], writes=[rnres])
                P.op("dve", lambda e, rn=rn, W=W: e.reciprocal(out=rn[:, :W], in_=rn[:, :W]), reads=[rnres], writes=[rnres])
                qs = (128 ** -0.5) if ch < 8 else 1.0
                P.op("dve", lambda e, fb=fb, sl=sl, rn=rn, W=W, qs=qs: e.scalar_tensor_tensor(out=fb[:, :W], in0=sl[:, :W], scalar=qs, in1=rn[:, :W], op0=ALU.mult, op1=ALU.mult), reads=[slres, rnres], writes=[fres])
                dstT = (QNT if ch < 8 else KNT)[ch % 8, :, s0:s0 + W]
                P.dma("sp", dstT, fb[:, :W], fres, reads=[fres], writes=[R_F])
            else:
                P.op("act", lambda e, fb=fb, sl=sl, W=W: e.activation(out=fb[:, :W], in_=sl[:, :W], func=AF.Copy), reads=[slres], writes=[fres])
            pb, pbr = ptb.next()
            for ti in range(nt):
                P.op("pe", lambda e, pb=pb, fb=fb, ti=ti: e.transpose(out=pb[:, ti * 128:(ti + 1) * 128], in_=fb[:, ti * 128:(ti + 1) * 128], identity=identb), reads=[fres, R_c], writes=[pbr], signal=(ti == nt - 1))
            tk, tkres = tkr.next()
            P.op("act", lambda e, tk=tk, pb=pb, W=W: e.activation(out=tk[:, :W], in_=pb[:, :W], func=AF.Copy), reads=[pbr], writes=[tkres])
            dtok = (QTOK, KTOK, VTOK)[ch // 8]
            P.dma("sp", dtok[s0:s0 + W, ch % 8, :].rearrange("(n p) d -> p n d", p=128), tk[:, :W].rearrange("p (n d) -> p n d", d=128), tkres, reads=[tkres], writes=[R_F])
    P.pop()

    P.push()
    psum = Ring(P, 6, [128, 512], F32, "ps", psum=True)
    ops_ = [P.ps([128, 512], F32, "po") for _ in range(2)]
    R_ops = [Res("po0"), Res("po1")]
    Esel = P.sb([128, 8, 128], F32, "Esel")
    R_E = Res("E")
    for h in range(8):
        P.op("dve", lambda e, h=h: e.tensor_copy(out=Esel[:, h, :], in_=ident[:, h:h + 1].to_broadcast([128, 128])), reads=[R_c], writes=[R_E])
    S32 = P.sb([128, 8, 128], F32, "S32")
    Sb = P.sb([128, 8, 128], BF16, "Sb")
    R_S = Res("S")
    R_Sb = Res("Sb")
    ldr = {n: Ring(P, 2, [128, 8, 128], BF16, n) for n in ("kT", "qT", "ktok", "qtok", "vtok")}
    gpr = Ring(P, 2, [128, 128], F32, "gpad")
    for _ in range(2):
        t_, r_ = gpr.next()
        P.op("pool", lambda e, t_=t_: e.memset(t_[:], 0.0), writes=[r_])
    btr = Ring(P, 2, [128, 8], F32, "beta")
    smr = Ring(P, 2, [128, 12, 8], F32, "sm")
    ctr = Ring(P, 2, [128, 128], F32, "cumT")
    big = {n: Ring(P, 2, [128, 8, 128], BF16, n) for n in ("vb", "kbg", "kd0", "kd1", "dg0", "dg1", "qd0", "qd1", "wTb", "aqk", "v0", "v1", "vf")}
    um = {n: Ring(P, 2, [128, 8, 128], F32, n) for n in ("um0", "um1")}
    f4 = {n: Ring(P, 2, [128, 4, 128], F32, n) for n in ("X", "Ds", "Dt", "N", "A", "inv")}
    invb_r = Ring(P, 2, [128, 4, 128], BF16, "invb")
    ostg = Ring(P, 2, [128, 8, 128], F32, "ostg")
    m_ap = [cst[:, 6, 0:1], cst[:, 7, 0:1]]
    negm = P.sb([128, 2], F32, "negm")
    for c in range(2):
        P.op("dve", lambda e, c=c: e.tensor_scalar(out=negm[:, c:c + 1], in0=m_ap[c], scalar1=-1.0, scalar2=None, op0=ALU.mult), reads=[R_c], writes=[R_E])
    nct, nlt = L // 128, T // 128
    for d in range(2):
        P.op("dve", lambda e: e.memset(S32[:], 0.0), writes=[R_S])
        P.op("dve", lambda e: e.memset(Sb[:], 0.0), writes=[R_Sb])
        tiles = [i * 128 for i in range(nct)] + [L + i * 128 for i in range(nlt)]
        if d == 1:
            tiles = [i * 128 for i in reversed(range(nct))] + [L + i * 128 for i in reversed(range(nlt))]
        order = (0, 1) if d == 0 else (1, 0)
        triT = cst[:, 3 + d, :]
        mposS = cst[:, 8 + d, :]
        mposT = cst[:, 10 + d, :]
        for t0 in tiles:
            ld = {}
            for n, src in (("kT", KNT), ("qT", QNT)):
                t_, r_ = ldr[n].next()
                P.dma("sp", t_[:], src[:, :, t0:t0 + 128].rearrange("h p t -> p h t"), r_, reads=[R_F], writes=[r_])
                ld[n] = (t_, r_)
            for n, src in (("ktok", KTOK), ("qtok", QTOK), ("vtok", VTOK)):
                t_, r_ = ldr[n].next()
                P.dma("sp", t_[:], src[t0:t0 + 128, :, :], r_, reads=[R_F], writes=[r_])
                ld[n] = (t_, r_)
            gp, gpres = gpr.next()
            bt, btres = btr.next()
            P.dma("sp", gp[:, 0:8], GG[t0:t0 + 128, d * 8:(d + 1) * 8], gpres, reads=[R_P1], writes=[gpres])
            P.dma("sp", bt[:], BETA[t0:t0 + 128, d * 8:(d + 1) * 8], btres, reads=[R_P1], writes=[btres])
            pt, ptr = psum.next()
            for j, lm in enumerate((triT, cst[:, 5, :], cst[:, 6, :], cst[:, 7, :])):
                P.op("pe", lambda e, pt=pt, j=j, lm=lm, gp=gp: e.matmul(out=pt[:, j * 128:(j + 1) * 128], lhsT=lm, rhs=gp[:], start=True, stop=True), reads=[gpres, R_c], writes=[ptr], signal=(j == 3))
            sm, smres = smr.next()
            P.op("dve", lambda e, sm=sm, pt=pt: e.tensor_copy(out=sm[:, 0:4, :], in_=pt[:].rearrange("p (a n) -> p a n", n=128)[:, :, 0:8]), reads=[ptr], writes=[smres])
            p2, p2r = psum.next()
            P.op("pe", lambda e, p2=p2, gp=gp, triT=triT: e.matmul(out=p2[:, 0:128], lhsT=gp[:], rhs=triT, start=True, stop=True), reads=[gpres, R_c], writes=[p2r])
            cT_, cTres = ctr.next()
            P.op("act", lambda e, cT_=cT_, p2=p2: e.activation(out=cT_[:], in_=p2[:, 0:128], func=AF.Copy), reads=[p2r], writes=[cTres])
            P.op("act", lambda e, sm=sm: e.activation(out=sm[:, 4, :], in_=sm[:, 0, :], func=AF.Exp), reads=[smres], writes=[smres])
            P.op("dve", lambda e, sm=sm: e.tensor_tensor(out=sm[:, 5, :], in0=sm[:, 1, :], in1=sm[:, 0, :], op=ALU.subtract), reads=[smres], writes=[smres])
            P.op("act", lambda e, sm=sm: e.activation(out=sm[:, 5:8, :], in_=sm[:, (5, 2, 3)[0]:(5, 2, 3)[0] + 1, :], func=AF.Exp) if False else e.activation(out=sm[:, 5, :], in_=sm[:, 5, :], func=AF.Exp), reads=[smres], writes=[smres])
            P.op("act", lambda e, sm=sm: e.activation(out=sm[:, 6:8, :], in_=sm[:, 2:4, :], func=AF.Exp), reads=[smres], writes=[smres])
            P.op("dve", lambda e, sm=sm, bt=bt: e.tensor_scalar(out=sm[:, 8, :], in0=bt[:], scalar1=-1.0, scalar2=None, op0=ALU.mult), reads=[btres], writes=[smres])
            P.op("dve", lambda e, sm=sm, bt=bt: e.tensor_tensor(out=sm[:, 9, :], in0=bt[:], in1=sm[:, 4, :], op=ALU.mult), reads=[btres, smres], writes=[smres])
            kT, kTres = ld["kT"]
            qT, qTres = ld["qT"]
            ktok, ktres = ld["ktok"]
            qtok, qtres = ld["qtok"]
            vtok, vtres = ld["vtok"]
            B = {n: big[n].next() for n in big}
            U = {n: um[n].next() for n in um}
            P.op("dve", lambda e, o=B["vb"][0], vtok=vtok, bt=bt: e.tensor_tensor(out=o[:], in0=vtok[:], in1=bc3(bt[:], 128), op=ALU.mult), reads=[vtres, btres], writes=[B["vb"][1]])
            P.op("dve", lambda e, o=B["kbg"][0], ktok=ktok, sm=sm: e.tensor_tensor(out=o[:], in0=ktok[:], in1=bc3(sm[:, 9, :], 128), op=ALU.mult), reads=[ktres, smres], writes=[B["kbg"][1]])
            for c in range(2):
                P.op("dve", lambda e, sm=sm, c=c: e.tensor_scalar(out=sm[:, 10 + c, :], in0=sm[:, 5, :], scalar1=m_ap[c], scalar2=None, op0=ALU.mult), reads=[smres, R_c], writes=[smres])
                P.op("dve", lambda e, o=B[f"kd{c}"][0], ktok=ktok, sm=sm, c=c: e.tensor_tensor(out=o[:], in0=ktok[:], in1=bc3(sm[:, 10 + c, :], 128), op=ALU.mult), reads=[ktres, smres], writes=[B[f"kd{c}"][1]])
                P.op("dve", lambda e, sm=sm, c=c: e.tensor_scalar(out=sm[:, 10 + c, :], in0=sm[:, 4, :], scalar1=m_ap[c], scalar2=None, op0=ALU.mult), reads=[smres, R_c], writes=[smres])
                P.op("dve", lambda e, o=B[f"dg{c}"][0], sm=sm, c=c: e.tensor_tensor(out=o[:], in0=identb.unsqueeze(1).to_broadcast([128, 8, 128]), in1=bc3(sm[:, 10 + c, :], 128), op=ALU.mult), reads=[smres, R_c], writes=[B[f"dg{c}"][1]])
                for hg in range(2):
                    pq, pqr = psum.next()
                    for hh in range(4):
                        h = hg * 4 + hh
                        P.op("pe", lambda e, pq=pq, hh=hh, h=h, qtok=qtok, dg=B[f"dg{c}"][0]: e.matmul(out=pq[:, hh * 128:(hh + 1) * 128], lhsT=qtok[:, h, :], rhs=dg[:, h, :], start=True, stop=True), reads=[qtres, B[f"dg{c}"][1]], writes=[pqr], signal=(hh == 3))
                    P.op("act", lambda e, o=B[f"qd{c}"][0], pq=pq, hg=hg: e.activation(out=o[:, hg * 4:(hg + 1) * 4, :].rearrange("p a n -> p (a n)"), in_=pq[:], func=AF.Copy), reads=[pqr], writes=[B[f"qd{c}"][1]])
            for hg in range(2):
                hs = slice(hg * 4, (hg + 1) * 4)
                F = {n: f4[n].next() for n in f4}
                pR, pRr = psum.next()
                pK, pKr = psum.next()
                pQ, pQr = psum.next()
                for hh in range(4):
                    h = hg * 4 + hh
                    cs = slice(hh * 128, (hh + 1) * 128)
                    P.op("pe", lambda e, pR=pR, cs=cs, h=h, cT_=cT_: e.matmul(out=pR[:, cs], lhsT=Esel[:, h, :], rhs=cT_[:], start=True, stop=True), reads=[R_E, cTres], writes=[pRr], signal=(hh == 3))
                for hh in range(4):
                    h = hg * 4 + hh
                    cs = slice(hh * 128, (hh + 1) * 128)
                    P.op("pe", lambda e, pK=pK, cs=cs, h=h, kT=kT: e.matmul(out=pK[:, cs], lhsT=kT[:, h, :], rhs=kT[:, h, :], start=True, stop=True), reads=[kTres], writes=[pKr], signal=(hh == 3))
                for hh in range(4):
                    h = hg * 4 + hh
                    cs = slice(hh * 128, (hh + 1) * 128)
                    P.op("pe", lambda e, pQ=pQ, cs=cs, h=h, kT=kT, qT=qT: e.matmul(out=pQ[:, cs], lhsT=kT[:, h, :], rhs=qT[:, h, :], start=True, stop=True), reads=[kTres, qTres], writes=[pQr], signal=(hh == 3))
                X, Xr = F["X"]
                Ds, Dsr = F["Ds"]
                Dt, Dtr = F["Dt"]
                N_, Nr = F["N"]
                A_, Ar = F["A"]
                inv, invr = F["inv"]
                fl = lambda t: t[:].rearrange("p a n -> p (a n)")
                P.op("dve", lambda e, X=X, pR=pR, sm=sm, hs=hs: e.tensor_tensor(out=X[:], in0=pR[:].rearrange("p (a n) -> p a n", n=128), in1=bc3(sm[:, 0, hs], 128), op=ALU.subtract), reads=[pRr, smres], writes=[Xr])
                P.op("dve", lambda e, X=X, Ds=Ds, mposS=mposS: e.tensor_tensor(out=Ds[:], in0=X[:], in1=mposS.unsqueeze(1).to_broadcast([128, 4, 128]), op=ALU.add), reads=[Xr, R_c], writes=[Dsr])
                P.op("act", lambda e, Ds=Ds: e.activation(out=fl(Ds), in_=fl(Ds), func=AF.Exp, scale=-1.0), reads=[Dsr], writes=[Dsr])
                P.op("dve", lambda e, X=X, Dt=Dt, mposT=mposT: e.tensor_tensor(out=Dt[:], in0=X[:], in1=mposT.unsqueeze(1).to_broadcast([128, 4, 128]), op=ALU.subtract), reads=[Xr, R_c], writes=[Dtr])
                P.op("act", lambda e, Dt=Dt: e.activation(out=fl(Dt), in_=fl(Dt), func=AF.Exp), reads=[Dtr], writes=[Dtr])
                P.op("dve", lambda e, N_=N_, pK=pK, sm=sm, hs=hs: e.tensor_tensor(out=N_[:], in0=pK[:].rearrange("p (a n) -> p a n", n=128), in1=bc3(sm[:, 8, hs], 128), op=ALU.mult), reads=[pKr, smres], writes=[Nr])
                P.op("dve", lambda e, N_=N_, Ds=Ds: e.tensor_tensor(out=N_[:], in0=N_[:], in1=Ds[:], op=ALU.mult), reads=[Nr, Dsr], writes=[Nr])
                if "DBG" in g and t0 == tiles[0] and hg == 0:
                    P.dma("sp", g["DBG"][d, 0, :, 0:96], sm[:].rearrange("p a n -> p (a n)"), smres, reads=[smres])
                    P.dma("sp", g["DBG"][d, 1], Ds[:].rearrange("p a n -> p (a n)"), Dsr, reads=[Dsr])
                    P.dma("sp", g["DBG"][d, 2], N_[:].rearrange("p a n -> p (a n)"), Nr, reads=[Nr])
                aq, aqr = B["aqk"]
                P.op("dve", lambda e, aq=aq, pQ=pQ, Dt=Dt, hs=hs: e.tensor_tensor(out=aq[:, hs, :], in0=pQ[:].rearrange("p (a n) -> p a n", n=128), in1=Dt[:], op=ALU.mult), reads=[pQr, Dtr], writes=[aqr])
                pT, pTr = psum.next()
                for hh in range(4):
                    P.op("pe", lambda e, pT=pT, hh=hh, N_=N_: e.transpose(out=pT[:, hh * 128:(hh + 1) * 128], in_=N_[:, hh, :], identity=ident), reads=[Nr, R_c], writes=[pTr], signal=(hh == 3))
                P.op("act", lambda e, A_=A_, pT=pT: e.activation(out=fl(A_), in_=pT[:], func=AF.Copy), reads=[pTr], writes=[Ar])
                P.op("dve", lambda e, inv=inv, A_=A_: e.tensor_tensor(out=inv[:], in0=A_[:], in1=ident.unsqueeze(1).to_broadcast([128, 4, 128]), op=ALU.add), reads=[Ar, R_c], writes=[invr])
                for it in range(5):
                    pA, pAr = psum.next()
                    pB, pBr = psum.next()
                    for hh in range(4):
                        cs = slice(hh * 128, (hh + 1) * 128)
                        P.op("pe", lambda e, pA=pA, cs=cs, hh=hh, A_=A_, N_=N_: e.matmul(out=pA[:, cs], lhsT=N_[:, hh, :], rhs=A_[:, hh, :], start=True, stop=True), reads=[Ar, Nr], writes=[pAr], signal=(hh == 3))
                    for hh in range(4):
                        cs = slice(hh * 128, (hh + 1) * 128)
                        P.op("pe", lambda e, pB=pB, cs=cs, hh=hh, A_=A_, N_=N_: e.matmul(out=pB[:, cs], lhsT=A_[:, hh, :], rhs=N_[:, hh, :], start=True, stop=True), reads=[Ar, Nr], writes=[pBr], signal=(hh == 3))
                    P.op("act", lambda e, A_=A_, pA=pA: e.activation(out=fl(A_), in_=pA[:], func=AF.Copy), reads=[pAr], writes=[Ar])
                    P.op("dve", lambda e, N_=N_, pB=pB: e.tensor_copy(out=fl(N_), in_=pB[:]), reads=[pBr], writes=[Nr])
                    pU, pUr = psum.next()
                    for hh in range(4):
                        cs = slice(hh * 128, (hh + 1) * 128)
                        P.op("pe", lambda e, pU=pU, cs=cs, hh=hh, inv=inv, N_=N_: e.matmul(out=pU[:, cs], lhsT=N_[:, hh, :], rhs=inv[:, hh, :], start=True, stop=True), reads=[Nr, invr], writes=[pUr], signal=(hh == 3))
                    P.op("dve", lambda e, inv=inv, pU=pU: e.tensor_tensor(out=fl(inv), in0=fl(inv), in1=pU[:], op=ALU.add), reads=[pUr, invr], writes=[invr])
                if "DBG" in g and t0 == tiles[0] and hg == 0:
                    P.dma("sp", g["DBG"][d, 3], inv[:].rearrange("p a n -> p (a n)"), invr, reads=[invr])
                ib, ibr = invb_r.next()
                P.op("act", lambda e, ib=ib, inv=inv: e.activation(out=fl(ib), in_=fl(inv), func=AF.Copy), reads=[invr], writes=[ibr])
                pu, pur = psum.next()
                pw, pwr = psum.next()
                for hh in range(4):
                    h = hg * 4 + hh
                    cs = slice(hh * 128, (hh + 1) * 128)
                    P.op("pe", lambda e, pu=pu, cs=cs, hh=hh, h=h, ib=ib, vb=B["vb"][0]: e.matmul(out=pu[:, cs], lhsT=ib[:, hh, :], rhs=vb[:, h, :], start=True, stop=True), reads=[ibr, B["vb"][1]], writes=[pur], signal=(hh == 3))
                for hh in range(4):
                    h = hg * 4 + hh
                    cs = slice(hh * 128, (hh + 1) * 128)
                    P.op("pe", lambda e, pw=pw, cs=cs, hh=hh, h=h, ib=ib, kbg=B["kbg"][0]: e.matmul(out=pw[:, cs], lhsT=kbg[:, h, :], rhs=ib[:, hh, :], start=True, stop=True), reads=[ibr, B["kbg"][1]], writes=[pwr], signal=(hh == 3))
                for c in range(2):
                    P.op("dve", lambda e, o=U[f"um{c}"][0], pu=pu, hs=hs, c=c: e.tensor_scalar(out=o[:, hs, :].rearrange("p a n -> p (a n)"), in0=pu[:], scalar1=m_ap[c], scalar2=None, op0=ALU.mult), reads=[pur, R_c], writes=[U[f"um{c}"][1]])
                P.op("act", lambda e, o=B["wTb"][0], pw=pw, hs=hs: e.activation(out=o[:, hs, :].rearrange("p a n -> p (a n)"), in_=pw[:], func=AF.Copy), reads=[pwr], writes=[B["wTb"][1]])
            vnames = ("v0", "v1")
            for ci, c in enumerate(order):
                vc, vcr = B[vnames[ci]]
                for hg in range(2):
                    hs = slice(hg * 4, (hg + 1) * 4)
                    pS, pSr = psum.next()
                    for hh in range(4):
                        h = hg * 4 + hh
                        cs = slice(hh * 128, (hh + 1) * 128)
                        P.op("pe", lambda e, pS=pS, cs=cs, h=h, w=B["wTb"][0]: e.matmul(out=pS[:, cs], lhsT=w[:, h, :], rhs=Sb[:, h, :], start=True, stop=True), reads=[B["wTb"][1], R_Sb], writes=[pSr], signal=(hh == 3))
                    P.op("dve", lambda e, vc=vc, pS=pS, hs=hs, c=c, u=U[f"um{c}"][0]: e.scalar_tensor_tensor(out=vc[:, hs, :].rearrange("p a n -> p (a n)"), in0=pS[:], scalar=negm[:, c:c + 1], in1=u[:, hs, :].rearrange("p a n -> p (a n)"), op0=ALU.mult, op1=ALU.add), reads=[pSr, U[f"um{c}"][1], R_E], writes=[vcr])
                    for hh in range(4):
                        h = hg * 4 + hh
                        cs = slice(hh * 128, (hh + 1) * 128)
                        P.op("pe", lambda e, hg=hg, cs=cs, h=h, qd=B[f"qd{c}"][0], ci=ci: e.matmul(out=ops_[hg][:, cs], lhsT=qd[:, h, :], rhs=Sb[:, h, :], start=(ci == 0 and cs.start == 0), stop=False), reads=[B[f"qd{c}"][1], R_Sb], writes=[R_ops[hg]], signal=False)
                for hg in range(2):
                    hs = slice(hg * 4, (hg + 1) * 4)
                    pD, pDr = psum.next()
                    for hh in range(4):
                        h = hg * 4 + hh
                        cs = slice(hh * 128, (hh + 1) * 128)
                        P.op("pe", lambda e, pD=pD, cs=cs, h=h, kd=B[f"kd{c}"][0], vc=vc: e.matmul(out=pD[:, cs], lhsT=kd[:, h, :], rhs=vc[:, h, :], start=True, stop=True), reads=[B[f"kd{c}"][1], vcr], writes=[pDr], signal=(hh == 3))
                    P.op("dve", lambda e, hs=hs, sm=sm, c=c: e.tensor_tensor(out=S32[:, hs, :], in0=S32[:, hs, :], in1=bc3(sm[:, 6 + c, hs], 128), op=ALU.mult), reads=[R_S, smres], writes=[R_S])
                    P.op("dve", lambda e, hs=hs, pD=pD: e.tensor_tensor(out=S32[:, hs, :].rearrange("p a n -> p (a n)"), in0=S32[:, hs, :].rearrange("p a n -> p (a n)"), in1=pD[:], op=ALU.add), reads=[R_S, pDr], writes=[R_S])
                P.op("act", lambda e: e.activation(out=Sb[:].rearrange("p a n -> p (a n)"), in_=S32[:].rearrange("p a n -> p (a n)"), func=AF.Copy), reads=[R_S], writes=[R_Sb])
            if "DBGB" in g and t0 == tiles[0]:
                for k_, n_ in enumerate(("aqk", "wTb", "qd0", "qd1", "kd0", "kd1", "v0", "v1")):
                    P.dma("sp", g["DBGB"][d, k_], B[n_][0][:, 0:4, :].rearrange("p a n -> p (a n)"), B[n_][1], reads=[B[n_][1]])
                for k_, n_ in enumerate(("um0", "um1")):
                    P.dma("sp", g["DBGF"][d, k_], U[n_][0][:, 0:4, :].rearrange("p a n -> p (a n)"), U[n_][1], reads=[U[n_][1]])
                P.dma("sp", g["DBGF"][d, 2], S32[:, 0:4, :].rearrange("p a n -> p (a n)"), R_S, reads=[R_S])
            vf, vfr = B["vf"]
            P.op("pool", lambda e, vf=vf, a=B["v0"][0], b=B["v1"][0]: e.tensor_tensor(out=vf[:], in0=a[:], in1=b[:], op=ALU.add), reads=[B["v0"][1], B["v1"][1]], writes=[vfr])
            og, ogr = ostg.next()
            for hg in range(2):
                hs = slice(hg * 4, (hg + 1) * 4)
                for hh in range(4):
                    h = hg * 4 + hh
                    cs = slice(hh * 128, (hh + 1) * 128)
                    P.op("pe", lambda e, hg=hg, cs=cs, h=h, aq=B["aqk"][0], vf=vf: e.matmul(out=ops_[hg][:, cs], lhsT=aq[:, h, :], rhs=vf[:, h, :], start=False, stop=True), reads=[B["aqk"][1], vfr], writes=[R_ops[hg]], signal=(hh == 3))
                P.op("act", lambda e, og=og, hg=hg, hs=hs: e.activation(out=og[:, hs, :].rearrange("p a n -> p (a n)"), in_=ops_[hg][:], func=AF.Copy), reads=[R_ops[hg]], writes=[ogr])
            P.dma("sp", OACC[d, t0:t0 + 128, :, :], og[:], ogr, reads=[ogr], writes=[R_O])
    P.pop()

    P.push()
    ptb = Ring(P, 2, [128, 1024], BF16, "ptb", psum=True)
    gn = P.sb([128, 1], F32, "gn")
    R_gn = Res("gn")
    P.dma("sp", gn[:], g["gnormT"][l], R_gn, writes=[R_gn])
    oar = Ring(P, 2, [128, 8, 128], F32, "oa")
    obr = Ring(P, 2, [128, 8, 128], F32, "ob")
    sqr2 = Ring(P, 2, [128, 8, 128], F32, "sq2")
    ssr = Ring(P, 2, [128, 8], F32, "ss")
    onr = Ring(P, 2, [128, 8, 128], BF16, "on")
    zr = Ring(P, 2, [128, 8, 128], BF16, "zz")
    yr = Ring(P, 2, [128, 8, 128], BF16, "ya")
    for t0 in range(0, Tt, 128):
        if t0 < L and not need_ctx:
            continue
        oa, oares = oar.next()
        ob, obres = obr.next()
        P.dma("sp", oa[:], OACC[0, t0:t0 + 128, :, :], oares, reads=[R_O], writes=[oares])
        P.dma("sp", ob[:], OACC[1, t0:t0 + 128, :, :], obres, reads=[R_O], writes=[obres])
        z, zres = zr.next()
        P.dma("sp", z[:], ZT[:, :, t0:t0 + 128].rearrange("h p t -> p h t"), zres, reads=[R_P1], writes=[zres])
        P.op("dve", lambda e, oa=oa, ob=ob: e.tensor_tensor(out=oa[:], in0=oa[:], in1=ob[:], op=ALU.add), reads=[oares, obres], writes=[oares])
        sq, sqres = sqr2.next()
        P.op("pool", lambda e, sq=sq, oa=oa: e.tensor_tensor(out=sq[:], in0=oa[:], in1=oa[:], op=ALU.mult), reads=[oares], writes=[sqres])
        ss, ssres = ssr.next()
        P.op("dve", lambda e, ss=ss, sq=sq: e.tensor_reduce(out=ss[:], in_=sq[:], axis=mybir.AxisListType.X, op=ALU.add), reads=[sqres], writes=[ssres])
        P.op("act", lambda e, ss=ss: e.activation(out=ss[:], in_=ss[:], func=AF.Sqrt, scale=1.0 / 128.0, bias=epsln[:, 1:2]), reads=[ssres, R_c], writes=[ssres])
        P.op("dve", lambda e, ss=ss: e.reciprocal(out=ss[:], in_=ss[:]), reads=[ssres], writes=[ssres])
        on, onres = onr.next()
        P.op("dve", lambda e, on=on, oa=oa, ss=ss: e.tensor_tensor(out=on[:], in0=oa[:], in1=bc3(ss[:], 128), op=ALU.mult), reads=[oares, ssres], writes=[onres])
        pb, pbr = ptb.next()
        for h in range(8):
            P.op("pe", lambda e, pb=pb, h=h, on=on: e.transpose(out=pb[:, h * 128:(h + 1) * 128], in_=on[:, h, :], identity=identb), reads=[onres, R_c], writes=[pbr], signal=(h == 7))
        y, yres = yr.next()
        P.op("dve", lambda e, y=y, pb=pb, z=z: e.scalar_tensor_tensor(out=y[:].rearrange("p a n -> p (a n)"), in0=pb[:], scalar=gn[:, 0:1], in1=z[:].rearrange("p a n -> p (a n)"), op0=ALU.mult, op1=ALU.mult), reads=[pbr, R_gn, zres], writes=[yres])
        P.dma("sp", YAT[:, :, t0:t0 + 128].rearrange("h p t -> p h t"), y[:], yres, reads=[yres], writes=[R_Y])
    P.pop()


def build_gdn_test(T, L, need_ctx=True):
    Tt = T + L
    nc = bass.Bass("TRN2", target_bir_lowering=False)
    P = Prog(nc)
    din = lambda name, shape, dt=F32: nc.dram_tensor(name, list(shape), dt, kind="ExternalInput").ap()
    dout = lambda name, shape, dt=F32: nc.dram_tensor(name, list(shape), dt, kind="ExternalOutput").ap()
    dscr = lambda name, shape, dt=F32: nc.dram_tensor(name, list(shape), dt).ap()
    env = dict(L=L, T=T, Tt=Tt, need_ctx=need_ctx)
    env["QKVT"] = din("QKVT", [24, 128, Tt], BF16)
    env["ZT"] = din("ZT", [8, 128, Tt], BF16)
    env["GG"] = din("GG", [Tt, 16])
    env["BETA"] = din("BETA", [Tt, 16])
    env["convT"] = din("convT", [1, 128, 24, 5])
    env["gnormT"] = din("gnormT", [1, 128, 1])
    consts = din("consts", [12, 128, 128])
    env["QNT"] = dout("QNT", [8, 128, Tt], BF16)
    env["KNT"] = dout("KNT", [8, 128, Tt], BF16)
    env["QTOK"] = dscr("QTOK", [Tt, 8, 128], BF16)
    env["KTOK"] = dscr("KTOK", [Tt, 8, 128], BF16)
    env["VTOK"] = dout("VTOK", [Tt, 8, 128], BF16)
    env["OACC"] = dout("OACC", [2, Tt, 8, 128])
    env["YAT"] = dout("YAT", [8, 128, Tt], BF16)
    env["DBG"] = dout("DBG", [2, 4, 128, 512])
    env["DBGB"] = dout("DBGB", [2, 8, 128, 512], BF16)
    env["DBGF"] = dout("DBGF", [2, 3, 128, 512])
    R_c = Res("c")
    cst = P.sb([128, 12, 128], F32, "cst")
    cstb = P.sb([128, 12, 128], BF16, "cstb")
    P.dma("sp", cst[:], consts.rearrange("c p m -> p c m"), R_c, writes=[R_c])
    P.op("dve", lambda e: e.tensor_copy(out=cstb[:], in_=cst[:]), reads=[R_c], writes=[R_c])
    epsln = P.sb([128, 2], F32, "epsln")
    P.op("dve", lambda e: e.memset(epsln[:, 0:1], LN_EPS), writes=[R_c])
    P.op("dve", lambda e: e.memset(epsln[:, 1:2], RMS_EPS), writes=[R_c])
    env.update(cst=cst, cstb=cstb, epsln=epsln, R_c=R_c, R_P1=Res("p1"), R_Y=Res("y"))
    build_gdn(P, nc, env, 0)
    P.finish()
    return nc


def build_merge_ffn(P, nc, env, l):
    g = env
    L, T, Tt = g["L"], g["T"], g["Tt"]
    need_ctx = g["need_ctx"]
    R_Y, R_c, R_mod, R_X, R_W, R_P1 = g["R_Y"], g["R_c"], g["R_mod"], g["R_X"], g["R_W"], g["R_P1"]
    modT, ident = g["modT"], g["ident"]
    X = g["X"]
    layer_norm_tile = g["layer_norm_tile"]
    load_wblk = g["load_wblk"]
    P.push()
    psum = Ring(P, 8, [128, 512], F32, "ps", psum=True)
    g["psum_box"][0] = psum
    wring = Ring(P, 3, [128, KC, 512], BF16, "w5")
    YT = [P.sb([128, 8, 256], BF16, "yt") for _ in range(3)]
    R_yt = Res("yt")
    gring = Ring(P, 6, [128, 256], BF16, "g5")
    mT = P.sb([128, KC, 256], BF16, "mT")
    R_m = Res("m")
    h2T = P.sb([128, KC, 256], BF16, "h2T")
    R_h2 = Res("h2")
    r = P.sb([128, 2, D], F32, "r")
    R_r = [Res("r0"), Res("r1")]
    f1T = P.sb([128, 64, 256], BF16, "f1T")
    R_f1 = Res("f1")
    acc4 = P.sb([128, 4, 256], F32, "acc4")
    R_acc = [Res("a") for _ in range(4)]
    lnp = [P.sb([128, D], F32, "lnp") for _ in range(4)]
    for t, src in zip(lnp, (g["lnm_g"], g["lnm_b"], g["lnf_g"], g["lnf_b"])):
        P.dma("sp", t[:], src[l], R_mod, writes=[R_mod])
    mbc = [P.sb([128, D], F32, "mbc") for _ in range(2)]
    R_mb = Res("mb")
    tmpr_ = Ring(P, 2, [128, 512], F32, "t5")
    small = Ring(P, 2, [128, 32], F32, "sm5")
    cur_b = None
    segs = [(0, 256, True)] if L >= 256 else []
    segs = [(i * 256, 256, True) for i in range(L // 256)] + [(L + i * 256, 256, False) for i in range(T // 256)]
    for (s0, W, isctx) in segs:
        if isctx and not need_ctx:
            continue
        b = 1 if isctx else 0
        nt = W // 128
        if cur_b != b:
            for which in range(2):
                P.dma("sp", mbc[which][:], g["MODBC"][which, b], R_mb, reads=[g["R_mbc"]], writes=[R_mb])
            cur_b = b
        for bi, src in enumerate((g["YAT"], g["YBT"], g["YCT"])):
            P.dma("sp", YT[bi][:, :, :W], src[:, :, s0:s0 + W].rearrange("c p t -> p c t"), R_yt, reads=[R_Y], writes=[R_yt])
        for ti in range(nt):
            P.dma("sp", r[:, ti, :], X[s0 + ti * 128:s0 + (ti + 1) * 128, :], R_r[ti], reads=[R_X], writes=[R_r[ti]])
        wsrc = (g["wb_a"][l], g["wb_b"][l], g["wb_c"][l])
        for mb in range(4):
            for bi in range(3):
                wt, wres = load_wblk(wring, wsrc[bi], 0, 8, mb * 512)
                for j in range(4):
                    mc = mb * 4 + j
                    gt, gres = gring.next()
                    P.dma("sp", gt[:, :W], g["GT"][bi * 16 + mc, :, s0:s0 + W], gres, reads=[R_P1], writes=[gres])
                    pt, ptr = psum.next()
                    for kc in range(8):
                        P.op("pe", lambda e, pt=pt, wt=wt, kc=kc, j=j, bi=bi, W=W: e.matmul(out=pt[:, :W], lhsT=wt[:, kc, j * 128:(j + 1) * 128], rhs=YT[bi][:, kc, :W], start=(kc == 0), stop=(kc == 7)),
                             reads=[wres, R_yt], writes=[ptr], signal=(kc == 7))
                    if bi == 0:
                        P.op("dve", lambda e, pt=pt, gt=gt, j=j, W=W: e.tensor_tensor(out=acc4[:, j, :W], in0=pt[:, :W], in1=gt[:, :W], op=ALU.mult), reads=[ptr, gres], writes=[R_acc[j]])
                    else:
                        tmp, tres = tmpr_.next()
                        P.op("dve", lambda e, pt=pt, gt=gt, tmp=tmp, W=W: e.tensor_tensor(out=tmp[:, :W], in0=pt[:, :W], in1=gt[:, :W], op=ALU.mult), reads=[ptr, gres], writes=[tres])
                        if bi == 1:
                            P.op("pool", lambda e, tmp=tmp, j=j, W=W: e.tensor_tensor(out=acc4[:, j, :W], in0=acc4[:, j, :W], in1=tmp[:, :W], op=ALU.add), reads=[tres, R_acc[j]], writes=[R_acc[j]])
                        else:
                            P.op("pool", lambda e, tmp=tmp, j=j, mc=mc, W=W: e.tensor_tensor(out=mT[:, mc, :W], in0=acc4[:, j, :W], in1=tmp[:, :W], op=ALU.add), reads=[tres, R_acc[j]], writes=[R_m])
        for nb in range(4):
            wt, wres = load_wblk(wring, g["wb_o"][l], 0, KC, nb * 512)
            for ti in range(nt):
                pt, ptr = psum.next()
                for kc in range(KC):
                    P.op("pe", lambda e, pt=pt, wt=wt, kc=kc, ti=ti: e.matmul(out=pt[:], lhsT=mT[:, kc, ti * 128:(ti + 1) * 128], rhs=wt[:, kc, :], start=(kc == 0), stop=(kc == KC - 1)),
                         reads=[wres, R_m], writes=[ptr], signal=(kc == KC - 1))
                tmp, tres = tmpr_.next()
                P.op("dve", lambda e, pt=pt, tmp=tmp, nb=nb: e.tensor_tensor(out=tmp[:], in0=pt[:], in1=mbc[0][:, nb * 512:(nb + 1) * 512], op=ALU.mult), reads=[ptr, R_mb], writes=[tres])
                P.op("dve", lambda e, tmp=tmp, ti=ti, nb=nb: e.scalar_tensor_tensor(out=r[:, ti, nb * 512:(nb + 1) * 512], in0=r[:, ti, nb * 512:(nb + 1) * 512], scalar=ALPHA, in1=tmp[:], op0=ALU.mult, op1=ALU.add), reads=[tres, R_r[ti]], writes=[R_r[ti]])
        for ti in range(nt):
            sm, smr = small.next()
            layer_norm_tile(r[:, ti, :], lnp[0][:], lnp[1][:], r[:, ti, :], R_r[ti], sm, smr)
            g["transpose_mod"](r[:, ti, :], R_r[ti], h2T, R_h2, ti, 3, 4, b)
        for blk in range(DFF // 512):
            wt, wres = load_wblk(wring, g["wb_f1"][l], 0, KC, blk * 512)
            for j in range(4):
                fc = blk * 4 + j
                pt, ptr = psum.next()
                for kc in range(KC):
                    P.op("pe", lambda e, pt=pt, wt=wt, kc=kc, j=j, W=W: e.matmul(out=pt[:, :W], lhsT=wt[:, kc, j * 128:(j + 1) * 128], rhs=h2T[:, kc, :W], start=(kc == 0), stop=(kc == KC - 1)),
                         reads=[wres, R_h2], writes=[ptr], signal=(kc == KC - 1))
                tmp, tres = tmpr_.next()
                P.op("act", lambda e, pt=pt, tmp=tmp, W=W: e.activation(out=tmp[:, :W], in_=pt[:, :W], func=AF.Relu), reads=[ptr], writes=[tres])
                P.op("pool", lambda e, tmp=tmp, fc=fc, W=W: e.tensor_tensor(out=f1T[:, fc, :W], in0=tmp[:, :W], in1=tmp[:, :W], op=ALU.mult), reads=[tres], writes=[R_f1])
        for nb in range(4):
            pts = [psum.next() for _ in range(nt)]
            for kp in range(4):
                wt, wres = load_wblk(wring, g["wb_f2"][l], kp * 16, 16, nb * 512)
                for ti in range(nt):
                    pt, ptr = pts[ti]
                    for kc in range(16):
                        P.op("pe", lambda e, pt=pt, wt=wt, kc=kc, kp=kp, ti=ti: e.matmul(out=pt[:], lhsT=f1T[:, kp * 16 + kc, ti * 128:(ti + 1) * 128], rhs=wt[:, kc, :], start=(kp == 0 and kc == 0), stop=(kp == 3 and kc == 15)),
                             reads=[wres, R_f1], writes=[ptr], signal=(kc == 15))
            for ti in range(nt):
                pt, ptr = pts[ti]
                tmp, tres = tmpr_.next()
                P.op("dve", lambda e, pt=pt, tmp=tmp, nb=nb: e.tensor_tensor(out=tmp[:], in0=pt[:], in1=mbc[1][:, nb * 512:(nb + 1) * 512], op=ALU.mult), reads=[ptr, R_mb], writes=[tres])
                P.op("dve", lambda e, tmp=tmp, ti=ti, nb=nb: e.scalar_tensor_tensor(out=r[:, ti, nb * 512:(nb + 1) * 512], in0=r[:, ti, nb * 512:(nb + 1) * 512], scalar=ALPHA, in1=tmp[:], op0=ALU.mult, op1=ALU.add), reads=[tres, R_r[ti]], writes=[R_r[ti]])
        for ti in range(nt):
            sm, smr = small.next()
            layer_norm_tile(r[:, ti, :], lnp[2][:], lnp[3][:], r[:, ti, :], R_r[ti], sm, smr)
            P.dma("sp", X[s0 + ti * 128:s0 + (ti + 1) * 128, :], r[:, ti, :], R_r[ti], reads=[R_r[ti]], writes=[R_X])
    P.pop()


def _consts(T, L):
    c = np.zeros((12, 128, 128), np.float32)
    idx = np.arange(128)
    c[0] = np.eye(128)
    sw = np.where((idx % 64) < 32, idx + 32, idx - 32)
    c[1][sw, idx] = 1.0
    c[2] = 1.0
    same = (idx[:, None] // 64) == (idx[None, :] // 64)
    jj, ii = idx[:, None], idx[None, :]
    c[3] = (same & (jj <= ii))
    c[4] = (same & (jj >= ii))
    c[5] = same
    c[6] = (jj < 64) * np.ones((1, 128))
    c[7] = (jj >= 64) * np.ones((1, 128))
    BIG = 30000.0
    i2, j2 = idx[:, None], idx[None, :]
    c[8] = np.where(same & (j2 < i2), 0.0, BIG)
    c[9] = np.where(same & (j2 > i2), 0.0, BIG)
    c[10] = np.where(same & (jj <= ii), 0.0, BIG)
    c[11] = np.where(same & (jj >= ii), 0.0, BIG)
    es = np.zeros((8, 8, 128), np.float32)
    for h in range(8):
        es[h, h, :] = 1.0
    sm = np.zeros((2, 128, 128), np.float32)
    kk, qq = idx[:, None], idx[None, :]
    sm[0] = (kk >= qq)
    sm[1] = (kk <= qq)
    t = np.arange(T)
    row = (t // GW).astype(np.float32)
    col = (t % GW).astype(np.float32)
    freq = np.power(np.float32(10000.0), -np.arange(16, dtype=np.float32) / np.float32(16)).astype(np.float32)
    ang = np.concatenate([row[:, None] * freq, col[:, None] * freq], axis=-1).astype(np.float32)
    cosv, sinv = np.cos(ang).astype(np.float32), np.sin(ang).astype(np.float32)
    cosT = np.ones((128, L + T), np.float32)
    sinT = np.zeros((128, L + T), np.float32)
    for m in range(128):
        f = m % 32
        sgn = -1.0 if (m % 64) < 32 else 1.0
        cosT[m, L:] = cosv[:, f]
        sinT[m, L:] = sgn * sinv[:, f]
    return c, es, sm, cosT, sinT


def prep_shared(inp, T, L, NL):
    f = lambda a: np.ascontiguousarray(np.asarray(a, dtype=np.float32))
    bc = lambda a: np.ascontiguousarray(np.broadcast_to(np.asarray(a, np.float32)[:, None, :], (a.shape[0], 128, a.shape[-1])))
    perm = perm_in_cols()
    w_in = np.asarray(inp["w_in"], np.float32)[:NL]
    b_in = np.asarray(inp["b_in"], np.float32)[:NL]
    wp = np.zeros((NL, D, NIN), np.float32)
    bp = np.zeros((NL, NIN), np.float32)
    ok = perm >= 0
    wp[:, :, ok] = w_in[:, :, perm[ok]]
    bp[:, ok] = b_in[:, perm[ok]]
    c, es, sm, cosT, sinT = _consts(T, L)
    sh = {
        "w_ada": f(inp["w_ada"][:NL]),
        "b_adaT": f(np.asarray(inp["b_ada"])[:NL].reshape(NL, 96, 128).transpose(0, 2, 1)),
        "b_adabc": bc(np.asarray(inp["b_ada"])[:NL]),
        "w_in": wp,
        "b_inT": f(bp[:, :NFM * 128].reshape(NL, NFM, 128).transpose(0, 2, 1)),
        "b_intok": bc(bp[:, NFM * 128:]),
        "convT": f(np.asarray(inp["gdn_conv"])[:NL].reshape(NL, 5, 24, 128).transpose(0, 3, 2, 1)),
        "alog": bc(np.asarray(inp["gdn_a_log"])[:NL].reshape(NL, 16)),
        "dtb": bc(np.asarray(inp["gdn_dt_bias"])[:NL].reshape(NL, 16)),
        "gnormT": f(np.asarray(inp["gdn_norm"])[:NL].reshape(NL, 128, 1)),
        "sinks": bc(np.asarray(inp["swa_sinks"])[:NL]),
        "sgu_g": bc(np.asarray(inp["sgu_ln_g"])[:NL]),
        "sgu_bb": bc(np.asarray(inp["sgu_ln_b"])[:NL]),
        "sgu_wT": f(np.asarray(inp["sgu_w"])[:NL].transpose(0, 1, 3, 2)),
        "sgu_bias": bc(np.asarray(inp["sgu_b"])[:NL].reshape(NL, 1024)),
        "w_a": f(inp["w_branch_a"][:NL]), "w_b": f(inp["w_branch_b"][:NL]), "w_c": f(inp["w_branch_c"][:NL]),
        "w_o": f(inp["w_out"][:NL]),
        "lnm_g": bc(np.asarray(inp["ln_mix_g"])[:NL]), "lnm_b": bc(np.asarray(inp["ln_mix_b"])[:NL]),
        "w_f1": f(inp["w_ff1"][:NL]), "w_f2": f(inp["w_ff2"][:NL]),
        "lnf_g": bc(np.asarray(inp["ln_ff_g"])[:NL]), "lnf_b": bc(np.asarray(inp["ln_ff_b"])[:NL]),
        "cosT": cosT, "sinT": sinT, "consts": c, "esel": es, "swamask": sm,
    }
    return sh


def prep_core(inp, b):
    x = np.asarray(inp["x"], np.float32)[b]
    ctx = np.asarray(inp["ctx"], np.float32)[b]
    xin = np.ascontiguousarray(np.concatenate([ctx, x], axis=0))
    cT = np.stack([np.asarray(inp["c"], np.float32)[b].reshape(KC, 128).T,
                   np.asarray(inp["c_ctx"], np.float32).reshape(KC, 128).T], axis=-1)
    return {"xin": xin, "cT": np.ascontiguousarray(cT)}


def kernel(**inputs):
    B, T, _ = inputs["x"].shape
    L = inputs["ctx"].shape[1]
    nc = build(T, L, DEPTH)
    sh = prep_shared(inputs, T, L, DEPTH)
    in_maps = []
    for core in range(B):
        m = dict(sh)
        m.update(prep_core(inputs, core))
        in_maps.append(m)
    res = run_bass_kernel_spmd(nc, in_maps, core_ids=list(range(B)))
    return np.stack([res.results[b]["out"] for b in range(B)], axis=0).astype(np.float32)
```
